# Optimizing a Trainium2 kernel written in Bass

```python
import math
import jax, jax.numpy as jnp
from jax import lax
import numpy as np

D_MODEL = 2048
BATCH = 4
SEQ = 4096
DEPTH = 2

MEM_LEN = 256
CONV_WIDTH = D_MODEL // 4
CONV_KERNEL = 31
HEAD_DIM = 128
ATTN_WIDTH = D_MODEL // 2
ATTN_HEADS = ATTN_WIDTH // HEAD_DIM
SB_BLOCK = 128
LRU_WIDTH = D_MODEL // 4
LRU_HEAD_DIM = 128
LRU_HEADS = LRU_WIDTH // LRU_HEAD_DIM
LRU_CONV_KERNEL = 4
LRU_C = 8.0
MIX_WIDTH = CONV_WIDTH + ATTN_WIDTH + LRU_WIDTH
XATTN_HEADS = 4
XATTN_HEAD_DIM = 128
XATTN_WIDTH = XATTN_HEADS * XATTN_HEAD_DIM
IN_SIZES = (CONV_WIDTH, CONV_WIDTH, CONV_WIDTH,
            ATTN_WIDTH, ATTN_WIDTH, ATTN_WIDTH, ATTN_WIDTH,
            LRU_WIDTH, LRU_WIDTH)
IN_WIDTH = 3 * CONV_WIDTH + 4 * ATTN_WIDTH + 2 * LRU_WIDTH

kernel_name = "hymba_style_conv_stickbreak_rglru_trunk"


def rms_norm(x, g, eps=1e-6):
    xf = x.astype(jnp.float32)
    y = xf * lax.rsqrt(jnp.mean(xf * xf, axis=-1, keepdims=True) + eps)
    return (y * g.astype(jnp.float32)).astype(x.dtype)


def layer_norm(x, g, b, eps=1e-5):
    xf = x.astype(jnp.float32)
    mu = jnp.mean(xf, axis=-1, keepdims=True)
    var = jnp.mean(jnp.square(xf - mu), axis=-1, keepdims=True)
    y = (xf - mu) * lax.rsqrt(var + eps)
    return (y * g.astype(jnp.float32) + b.astype(jnp.float32)).astype(x.dtype)


def causal_depthwise_conv(x, w, b):
    K, C = w.shape
    y = lax.conv_general_dilated(
        x, w[:, None, :].astype(x.dtype), window_strides=(1,),
        padding=((K - 1, 0),), dimension_numbers=("NWC", "WIO", "NWC"),
        feature_group_count=C)
    return y + b.astype(x.dtype)


def conformer_conv(val, glu_gate, dw_w, dw_b, ln_g, ln_b, pw_w):
    u = val * jax.nn.sigmoid(glu_gate)
    u = causal_depthwise_conv(u, dw_w, dw_b)
    u = jax.nn.silu(layer_norm(u, ln_g, ln_b))
    return u @ pw_w


def stick_breaking_attention(q, k, v):
    S = q.shape[2]
    scale = HEAD_DIM ** -0.5
    outs = []
    for qb in range(S // SB_BLOCK):
        start = qb * SB_BLOCK
        end = start + SB_BLOCK
        z = jnp.einsum("bhqd,bhkd->bhqk", q[:, :, start:end], k[:, :, :end]).astype(jnp.float32) * scale
        t_pos = start + jnp.arange(SB_BLOCK)
        s_pos = jnp.arange(end)
        causal = s_pos[None, :] < t_pos[:, None]
        log_1mb = jnp.where(causal, jax.nn.log_sigmoid(-z), 0.0)
        after = jnp.sum(log_1mb, axis=-1, keepdims=True) - jnp.cumsum(log_1mb, axis=-1)
        w = jnp.where(causal, jnp.exp(jax.nn.log_sigmoid(z) + after), 0.0)
        outs.append(jnp.einsum("bhqk,bhkd->bhqd", w.astype(v.dtype), v[:, :, :end]))
    return jnp.concatenate(outs, axis=2)


def rg_lru(xc, wa, ba, wx, bx, lam):
    B, S, W = xc.shape
    xh = xc.reshape(B, S, LRU_HEADS, LRU_HEAD_DIM)
    r = jax.nn.sigmoid(jnp.einsum("bsnd,nde->bsne", xh, wa).reshape(B, S, W) + ba)
    i = jax.nn.sigmoid(jnp.einsum("bsnd,nde->bsne", xh, wx).reshape(B, S, W) + bx)
    log_a = -LRU_C * r.astype(jnp.float32) * jax.nn.softplus(-lam.astype(jnp.float32))
    a = jnp.exp(log_a)
    mult = jnp.sqrt(-jnp.expm1(2.0 * log_a))
    b = mult * (i * xc).astype(jnp.float32)

    def combine(left, right):
        a1, b1 = left
        a2, b2 = right
        return a1 * a2, a2 * b1 + b2

    _, h = lax.associative_scan(combine, (a, b), axis=1)
    return h.astype(xc.dtype)


def hybrid_mixer(h, w_in, conv_dw_w, conv_dw_b, conv_ln_g, conv_ln_b, conv_pw_w,
                 lru_conv_w, lru_conv_b, lru_wa, lru_ba, lru_wx, lru_bx, lru_lambda,
                 out_norm_conv, out_norm_attn, out_norm_lru, w_out):
    B, S, _ = h.shape
    u = h @ w_in
    points = []
    acc = 0
    for size in IN_SIZES[:-1]:
        acc += size
        points.append(acc)
    c_val, c_glu, c_gate, q, k, v, a_gate, r_x, r_gate = jnp.split(u, points, axis=-1)

    y_conv = conformer_conv(c_val, c_glu, conv_dw_w, conv_dw_b, conv_ln_g, conv_ln_b, conv_pw_w)

    def heads(t):
        return t.reshape(B, S, ATTN_HEADS, HEAD_DIM).transpose(0, 2, 1, 3)
    y_attn = stick_breaking_attention(heads(q), heads(k), heads(v))
    y_attn = y_attn.transpose(0, 2, 1, 3).reshape(B, S, ATTN_WIDTH)

    xc = causal_depthwise_conv(r_x, lru_conv_w, lru_conv_b)
    y_lru = rg_lru(xc, lru_wa, lru_ba, lru_wx, lru_bx, lru_lambda)

    y = jnp.concatenate([
        rms_norm(y_conv, out_norm_conv) * jax.nn.silu(c_gate),
        rms_norm(y_attn, out_norm_attn) * jax.nn.silu(a_gate),
        rms_norm(y_lru, out_norm_lru) * jax.nn.silu(r_gate),
    ], axis=-1)
    return y @ w_out


def memory_cross_attention(h, memn, wq, wkv, wo):
    B, S, _ = h.shape
    M = memn.shape[1]
    q = (h @ wq).reshape(B, S, XATTN_HEADS, XATTN_HEAD_DIM)
    k, v = jnp.split(memn @ wkv, 2, axis=-1)
    k = k.reshape(B, M, XATTN_HEADS, XATTN_HEAD_DIM)
    v = v.reshape(B, M, XATTN_HEADS, XATTN_HEAD_DIM)
    s = jnp.einsum("bqhd,bkhd->bhqk", q, k).astype(jnp.float32) * (XATTN_HEAD_DIM ** -0.5)
    p = jax.nn.softmax(s, axis=-1).astype(v.dtype)
    o = jnp.einsum("bhqk,bkhd->bqhd", p, v).reshape(B, S, XATTN_WIDTH)
    return o @ wo


def setup_inputs(seed: int = 0) -> dict:
    key = jax.random.key(seed)
    ks = iter(jax.random.split(key, 32))
    f32 = jnp.float32

    def normal(shape, scale):
        return jax.random.normal(next(ks), shape, f32) * scale

    def gain(shape):
        return 1.0 + normal(shape, 0.02)

    u = jax.random.uniform(next(ks), (DEPTH, LRU_WIDTH), f32, minval=0.9, maxval=0.999)
    a0 = u ** (1.0 / LRU_C)
    lru_lambda = jnp.log(a0) - jnp.log1p(-a0)

    return {
        "x": normal((BATCH, SEQ, D_MODEL), 1.0),
        "mem": normal((BATCH, MEM_LEN, D_MODEL), 1.0),
        "mix_norm_g": gain((DEPTH, D_MODEL)),
        "w_in": normal((DEPTH, D_MODEL, IN_WIDTH), D_MODEL ** -0.5),
        "conv_dw_w": normal((DEPTH, CONV_KERNEL, CONV_WIDTH), CONV_KERNEL ** -0.5),
        "conv_dw_b": normal((DEPTH, CONV_WIDTH), 0.01),
        "conv_ln_g": gain((DEPTH, CONV_WIDTH)),
        "conv_ln_b": normal((DEPTH, CONV_WIDTH), 0.01),
        "conv_pw_w": normal((DEPTH, CONV_WIDTH, CONV_WIDTH), CONV_WIDTH ** -0.5),
        "lru_conv_w": normal((DEPTH, LRU_CONV_KERNEL, LRU_WIDTH), LRU_CONV_KERNEL ** -0.5),
        "lru_conv_b": normal((DEPTH, LRU_WIDTH), 0.01),
        "lru_wa": normal((DEPTH, LRU_HEADS, LRU_HEAD_DIM, LRU_HEAD_DIM), LRU_HEAD_DIM ** -0.5),
        "lru_ba": normal((DEPTH, LRU_WIDTH), 0.01),
        "lru_wx": normal((DEPTH, LRU_HEADS, LRU_HEAD_DIM, LRU_HEAD_DIM), LRU_HEAD_DIM ** -0.5),
        "lru_bx": normal((DEPTH, LRU_WIDTH), 0.01),
        "lru_lambda": lru_lambda,
        "out_norm_conv": gain((DEPTH, CONV_WIDTH)),
        "out_norm_attn": gain((DEPTH, ATTN_WIDTH)),
        "out_norm_lru": gain((DEPTH, LRU_WIDTH)),
        "w_out": normal((DEPTH, MIX_WIDTH, D_MODEL), MIX_WIDTH ** -0.5),
        "xattn_norm_g": gain((DEPTH, D_MODEL)),
        "mem_norm_g": gain((DEPTH, D_MODEL)),
        "xattn_wq": normal((DEPTH, D_MODEL, XATTN_WIDTH), D_MODEL ** -0.5),
        "xattn_wkv": normal((DEPTH, D_MODEL, 2 * XATTN_WIDTH), D_MODEL ** -0.5),
        "xattn_wo": normal((DEPTH, XATTN_WIDTH, D_MODEL), XATTN_WIDTH ** -0.5),
        "final_norm_g": gain((D_MODEL,)),
    }


def reference(x, mem, mix_norm_g, w_in, conv_dw_w, conv_dw_b, conv_ln_g, conv_ln_b, conv_pw_w,
              lru_conv_w, lru_conv_b, lru_wa, lru_ba, lru_wx, lru_bx, lru_lambda,
              out_norm_conv, out_norm_attn, out_norm_lru, w_out,
              xattn_norm_g, mem_norm_g, xattn_wq, xattn_wkv, xattn_wo, final_norm_g):
    for l in range(DEPTH):
        h = rms_norm(x, mix_norm_g[l])
        x = x + hybrid_mixer(h, w_in[l], conv_dw_w[l], conv_dw_b[l], conv_ln_g[l], conv_ln_b[l],
                             conv_pw_w[l], lru_conv_w[l], lru_conv_b[l], lru_wa[l], lru_ba[l],
                             lru_wx[l], lru_bx[l], lru_lambda[l], out_norm_conv[l],
                             out_norm_attn[l], out_norm_lru[l], w_out[l])
        h = rms_norm(x, xattn_norm_g[l])
        memn = rms_norm(mem, mem_norm_g[l])
        x = x + memory_cross_attention(h, memn, xattn_wq[l], xattn_wkv[l], xattn_wo[l])
    return rms_norm(x, final_norm_g)
```

```python
import numpy as np
from contextlib import ExitStack
import concourse.bass as bass
import concourse.mybir as mybir
from concourse.bass_utils import run_bass_kernel_spmd

F32 = mybir.dt.float32
BF16 = mybir.dt.bfloat16
AF = mybir.ActivationFunctionType
ALU = mybir.AluOpType

D = 2048
NL = 2
MEM = 256
INW = 6656
CK = 31
PAD = 32
SEM_LIMIT = 16000
SAME_ENG_SYNC = True

PP_DW = 0
PP_DWB = 124
PP_LNG = 128
PP_LNB = 132
PP_LCW = 136
PP_LCB = 152
PP_BA = 156
PP_BX = 160
PP_LAM = 164
PP_GNC = 168
PP_GNA = 172
PP_GNL = 180
NPP = 184

def _chunks():
    ch = []
    for j in range(4):
        ch.append(("vg", j * 128, 512 + j * 128, j))
    for j in range(2):
        ch.append(("gate", 1024 + j * 256, 1024 + j * 256 + 128, 0 + j * 256))
    for j in range(4):
        ch.append(("gate", 4608 + j * 256, 4608 + j * 256 + 128, 512 + j * 256))
    for j in range(2):
        ch.append(("gate", 6144 + j * 256, 6144 + j * 256 + 128, 1536 + j * 256))
    for j in range(4):
        ch.append(("q", 1536 + j * 256, 1536 + j * 256 + 128, j * 256))
    for j in range(4):
        ch.append(("k", 2560 + j * 256, 2560 + j * 256 + 128, j * 256))
    for j in range(2):
        ch.append(("rx", 5632 + j * 256, 5632 + j * 256 + 128, j * 256))
    for j in range(4):
        ch.append(("v", 3584 + j * 256, 3584 + j * 256 + 128, j * 2))
    return ch


CHUNKS = _chunks()
NCH = len(CHUNKS)


class Res:
    __slots__ = ("name", "w", "rs", "t", "ds")

    def __init__(self, name, t=None):
        self.name = name
        self.w = None
        self.rs = {}
        self.t = t
        self.ds = None


class EngState:
    def __init__(self, name, eng, sem):
        self.name = name
        self.eng = eng
        self.sem = sem
        self.cnt = 0
        self.waited = {}
        self.pending = False


class FW:
    def __init__(self, nc, stack, nsem):
        self.nc = nc
        self.stack = stack
        self.semi = 0
        self.E = {}
        for n in ("tensor", "vector", "scalar", "gpsimd", "sync"):
            self.E[n] = EngState(n, getattr(nc, n), self.new_sem())
        self.phase = "top"
        self.names = {}
        self.dpool = {"sync": [], "gpsimd": []}
        self.dlive = []
        self.dall = []

    def new_sem(self):
        s = self.stack.enter_context(self.nc.semaphore(f"s{self.semi}"))
        self.semi += 1
        return s

    def _wait(self, E, tok):
        sem, cnt = tok
        k = id(sem)
        if E.waited.get(k, 0) >= cnt:
            return
        E.eng.wait_ge(sem, cnt)
        E.waited[k] = cnt

    @staticmethod
    def _deps(reads, writes):
        toks = []
        for r in reads:
            if r.w is not None:
                toks.append(r.w)
        for w in writes:
            if w.w is not None:
                toks.append(w.w)
            toks.extend(w.rs.values())
        return toks

    @staticmethod
    def _record(tok, reads, writes):
        for r in reads:
            k = id(tok[0])
            o = r.rs.get(k)
            if o is None or o[1] < tok[1]:
                r.rs[k] = tok
        for w in writes:
            w.w = tok
            w.rs = {}

    def op(self, en, meth, reads, writes, inc=True, **kw):
        E = self.E[en]
        if E.cnt >= SEM_LIMIT and not E.pending:
            E.sem = self.new_sem()
            E.cnt = 0
        for tok in self._deps(reads, writes):
            if tok[0] is E.sem and (en == "tensor" or (not SAME_ENG_SYNC and en != "gpsimd")):
                continue
            self._wait(E, tok)
        ins = getattr(E.eng, meth)(**kw)
        if self.names is not None:
            self.names[ins.ins.name] = self.phase
        tok = (E.sem, E.cnt + 1)
        if inc:
            ins.then_inc(E.sem, 1)
            E.cnt += 1
            E.pending = False
        else:
            E.pending = True
        self._record(tok, reads, writes)
        return tok

    def _dsem(self, owner, qn):
        if owner.ds is None or owner.ds[1] >= SEM_LIMIT:
            pool = self.dpool[qn]
            pool.sort(key=lambda d: -d[1])
            if pool and pool[-1][1] < SEM_LIMIT - 4000:
                owner.ds = pool.pop()
            else:
                owner.ds = [self.new_sem(), 0, qn]
                self.dall.append(owner.ds)
            self.dlive.append(owner.ds)
        assert owner.ds[2] == qn, (owner.name, qn)
        return owner.ds

    def dma(self, qn, out, in_, reads, writes, owner, **kw):
        E = self.E[qn]
        for tok in self._deps(reads, writes):
            self._wait(E, tok)
        ds = self._dsem(owner, qn)
        ins = E.eng.dma_start(out=out, in_=in_, **kw)
        ins.then_inc(ds[0], 16)
        if self.names is not None:
            self.names[ins.ins.name] = self.phase
        ds[1] += 16
        tok = (ds[0], ds[1])
        self._record(tok, reads, writes)
        return tok

    def barrier(self):
        toks = []
        for E in self.E.values():
            assert not E.pending
            if E.cnt > 0:
                toks.append((E.sem, E.cnt))
        for ds in self.dlive:
            if ds[1] > 0:
                toks.append((ds[0], ds[1]))
        for E in self.E.values():
            for tok in toks:
                if tok[0] is E.sem:
                    continue
                self._wait(E, tok)
        for d in self.dlive:
            self.dpool[d[2]].append(d)
        self.dlive = []


def build_program(T, debug=False, stop=None, nl=NL):
    nc = bass.Bass("TRN2", target_bir_lowering=False)
    nc._fw_names = {}
    NTT = T // 128
    NTB = T // 512
    SBT = min(2048, T)
    NSB = T // SBT
    dbg_kind = "ExternalOutput" if debug else "Internal"

    def din(name, shape, dt=F32):
        return nc.dram_tensor(name, list(shape), dt, kind="ExternalInput").ap()

    def dscr(name, shape, dt):
        return nc.dram_tensor(name, list(shape), dt, kind=dbg_kind).ap()

    x_in = din("x", [T, D])
    mem_in = din("mem", [MEM, D])
    w_in = din("w_in", [nl, NCH, 128, 16, 256])
    w_out = din("w_out", [nl, D, D])
    w_q = din("wq", [nl, D, 512])
    w_kv = din("wkv", [nl, D, 1024])
    w_o = din("wo", [nl, 512, D])
    w_pw = din("pw", [nl, 512, 512])
    w_a = din("wa", [nl, 512, 128])
    w_x = din("wx", [nl, 512, 128])
    gbc = din("gbc", [7, 128, D])
    pp_in = din("pp", [nl, 128, NPP])
    out = nc.dram_tensor("out", [T, D], F32, kind="ExternalOutput").ap()

    xres = dscr("xres", [T, D], F32)
    hT_scr = dscr("hT_scr", [128, 16, T], BF16)
    u_scr = dscr("u_scr", [512, PAD + T], F32)
    rx_scr = dscr("rx_scr", [512, PAD + T], F32)
    g_scr = dscr("g_scr", [D, T], BF16)
    q_scr = dscr("q_scr", [1024, T], BF16)
    k_scr = dscr("k_scr", [1024, T], BF16)
    v_scr = dscr("v_scr", [8, 128, NTT, 128], BF16)
    y_scr = dscr("y_scr", [D, T], BF16)

    with ExitStack() as top:
        fw = FW(nc, top, 150)
        fw.names = nc._fw_names
        psb = []
        for i in range(8):
            t = top.enter_context(nc.psum_tensor(f"ps{i}", [128, 512], F32))
            psb.append(Res(f"ps{i}", t))
        prot = [0]

        def nps():
            r = psb[prot[0] % 8]
            prot[0] += 1
            return r

        uniq = [0]

        def sb(st, name, shape, dt):
            uniq[0] += 1
            name = f"{name}_{uniq[0]}"
            t = st.enter_context(nc.sbuf_tensor(name, list(shape), dt))
            return Res(name, t)

        def V(meth, reads, writes, **kw):
            return fw.op("vector", meth, reads, writes, **kw)

        def A(meth, reads, writes, **kw):
            return fw.op("scalar", meth, reads, writes, **kw)

        def G(meth, reads, writes, **kw):
            return fw.op("gpsimd", meth, reads, writes, **kw)

        def PE(meth, reads, writes, inc=True, **kw):
            return fw.op("tensor", meth, reads, writes, inc=inc, **kw)

        ident = sb(top, "ident", [128, 128], BF16)
        tri = sb(top, "tri", [128, 128], BF16)
        ones_bf = sb(top, "ones_bf", [128, 128], BF16)
        ones_f = sb(top, "ones_f", [128, 128], F32)
        mask0 = sb(top, "mask0", [128, 512], F32)
        onesw = sb(top, "onesw", [128, 512], F32)
        ceps6 = sb(top, "ceps6", [128, 1], F32)
        ceps5 = sb(top, "ceps5", [128, 1], F32)
        cone = sb(top, "cone", [128, 1], F32)
        czero = sb(top, "czero", [128, 1], F32)
        zpad = sb(top, "zpad", [128, PAD], F32)
        pp = [sb(top, f"pp{l}", [128, NPP], F32) for l in range(nl)]
        k2T = sb(top, "k2T", [128, 4, MEM], BF16)
        v2 = sb(top, "v2", [128, 2, 512], BF16)

        G("memset", [], [onesw], ap=onesw.t[:], constant=1.0)
        G("memset", [], [ones_f], ap=ones_f.t[:], constant=1.0)
        G("memset", [], [ones_bf], ap=ones_bf.t[:], constant=1.0)
        G("memset", [], [ceps6], ap=ceps6.t[:], constant=1e-6)
        G("memset", [], [ceps5], ap=ceps5.t[:], constant=1e-5)
        G("memset", [], [cone], ap=cone.t[:], constant=1.0)
        G("memset", [], [czero], ap=czero.t[:], constant=0.0)
        G("memset", [], [zpad], ap=zpad.t[:], constant=0.0)
        G("affine_select", [ones_bf], [ident], out=ident.t[:], in_=ones_bf.t[:], pattern=[[1, 128]],
          compare_op=ALU.is_equal, fill=0.0, base=0, channel_multiplier=-1)
        G("affine_select", [ones_bf], [tri], out=tri.t[:], in_=ones_bf.t[:], pattern=[[-1, 128]],
          compare_op=ALU.is_ge, fill=0.0, base=0, channel_multiplier=1)
        G("affine_select", [onesw], [mask0], out=mask0.t[:], in_=onesw.t[:], pattern=[[1, 512]],
          compare_op=ALU.is_gt, fill=0.0, base=0, channel_multiplier=-1)
        for l in range(nl):
            fw.dma("sync", pp[l].t[:], pp_in[l], [], [pp[l]], pp[l])
        for j in range(4):
            fw.dma("sync", u_scr[j * 128:(j + 1) * 128, 0:PAD], zpad.t[:], [zpad], [], zpad)
            fw.dma("sync", rx_scr[j * 128:(j + 1) * 128, 0:PAD], zpad.t[:], [zpad], [], zpad)
        if debug:
            dmask = nc.dram_tensor("dbg_mask", [128, 512], F32, kind="ExternalOutput").ap()
            dtri = nc.dram_tensor("dbg_tri", [128, 128], BF16, kind="ExternalOutput").ap()
            fw.dma("sync", dmask, mask0.t[:], [mask0], [], mask0)
            fw.dma("sync", dtri, tri.t[:], [tri], [], tri)
        fw.barrier()

        def norm_stats(xb, junk, ssq, rstd, eps_c):
            A("activation", [xb], [junk, ssq], out=junk.t[:], in_=xb.t[:], func=AF.Square,
              accum_out=ssq.t[:, 0:1])
            A("activation", [ssq, eps_c], [rstd], out=rstd.t[:, 0:1], in_=ssq.t[:, 0:1], func=AF.Sqrt,
              scale=1.0 / D, bias=eps_c.t[:, 0:1])
            V("reciprocal", [rstd], [rstd], out=rstd.t[:, 0:1], in_=rstd.t[:, 0:1])

        def norm_to_hT(xb, gt, hb, junk, ssq, rstd, hTt, col0, ncols=128, dst=None, oview=None):
            norm_stats(xb, junk, ssq, rstd, ceps6)
            V("scalar_tensor_tensor", [xb, rstd, gt], [hb], out=hb.t[:], in0=xb.t[:], scalar=rstd.t[:, 0:1],
              in1=gt.t[:], op0=ALU.mult, op1=ALU.mult)
            for half in range(2):
                pb = nps()
                pv = pb.t[:].bitcast(BF16)
                for k in range(8):
                    c = half * 8 + k
                    PE("transpose", [hb, ident], [pb], inc=(k == 7), out=pv[:, k * 128:(k + 1) * 128],
                       in_=hb.t[:, c * 128:(c + 1) * 128], identity=ident.t[:])
                ov = hTt.t[:, half * 8:half * 8 + 8, :] if oview is None else oview(half * 8, half * 8 + 8)
                if half == 0:
                    A("activation", [pb], [hTt], out=ov,
                      in_=pv.rearrange("p (c k) -> p c k", k=128), func=AF.Copy)
                else:
                    V("tensor_copy", [pb], [hTt], out=ov,
                      in_=pv.rearrange("p (c k) -> p c k", k=128))
            if dst is not None:
                fw.dma("sync", dst, hTt.t[:], [hTt], [], hTt)

        def phase_norm0():
            fw.phase = "phase_norm0"
            with ExitStack() as st:
                gt = sb(st, "p1_g", [128, D], F32)
                xb = [sb(st, f"p1_x{i}", [128, D], F32) for i in range(4)]
                hb = [sb(st, f"p1_h{i}", [128, D], BF16) for i in range(4)]
                junk = sb(st, "p1_junk", [128, D], BF16)
                ssq = [sb(st, f"p1_ssq{i}", [128, 1], F32) for i in range(4)]
                rstd = [sb(st, f"p1_rstd{i}", [128, 1], F32) for i in range(4)]
                hTt = [sb(st, f"p1_hT{i}", [128, 16, 128], BF16) for i in range(4)]
                fw.dma("sync", gt.t[:], gbc[0], [], [gt], gt)
                for tt in range(min(3, NTT)):
                    fw.dma("sync", xb[tt % 4].t[:], x_in[tt * 128:(tt + 1) * 128, :], [], [xb[tt % 4]], xb[tt % 4])
                for tt in range(NTT):
                    i = tt % 4
                    if tt + 3 < NTT:
                        i3 = (tt + 3) % 4
                        fw.dma("sync", xb[i3].t[:], x_in[(tt + 3) * 128:(tt + 4) * 128, :], [], [xb[i3]], xb[i3])
                    norm_to_hT(xb[i], gt, hb[i], junk, ssq[i], rstd[i], hTt[i], 0,
                               dst=hT_scr[:, :, tt * 128:(tt + 1) * 128])
                fw.barrier()

        phase_norm0()
        if stop == "p1":
            return nc

        def phase_inproj(l):
            fw.phase = "phase_inproj"
            with ExitStack() as st:
                hT = sb(st, "p2_hT", [128, 16, SBT], BF16)
                wb = [sb(st, f"p2_w{i}", [128, 16, 256], BF16) for i in range(3)]
                valb = [sb(st, f"p2_val{i}", [128, 512], F32) for i in range(4)]
                sig = [sb(st, f"p2_sig{i}", [128, 512], F32) for i in range(2)]
                ub = [sb(st, f"p2_u{i}", [128, 512], F32) for i in range(3)]
                gb = [sb(st, f"p2_g{i}", [128, 512], BF16) for i in range(3)]
                vb = [sb(st, f"p2_v{i}", [128, 256], BF16) for i in range(3)]
                rot = {"w": 0, "sig": 0, "u": 0, "g": 0, "v": 0}

                def nxt(lst, key):
                    r = lst[rot[key] % len(lst)]
                    rot[key] += 1
                    return r

                ntb = SBT // 512
                for sbi in range(NSB):
                    t0 = sbi * SBT
                    fw.dma("sync", hT.t[:], hT_scr[:, :, t0:t0 + SBT], [], [hT], hT)
                    for ci, (kind, ca, cb, aux) in enumerate(CHUNKS):
                        w = nxt(wb, "w")
                        fw.dma("gpsimd", w.t[:], w_in[l, ci], [], [w], w)
                        if kind == "v":
                            for tt in range(SBT // 128):
                                pb = nps()
                                for c in range(16):
                                    PE("matmul", [hT, w], [pb], inc=(c == 15), out=pb.t[:, 0:256],
                                       lhsT=hT.t[:, c, tt * 128:(tt + 1) * 128], rhs=w.t[:, c, :],
                                       start=(c == 0), stop=(c == 15))
                                v = nxt(vb, "v")
                                if tt % 2 == 0:
                                    V("tensor_copy", [pb], [v], out=v.t[:], in_=pb.t[:, 0:256])
                                else:
                                    A("activation", [pb], [v], out=v.t[:], in_=pb.t[:, 0:256], func=AF.Copy)
                                gt = (t0 // 128) + tt
                                for hh in range(2):
                                    fw.dma("sync", v_scr[aux + hh, :, gt, :], v.t[:, hh * 128:(hh + 1) * 128],
                                           [v], [], v)
                            continue
                        for half in range(2):
                            for tb in range(ntb):
                                pb = nps()
                                for c in range(16):
                                    PE("matmul", [hT, w], [pb], inc=(c == 15), out=pb.t[:, :],
                                       lhsT=w.t[:, c, half * 128:(half + 1) * 128],
                                       rhs=hT.t[:, c, tb * 512:(tb + 1) * 512],
                                       start=(c == 0), stop=(c == 15))
                                tg = t0 + tb * 512
                                if kind == "vg":
                                    if half == 0:
                                        V("tensor_copy", [pb], [valb[tb]], out=valb[tb].t[:], in_=pb.t[:])
                                    else:
                                        sg = nxt(sig, "sig")
                                        A("activation", [pb], [sg], out=sg.t[:], in_=pb.t[:], func=AF.Sigmoid)
                                        u = nxt(ub, "u")
                                        V("tensor_tensor", [valb[tb], sg], [u], out=u.t[:], in0=valb[tb].t[:],
                                          in1=sg.t[:], op=ALU.mult)
                                        fw.dma("sync", u_scr[aux * 128:(aux + 1) * 128, PAD + tg:PAD + tg + 512],
                                               u.t[:], [u], [], u)
                                elif kind == "gate":
                                    g = nxt(gb, "g")
                                    A("activation", [pb], [g], out=g.t[:], in_=pb.t[:], func=AF.Silu)
                                    r0 = aux + half * 128
                                    fw.dma("sync", g_scr[r0:r0 + 128, tg:tg + 512], g.t[:], [g], [], g)
                                elif kind in ("q", "k"):
                                    g = nxt(gb, "g")
                                    V("tensor_copy", [pb], [g], out=g.t[:], in_=pb.t[:])
                                    r0 = aux + half * 128
                                    dst = q_scr if kind == "q" else k_scr
                                    fw.dma("sync", dst[r0:r0 + 128, tg:tg + 512], g.t[:], [g], [], g)
                                else:
                                    u = nxt(ub, "u")
                                    V("tensor_copy", [pb], [u], out=u.t[:], in_=pb.t[:])
                                    r0 = aux + half * 128
                                    fw.dma("sync", rx_scr[r0:r0 + 128, PAD + tg:PAD + tg + 512], u.t[:], [u], [], u)
                fw.barrier()

        def group_rstd(ytiles, nfeat, sqb, rstd, pb=None):
            if pb is None:
                pb = nps()
            n = len(ytiles)
            for j, yt in enumerate(ytiles):
                if isinstance(yt, tuple):
                    yt, yap = yt
                else:
                    yap = yt.t[:]
                sq = sqb[j % len(sqb)]
                A("activation", [yt], [sq], out=sq.t[:], in_=yap, func=AF.Square)
                PE("matmul", [ones_f, sq], [pb], out=pb.t[:], lhsT=ones_f.t[:], rhs=sq.t[:],
                   start=(j == 0), stop=(j == n - 1))
            A("activation", [pb, ceps6], [rstd], out=rstd.t[:], in_=pb.t[:], func=AF.Sqrt, scale=1.0 / nfeat,
              bias=ceps6.t[:, 0:1])
            V("reciprocal", [rstd], [rstd], out=rstd.t[:], in_=rstd.t[:])

        def gate_store(yt, rstd, gn_ap, grow, t0, tmp, gate, yo, preloaded=False):
            if isinstance(yt, tuple):
                yt, yap = yt
            else:
                yap = yt.t[:]
            if not preloaded:
                fw.dma("sync", gate.t[:], g_scr[grow:grow + 128, t0:t0 + 512], [], [gate], gate)
            V("tensor_tensor", [yt, rstd], [tmp], out=tmp.t[:], in0=yap, in1=rstd.t[:], op=ALU.mult)
            V("scalar_tensor_tensor", [tmp, gate], [yo], out=yo.t[:], in0=tmp.t[:], scalar=gn_ap, in1=gate.t[:],
              op0=ALU.mult, op1=ALU.mult)
            fw.dma("sync", y_scr[grow:grow + 128, t0:t0 + 512], yo.t[:], [yo], [], yo)

        def phase_conv(l):
            fw.phase = "phase_conv"
            P = pp[l]
            with ExitStack() as st:
                pw = sb(st, "p3_pw", [128, 4, 512], BF16)
                uin = [[sb(st, f"p3_u{i}_{j}", [128, 30 + 512], F32) for j in range(4)] for i in range(2)]
                acc = [sb(st, f"p3_acc{j}", [128, 512], F32) for j in range(4)]
                sqb = [sb(st, f"p3_sq{i}", [128, 512], F32) for i in range(2)]
                mt = sb(st, "p3_m", [128, 512], F32)
                msq = sb(st, "p3_msq", [128, 512], F32)
                rs = sb(st, "p3_rs", [128, 512], F32)
                xn = [sb(st, f"p3_xn{i}", [128, 512], F32) for i in range(2)]
                sbf = [sb(st, f"p3_s{j}", [128, 512], BF16) for j in range(4)]
                yb = [sb(st, f"p3_y{j}", [128, 512], F32) for j in range(4)]
                rs2 = sb(st, "p3_rs2", [128, 512], F32)
                tmp = [sb(st, f"p3_t{i}", [128, 512], F32) for i in range(2)]
                gate = [sb(st, f"p3_g{i}", [128, 512], BF16) for i in range(2)]
                yo = [sb(st, f"p3_yo{i}", [128, 512], BF16) for i in range(2)]
                fw.dma("gpsimd", pw.t[:], w_pw[l].rearrange("(c p) n -> p c n", p=128), [], [pw], pw)
                for tb in range(NTB):
                    t0 = tb * 512
                    us = uin[tb % 2]
                    for tbl in ([0, 1] if tb == 0 else [tb + 1]):
                        if tbl >= NTB:
                            continue
                        tl = tbl * 512
                        for j in range(4):
                            ul = uin[tbl % 2][j]
                            fw.dma("sync", ul.t[:], u_scr[j * 128:(j + 1) * 128, PAD + tl - 30:PAD + tl + 512],
                                   [], [ul], ul)
                    for j in range(4):
                        wcol = PP_DW + j * CK
                        V("tensor_scalar", [us[j], P], [acc[j]], out=acc[j].t[:], in0=us[j].t[:, 30:542],
                          scalar1=P.t[:, wcol + 30:wcol + 31], scalar2=P.t[:, PP_DWB + j:PP_DWB + j + 1],
                          op0=ALU.mult, op1=ALU.add)
                    for k in range(30):
                        for j in range(4):
                            wcol = PP_DW + j * CK
                            V("scalar_tensor_tensor", [us[j], P, acc[j]], [acc[j]], out=acc[j].t[:],
                              in0=us[j].t[:, k:k + 512], scalar=P.t[:, wcol + k:wcol + k + 1], in1=acc[j].t[:],
                              op0=ALU.mult, op1=ALU.add)
                    p1 = nps()
                    for j in range(4):
                        PE("matmul", [ones_f, acc[j]], [p1], out=p1.t[:], lhsT=ones_f.t[:],
                           rhs=acc[j].t[:], start=(j == 0), stop=(j == 3))
                    p2 = nps()
                    for j in range(4):
                        sq = sqb[j % 2]
                        A("activation", [acc[j]], [sq], out=sq.t[:], in_=acc[j].t[:], func=AF.Square)
                        PE("matmul", [ones_f, sq], [p2], out=p2.t[:], lhsT=ones_f.t[:],
                           rhs=sq.t[:], start=(j == 0), stop=(j == 3))
                    A("activation", [p1], [mt], out=mt.t[:], in_=p1.t[:], func=AF.Copy, scale=1.0 / 512)
                    V("tensor_tensor", [mt], [msq], out=msq.t[:], in0=mt.t[:], in1=mt.t[:], op=ALU.mult)
                    V("scalar_tensor_tensor", [p2, msq], [rs], out=rs.t[:], in0=p2.t[:], scalar=1.0 / 512,
                      in1=msq.t[:], op0=ALU.mult, op1=ALU.subtract)
                    A("activation", [rs, ceps5], [rs], out=rs.t[:], in_=rs.t[:], func=AF.Sqrt,
                      bias=ceps5.t[:, 0:1])
                    V("reciprocal", [rs], [rs], out=rs.t[:], in_=rs.t[:])
                    for j in range(4):
                        x1 = xn[j % 2]
                        V("tensor_tensor", [acc[j], mt], [x1], out=x1.t[:], in0=acc[j].t[:], in1=mt.t[:],
                          op=ALU.subtract)
                        V("tensor_tensor", [x1, rs], [x1], out=x1.t[:], in0=x1.t[:], in1=rs.t[:], op=ALU.mult)
                        A("activation", [x1, P], [sbf[j]], out=sbf[j].t[:], in_=x1.t[:], func=AF.Silu,
                          scale=P.t[:, PP_LNG + j:PP_LNG + j + 1], bias=P.t[:, PP_LNB + j:PP_LNB + j + 1])
                    for co in range(4):
                        pb = nps()
                        for ci in range(4):
                            PE("matmul", [pw, sbf[ci]], [pb], out=pb.t[:],
                               lhsT=pw.t[:, ci, co * 128:(co + 1) * 128], rhs=sbf[ci].t[:],
                               start=(ci == 0), stop=(ci == 3))
                        V("tensor_copy", [pb], [yb[co]], out=yb[co].t[:], in_=pb.t[:])
                    group_rstd(yb, 512, sqb, rs2)
                    for co in range(4):
                        gate_store(yb[co], rs2, P.t[:, PP_GNC + co:PP_GNC + co + 1], co * 128, t0,
                                   tmp[co % 2], gate[co % 2], yo[co % 2])
                fw.barrier()

        def phase_lru(l):
            fw.phase = "phase_lru"
            P = pp[l]
            with ExitStack() as st:
                wa = sb(st, "p5_wa", [128, 4, 128], BF16)
                wx = sb(st, "p5_wx", [128, 4, 128], BF16)
                kc = sb(st, "p5_kc", [128, 4], F32)
                hc = [sb(st, f"p5_hc{j}", [128, 1], F32) for j in range(4)]
                rxin = [sb(st, f"p5_rx{i}", [128, 3 + 512], F32) for i in range(8)]
                xc = [sb(st, f"p5_xc{i}", [128, 512], F32) for i in range(4)]
                xcb = [sb(st, f"p5_xcb{i}", [128, 512], BF16) for i in range(4)]
                rt = [sb(st, f"p5_r{i}", [128, 512], F32) for i in range(4)]
                it = [sb(st, f"p5_i{i}", [128, 512], F32) for i in range(4)]
                at = [sb(st, f"p5_a{i}", [128, 512], F32) for i in range(4)]
                om = [sb(st, f"p5_om{i}", [128, 512], F32) for i in range(4)]
                bt = [sb(st, f"p5_b{i}", [128, 512], F32) for i in range(4)]
                hb = [sb(st, f"p5_h{j}", [128, 512], F32) for j in range(4)]
                sqb = [sb(st, f"p5_sq{i}", [128, 512], F32) for i in range(2)]
                rs2 = sb(st, "p5_rs2", [128, 512], F32)
                tmp = [sb(st, f"p5_t{i}", [128, 512], F32) for i in range(2)]
                gate = [sb(st, f"p5_g{i}", [128, 512], BF16) for i in range(2)]
                yo = [sb(st, f"p5_yo{i}", [128, 512], BF16) for i in range(2)]
                fw.dma("gpsimd", wa.t[:], w_a[l].rearrange("(n p) e -> p n e", p=128), [], [wa], wa)
                fw.dma("gpsimd", wx.t[:], w_x[l].rearrange("(n p) e -> p n e", p=128), [], [wx], wx)
                A("activation", [P], [kc], out=kc.t[:], in_=P.t[:, PP_LAM:PP_LAM + 4], func=AF.Exp, scale=-1.0)
                A("activation", [kc, cone], [kc], out=kc.t[:], in_=kc.t[:], func=AF.Ln, bias=cone.t[:, 0:1])
                V("tensor_scalar", [kc], [kc], out=kc.t[:], in0=kc.t[:], scalar1=-8.0, scalar2=None, op0=ALU.mult)
                for j in range(4):
                    G("memset", [], [hc[j]], ap=hc[j].t[:], constant=0.0)
                for tb in range(NTB):
                    t0 = tb * 512
                    J = range(4)
                    rxs = [rxin[(tb % 2) * 4 + j] for j in J]
                    for tbl in ([0, 1] if tb == 0 else [tb + 1]):
                        if tbl >= NTB:
                            continue
                        tl = tbl * 512
                        for j in J:
                            rl = rxin[(tbl % 2) * 4 + j]
                            fw.dma("sync", rl.t[:], rx_scr[j * 128:(j + 1) * 128, PAD + tl - 3:PAD + tl + 512],
                                   [], [rl], rl)
                    for j in J:
                        wcol = PP_LCW + j * 4
                        V("tensor_scalar", [rxs[j], P], [xc[j]], out=xc[j].t[:], in0=rxs[j].t[:, 3:515],
                          scalar1=P.t[:, wcol + 3:wcol + 4], scalar2=P.t[:, PP_LCB + j:PP_LCB + j + 1],
                          op0=ALU.mult, op1=ALU.add)
                    for k in range(3):
                        for j in J:
                            wcol = PP_LCW + j * 4
                            V("scalar_tensor_tensor", [rxs[j], P, xc[j]], [xc[j]], out=xc[j].t[:],
                              in0=rxs[j].t[:, k:k + 512], scalar=P.t[:, wcol + k:wcol + k + 1], in1=xc[j].t[:],
                              op0=ALU.mult, op1=ALU.add)
                    for j in J:
                        G("tensor_copy", [xc[j]], [xcb[j]], out=xcb[j].t[:], in_=xc[j].t[:])
                    prs, pis = [], []
                    for j in J:
                        pr = psb[2 * j]
                        PE("matmul", [wa, xcb[j]], [pr], out=pr.t[:], lhsT=wa.t[:, j, :], rhs=xcb[j].t[:],
                           start=True, stop=True)
                        pi = psb[2 * j + 1]
                        PE("matmul", [wx, xcb[j]], [pi], out=pi.t[:], lhsT=wx.t[:, j, :], rhs=xcb[j].t[:],
                           start=True, stop=True)
                        prs.append(pr)
                        pis.append(pi)
                    for j in J:
                        A("activation", [prs[j], P], [rt[j]], out=rt[j].t[:], in_=prs[j].t[:], func=AF.Sigmoid,
                          bias=P.t[:, PP_BA + j:PP_BA + j + 1])
                        A("activation", [pis[j], P], [it[j]], out=it[j].t[:], in_=pis[j].t[:], func=AF.Sigmoid,
                          bias=P.t[:, PP_BX + j:PP_BX + j + 1])
                    for j in J:
                        A("activation", [rt[j], kc], [at[j]], out=at[j].t[:], in_=rt[j].t[:], func=AF.Exp,
                          scale=kc.t[:, j:j + 1])
                    for j in J:
                        V("tensor_tensor", [at[j]], [om[j]], out=om[j].t[:], in0=at[j].t[:], in1=at[j].t[:],
                          op=ALU.mult)
                    for j in J:
                        V("tensor_scalar", [om[j]], [om[j]], out=om[j].t[:], in0=om[j].t[:], scalar1=-1.0,
                          scalar2=1.0, op0=ALU.mult, op1=ALU.add)
                    for j in J:
                        V("tensor_scalar", [om[j]], [om[j]], out=om[j].t[:], in0=om[j].t[:], scalar1=1e-30,
                          scalar2=None, op0=ALU.max)
                    for j in J:
                        A("activation", [om[j]], [om[j]], out=om[j].t[:], in_=om[j].t[:], func=AF.Sqrt)
                    for j in J:
                        V("tensor_tensor", [it[j], xc[j]], [bt[j]], out=bt[j].t[:], in0=it[j].t[:], in1=xc[j].t[:],
                          op=ALU.mult)
                    for j in J:
                        V("tensor_tensor", [bt[j], om[j]], [bt[j]], out=bt[j].t[:], in0=bt[j].t[:],
                          in1=om[j].t[:], op=ALU.mult)
                    for j in J:
                        V("tensor_tensor_scan", [at[j], bt[j], hc[j]], [hb[j]], out=hb[j].t[:],
                          data0=at[j].t[:], data1=bt[j].t[:], initial=hc[j].t[:, 0:1], op0=ALU.mult,
                          op1=ALU.add)
                    for j in J:
                        V("tensor_copy", [hb[j]], [hc[j]], out=hc[j].t[:, 0:1], in_=hb[j].t[:, 511:512])
                    group_rstd(hb, 512, sqb, rs2)
                    for j in range(4):
                        gate_store(hb[j], rs2, P.t[:, PP_GNL + j:PP_GNL + j + 1], 1536 + j * 128, t0,
                                   tmp[j % 2], gate[j % 2], yo[j % 2])
                fw.barrier()

        def make_bg_pool(st):
            return {
                "f32": [sb(st, f"bg_f{i}", [128, 512], F32) for i in range(22)],
                "b16": [sb(st, f"bg_b{i}", [128, 512], BF16) for i in range(10)],
                "uin": [[sb(st, f"bg_u{i}_{j}", [128, 30 + 512], F32) for j in range(4)] for i in range(2)],
                "pw": sb(st, "bg_pw", [128, 4, 512], BF16),
                "wa": sb(st, "bg_wa", [128, 4, 128], BF16),
                "wx": sb(st, "bg_wx", [128, 4, 128], BF16),
                "kc": sb(st, "bg_kc", [128, 4], F32),
                "hc": [sb(st, f"bg_hc{j}", [128, 1], F32) for j in range(4)],
                "banks": [psb[6], psb[7]],
                "brot": [0],
            }

        def gen_conv(l, pool):
            P = pp[l]
            F = pool["f32"]
            B = pool["b16"]
            pw = pool["pw"]
            uin = pool["uin"]
            acc = F[0:4]
            sqb = F[4:6]
            mt, msq, rs, rs2 = F[6], F[7], F[8], F[9]
            xn = F[10:12]
            yb = F[12:16]
            tmp = F[16:18]
            sbf = B[0:4]
            gate = B[4:8]
            yo = B[8:10]

            def bps():
                r = pool["banks"][pool["brot"][0] % 2]
                pool["brot"][0] += 1
                return r

            fw.dma("gpsimd", pw.t[:], w_pw[l].rearrange("(c p) n -> p c n", p=128), [], [pw], pw)
            for tb in range(NTB):
                t0 = tb * 512
                us = uin[tb % 2]
                for tbl in ([0, 1] if tb == 0 else [tb + 1]):
                    if tbl >= NTB:
                        continue
                    tl = tbl * 512
                    for j in range(4):
                        ul = uin[tbl % 2][j]
                        fw.dma("sync", ul.t[:], u_scr[j * 128:(j + 1) * 128, PAD + tl - 30:PAD + tl + 512],
                               [], [ul], ul)
                for j in range(4):
                    fw.dma("sync", gate[j].t[:], g_scr[j * 128:(j + 1) * 128, t0:t0 + 512], [], [gate[j]], gate[j])
                yield
                for j in range(4):
                    wcol = PP_DW + j * CK
                    V("tensor_scalar", [us[j], P], [acc[j]], out=acc[j].t[:], in0=us[j].t[:, 30:542],
                      scalar1=P.t[:, wcol + 30:wcol + 31], scalar2=P.t[:, PP_DWB + j:PP_DWB + j + 1],
                      op0=ALU.mult, op1=ALU.add)
                yield
                for k in range(30):
                    for j in range(4):
                        wcol = PP_DW + j * CK
                        V("scalar_tensor_tensor", [us[j], P, acc[j]], [acc[j]], out=acc[j].t[:],
                          in0=us[j].t[:, k:k + 512], scalar=P.t[:, wcol + k:wcol + k + 1], in1=acc[j].t[:],
                          op0=ALU.mult, op1=ALU.add)
                        yield
                p1 = bps()
                for j in range(4):
                    PE("matmul", [ones_f, acc[j]], [p1], out=p1.t[:], lhsT=ones_f.t[:], rhs=acc[j].t[:],
                       start=(j == 0), stop=(j == 3))
                p2 = bps()
                for j in range(4):
                    sq = sqb[j % 2]
                    A("activation", [acc[j]], [sq], out=sq.t[:], in_=acc[j].t[:], func=AF.Square)
                    PE("matmul", [ones_f, sq], [p2], out=p2.t[:], lhsT=ones_f.t[:], rhs=sq.t[:],
                       start=(j == 0), stop=(j == 3))
                A("activation", [p1], [mt], out=mt.t[:], in_=p1.t[:], func=AF.Copy, scale=1.0 / 512)
                V("tensor_tensor", [mt], [msq], out=msq.t[:], in0=mt.t[:], in1=mt.t[:], op=ALU.mult)
                V("scalar_tensor_tensor", [p2, msq], [rs], out=rs.t[:], in0=p2.t[:], scalar=1.0 / 512,
                  in1=msq.t[:], op0=ALU.mult, op1=ALU.subtract)
                A("activation", [rs, ceps5], [rs], out=rs.t[:], in_=rs.t[:], func=AF.Sqrt, bias=ceps5.t[:, 0:1])
                V("reciprocal", [rs], [rs], out=rs.t[:], in_=rs.t[:])
                yield
                for j in range(4):
                    x1 = xn[j % 2]
                    V("tensor_tensor", [acc[j], mt], [x1], out=x1.t[:], in0=acc[j].t[:], in1=mt.t[:],
                      op=ALU.subtract)
                    V("tensor_tensor", [x1, rs], [x1], out=x1.t[:], in0=x1.t[:], in1=rs.t[:], op=ALU.mult)
                    A("activation", [x1, P], [sbf[j]], out=sbf[j].t[:], in_=x1.t[:], func=AF.Silu,
                      scale=P.t[:, PP_LNG + j:PP_LNG + j + 1], bias=P.t[:, PP_LNB + j:PP_LNB + j + 1])
                    yield
                for co in range(4):
                    pb = bps()
                    for ci in range(4):
                        PE("matmul", [pw, sbf[ci]], [pb], out=pb.t[:], lhsT=pw.t[:, ci, co * 128:(co + 1) * 128],
                           rhs=sbf[ci].t[:], start=(ci == 0), stop=(ci == 3))
                    V("tensor_copy", [pb], [yb[co]], out=yb[co].t[:], in_=pb.t[:])
                    yield
                group_rstd(yb, 512, sqb, rs2, pb=bps())
                yield
                for co in range(4):
                    gate_store(yb[co], rs2, P.t[:, PP_GNC + co:PP_GNC + co + 1], co * 128, t0, tmp[co % 2],
                               gate[co], yo[co % 2], preloaded=True)
                    yield

        def gen_lru(l, pool):
            P = pp[l]
            F = pool["f32"]
            B = pool["b16"]
            wa, wx, kc, hc = pool["wa"], pool["wx"], pool["kc"], pool["hc"]
            rxin = [pool["uin"][0][j] for j in range(4)] + [pool["uin"][1][j] for j in range(4)]
            xc = F[0:4]
            rt = F[4:8]
            it = F[8:12]
            om = F[12:16]
            hb = F[16:20]
            sqb = F[20:22]
            rs2 = F[4]
            tmp = F[5:7]
            xcb = B[0:4]
            gate = B[4:8]
            yo = B[8:10]
            J = range(4)

            def bps():
                r = pool["banks"][pool["brot"][0] % 2]
                pool["brot"][0] += 1
                return r

            fw.dma("gpsimd", wa.t[:], w_a[l].rearrange("(n p) e -> p n e", p=128), [], [wa], wa)
            fw.dma("gpsimd", wx.t[:], w_x[l].rearrange("(n p) e -> p n e", p=128), [], [wx], wx)
            A("activation", [P], [kc], out=kc.t[:], in_=P.t[:, PP_LAM:PP_LAM + 4], func=AF.Exp, scale=-1.0)
            A("activation", [kc, cone], [kc], out=kc.t[:], in_=kc.t[:], func=AF.Ln, bias=cone.t[:, 0:1])
            V("tensor_scalar", [kc], [kc], out=kc.t[:], in0=kc.t[:], scalar1=-8.0, scalar2=None, op0=ALU.mult)
            for j in J:
                G("memset", [], [hc[j]], ap=hc[j].t[:], constant=0.0)
            yield
            for tb in range(NTB):
                t0 = tb * 512
                rxs = [rxin[(tb % 2) * 4 + j] for j in J]
                for tbl in ([0, 1] if tb == 0 else [tb + 1]):
                    if tbl >= NTB:
                        continue
                    tl = tbl * 512
                    for j in J:
                        rl = rxin[(tbl % 2) * 4 + j]
                        fw.dma("sync", rl.t[:, 0:515], rx_scr[j * 128:(j + 1) * 128, PAD + tl - 3:PAD + tl + 512],
                               [], [rl], rl)
                for j in J:
                    fw.dma("sync", gate[j].t[:], g_scr[1536 + j * 128:1536 + (j + 1) * 128, t0:t0 + 512], [],
                           [gate[j]], gate[j])
                yield
                for j in J:
                    wcol = PP_LCW + j * 4
                    V("tensor_scalar", [rxs[j], P], [xc[j]], out=xc[j].t[:], in0=rxs[j].t[:, 3:515],
                      scalar1=P.t[:, wcol + 3:wcol + 4], scalar2=P.t[:, PP_LCB + j:PP_LCB + j + 1],
                      op0=ALU.mult, op1=ALU.add)
                yield
                for k in range(3):
                    for j in J:
                        wcol = PP_LCW + j * 4
                        V("scalar_tensor_tensor", [rxs[j], P, xc[j]], [xc[j]], out=xc[j].t[:],
                          in0=rxs[j].t[:, k:k + 512], scalar=P.t[:, wcol + k:wcol + k + 1], in1=xc[j].t[:],
                          op0=ALU.mult, op1=ALU.add)
                    yield
                for j in J:
                    A("activation", [xc[j]], [xcb[j]], out=xcb[j].t[:], in_=xc[j].t[:], func=AF.Copy)
                yield
                for j in J:
                    pr = bps()
                    PE("matmul", [wa, xcb[j]], [pr], out=pr.t[:], lhsT=wa.t[:, j, :], rhs=xcb[j].t[:],
                       start=True, stop=True)
                    A("activation", [pr, P], [rt[j]], out=rt[j].t[:], in_=pr.t[:], func=AF.Sigmoid,
                      bias=P.t[:, PP_BA + j:PP_BA + j + 1])
                    pi = bps()
                    PE("matmul", [wx, xcb[j]], [pi], out=pi.t[:], lhsT=wx.t[:, j, :], rhs=xcb[j].t[:],
                       start=True, stop=True)
                    A("activation", [pi, P], [it[j]], out=it[j].t[:], in_=pi.t[:], func=AF.Sigmoid,
                      bias=P.t[:, PP_BX + j:PP_BX + j + 1])
                    yield
                for j in J:
                    A("activation", [rt[j], kc], [rt[j]], out=rt[j].t[:], in_=rt[j].t[:], func=AF.Exp,
                      scale=kc.t[:, j:j + 1])
                yield
                for j in J:
                    V("tensor_tensor", [rt[j]], [om[j]], out=om[j].t[:], in0=rt[j].t[:], in1=rt[j].t[:],
                      op=ALU.mult)
                yield
                for j in J:
                    V("tensor_scalar", [om[j]], [om[j]], out=om[j].t[:], in0=om[j].t[:], scalar1=-1.0,
                      scalar2=1.0, op0=ALU.mult, op1=ALU.add)
                yield
                for j in J:
                    V("tensor_scalar", [om[j]], [om[j]], out=om[j].t[:], in0=om[j].t[:], scalar1=1e-30,
                      scalar2=None, op0=ALU.max)
                yield
                for j in J:
                    A("activation", [om[j]], [om[j]], out=om[j].t[:], in_=om[j].t[:], func=AF.Sqrt)
                yield
                for j in J:
                    V("tensor_tensor", [it[j], xc[j]], [it[j]], out=it[j].t[:], in0=it[j].t[:], in1=xc[j].t[:],
                      op=ALU.mult)
                yield
                for j in J:
                    V("tensor_tensor", [it[j], om[j]], [it[j]], out=it[j].t[:], in0=it[j].t[:], in1=om[j].t[:],
                      op=ALU.mult)
                yield
                for j in J:
                    V("tensor_tensor_scan", [rt[j], it[j], hc[j]], [hb[j]], out=hb[j].t[:], data0=rt[j].t[:],
                      data1=it[j].t[:], initial=hc[j].t[:, 0:1], op0=ALU.mult, op1=ALU.add)
                    yield
                for j in J:
                    V("tensor_copy", [hb[j]], [hc[j]], out=hc[j].t[:, 0:1], in_=hb[j].t[:, 511:512])
                yield
                group_rstd(hb, 512, sqb, rs2, pb=bps())
                yield
                for j in J:
                    gate_store(hb[j], rs2, P.t[:, PP_GNL + j:PP_GNL + j + 1], 1536 + j * 128, t0, tmp[j % 2],
                               gate[j], yo[j % 2], preloaded=True)
                    yield

        def phase_attn(l):
            fw.phase = "phase_attn"
            P = pp[l]
            SC = 128.0 ** -0.5
            with ExitStack() as st:
                qb = [sb(st, f"p4_q{i}", [128, 512], BF16) for i in range(2)]
                kb = [sb(st, f"p4_k{i}", [128, T], BF16) for i in range(2)]
                vb = [sb(st, f"p4_v{i}", [128, NTT, 128], BF16) for i in range(3)]
                eb = [sb(st, f"p4_e{i}", [128, 512], F32) for i in range(4)]
                spb = [sb(st, f"p4_sp{i}", [128, 512], BF16) for i in range(4)]
                gbf = [sb(st, f"p4_gb{i}", [128, 512], F32) for i in range(2)]
                ab = [sb(st, f"p4_a{i}", [128, 512], BF16) for i in range(3)]
                Sb = [sb(st, f"p4_S{i}", [128, 512], BF16) for i in range(2)]
                ob = [sb(st, f"p4_o{i}", [128, 8, 512], F32) for i in range(2)]
                sqb = [sb(st, f"p4_sq{i}", [128, 512], F32) for i in range(2)]
                rs2 = sb(st, "p4_rs2", [128, 512], F32)
                tmp = [sb(st, f"p4_t{i}", [128, 512], F32) for i in range(2)]
                gate = [sb(st, f"p4_g{i}", [128, 512], BF16) for i in range(2)]
                yo = [sb(st, f"p4_yo{i}", [128, 512], BF16) for i in range(2)]
                zps = psb[0:3]
                fps = [psb[3]]
                ops = psb[4:6]
                pool = make_bg_pool(st)

                def bg_all():
                    yield from gen_conv(l, pool)
                    yield from gen_lru(l, pool)

                bgen = bg_all()

                def bg_bps():
                    r = pool["banks"][pool["brot"][0] % 2]
                    pool["brot"][0] += 1
                    return r
                BG_ITEMS = NTB * 150 + 1 + NTB * 45
                bg_state = {"acc": 0.0, "done": False}
                chains = []
                tiles = []
                for QB in range(NTB):
                    for h in range(8):
                        nk = 4 * (QB + 1)
                        ci = len(chains)
                        chains.append((QB, h, nk))
                        for i in range(nk):
                            kt = nk - 1 - i
                            o = max(0, kt * 128 - QB * 512)
                            tiles.append((ci, i, kt, o))
                N = len(tiles)

                def load_chain(ci):
                    QB, h, nk = chains[ci]
                    p = ci % 2
                    fw.dma("sync", qb[p].t[:], q_scr[h * 128:(h + 1) * 128, QB * 512:(QB + 1) * 512], [], [qb[p]],
                           qb[p])
                    fw.dma("sync", kb[p].t[:, 0:nk * 128], k_scr[h * 128:(h + 1) * 128, 0:nk * 128], [], [kb[p]],
                           kb[p])

                def load_v(ci):
                    QB, h, nk = chains[ci]
                    p3 = ci % 3
                    fw.dma("sync", vb[p3].t[:, 0:nk, :], v_scr[h, :, 0:nk, :], [], [vb[p3]], vb[p3])

                def st0(n):
                    ci, i, kt, o = tiles[n]
                    p = ci % 2
                    if i == 0:
                        if ci == 0:
                            load_chain(0)
                        load_v(ci)
                        if ci + 1 < len(chains):
                            load_chain(ci + 1)
                    z = zps[n % 3]
                    PE("matmul", [kb[p], qb[p]], [z], out=z.t[:, o:512], lhsT=kb[p].t[:, kt * 128:(kt + 1) * 128],
                       rhs=qb[p].t[:, o:512], start=True, stop=True)

                def st1(n):
                    ci, i, kt, o = tiles[n]
                    QB, h, nk = chains[ci]
                    z = zps[n % 3]
                    e = eb[n % 4]
                    sp = spb[n % 4]
                    A("activation", [z], [e], out=e.t[:, o:512], in_=z.t[:, o:512], func=AF.Exp, scale=SC)
                    if kt >= 4 * QB:
                        V("tensor_tensor", [e, mask0], [e], out=e.t[:, o:512], in0=e.t[:, o:512],
                          in1=mask0.t[:, 0:512 - o], op=ALU.mult)
                    A("activation", [e, cone], [sp], out=sp.t[:, o:512], in_=e.t[:, o:512], func=AF.Ln,
                      bias=cone.t[:, 0:1])

                def st2(n):
                    ci, i, kt, o = tiles[n]
                    QB, h, nk = chains[ci]
                    sp = spb[n % 4]
                    S = Sb[ci % 2]
                    f = fps[0]
                    if i == 0:
                        G("memset", [], [S], ap=S.t[:], constant=0.0)
                    PE("matmul", [tri, sp], [f], out=f.t[:, o:512], lhsT=tri.t[:], rhs=sp.t[:, o:512], start=True,
                       stop=(i == 0))
                    if i > 0:
                        PE("matmul", [ones_bf, S], [f], out=f.t[:, o:512], lhsT=ones_bf.t[:], rhs=S.t[:, o:512],
                           start=False, stop=True)
                    if i < nk - 1:
                        V("tensor_tensor", [S, sp], [S], out=S.t[:, o:512], in0=S.t[:, o:512], in1=sp.t[:, o:512],
                          op=ALU.add)
                    g = gbf[n % 2]
                    A("activation", [f], [g], out=g.t[:, o:512], in_=f.t[:, o:512], func=AF.Exp, scale=-1.0)
                    a = ab[n % 3]
                    e = eb[n % 4]
                    if o > 0:
                        G("memset", [], [a], ap=a.t[:, 0:o], constant=0.0)
                    V("tensor_tensor", [e, g], [a], out=a.t[:, o:512], in0=e.t[:, o:512], in1=g.t[:, o:512],
                      op=ALU.mult)

                def st3(n):
                    ci, i, kt, o = tiles[n]
                    QB, h, nk = chains[ci]
                    p = ci % 3
                    a = ab[n % 3]
                    op_ = ops[ci % 2]
                    PE("matmul", [vb[p], a], [op_], out=op_.t[:], lhsT=vb[p].t[:, kt, :], rhs=a.t[:],
                       start=(i == 0), stop=(i == nk - 1))
                    if i == nk - 1:
                        o_ = ob[QB % 2]
                        V("tensor_copy", [op_], [o_], out=o_.t[:, h, :], in_=op_.t[:])
                        if h == 7:
                            group_rstd([(o_, o_.t[:, hh, :]) for hh in range(8)], 1024, sqb, rs2, pb=bg_bps())
                            for hh in range(8):
                                gate_store((o_, o_.t[:, hh, :]), rs2, P.t[:, PP_GNA + hh:PP_GNA + hh + 1],
                                           512 + hh * 128, QB * 512, tmp[hh % 2], gate[hh % 2], yo[hh % 2])

                for s_ in range(N + 6):
                    if s_ < N:
                        st0(s_)
                    if 0 <= s_ - 2 < N:
                        st1(s_ - 2)
                    if 0 <= s_ - 4 < N:
                        st2(s_ - 4)
                    if 0 <= s_ - 6 < N:
                        st3(s_ - 6)
                    if not bg_state["done"]:
                        bg_state["acc"] += BG_ITEMS * 1.1 / N
                        while bg_state["acc"] >= 1.0 and not bg_state["done"]:
                            bg_state["acc"] -= 1.0
                            try:
                                next(bgen)
                            except StopIteration:
                                bg_state["done"] = True
                for _ in bgen:
                    pass
                fw.barrier()

        def phase_memkv(l):
            fw.phase = "phase_memkv"
            with ExitStack() as st:
                wkv = sb(st, "p0_wkv", [128, 16, 1024], BF16)
                gt = sb(st, "p0_g", [128, D], F32)
                xb = [sb(st, f"p0_x{i}", [128, D], F32) for i in range(2)]
                hb = [sb(st, f"p0_h{i}", [128, D], BF16) for i in range(2)]
                junk = sb(st, "p0_junk", [128, D], BF16)
                ssq = [sb(st, f"p0_ssq{i}", [128, 1], F32) for i in range(2)]
                rstd = [sb(st, f"p0_rstd{i}", [128, 1], F32) for i in range(2)]
                memT = sb(st, "p0_memT", [128, 16, MEM], BF16)
                fw.dma("gpsimd", wkv.t[:], w_kv[l].rearrange("(c p) n -> p c n", p=128), [], [wkv], wkv)
                fw.dma("sync", gt.t[:], gbc[4 + l], [], [gt], gt)
                for kt in range(2):
                    fw.dma("sync", xb[kt].t[:], mem_in[kt * 128:(kt + 1) * 128, :], [], [xb[kt]], xb[kt])
                    norm_to_hT(xb[kt], gt, hb[kt], junk, ssq[kt], rstd[kt], memT, 0,
                               oview=lambda c0, c1, kt=kt: memT.t[:, c0:c1, kt * 128:(kt + 1) * 128])
                for h in range(4):
                    pb = nps()
                    for c in range(16):
                        PE("matmul", [wkv, memT], [pb], inc=(c == 15), out=pb.t[:, 0:MEM],
                           lhsT=wkv.t[:, c, h * 128:(h + 1) * 128], rhs=memT.t[:, c, :], start=(c == 0),
                           stop=(c == 15))
                    V("tensor_copy", [pb], [k2T], out=k2T.t[:, h, :], in_=pb.t[:, 0:MEM])
                for kt in range(2):
                    pb = nps()
                    for c in range(16):
                        PE("matmul", [wkv, memT], [pb], inc=(c == 15), out=pb.t[:],
                           lhsT=memT.t[:, c, kt * 128:(kt + 1) * 128], rhs=wkv.t[:, c, 512:1024],
                           start=(c == 0), stop=(c == 15))
                    V("tensor_copy", [pb], [v2], out=v2.t[:, kt, :], in_=pb.t[:])
                fw.barrier()

        def phase_wout(l):
            fw.phase = "phase_wout"
            with ExitStack() as st:
                wout = sb(st, "p6_wout", [128, 16, D], BF16)
                gt = sb(st, "p6_g", [128, D], F32)
                yT = [sb(st, f"p6_yT{i}", [128, 16, 512], BF16) for i in range(2)]
                xb = [sb(st, f"p6_x{i}", [128, D], F32) for i in range(2)]
                hb = [sb(st, f"p6_h{i}", [128, D], BF16) for i in range(2)]
                junk = sb(st, "p6_junk", [128, D], BF16)
                ssq = [sb(st, f"p6_ssq{i}", [128, 1], F32) for i in range(2)]
                rstd = [sb(st, f"p6_rstd{i}", [128, 1], F32) for i in range(2)]
                hTt = [sb(st, f"p6_hT{i}", [128, 16, 128], BF16) for i in range(2)]
                fw.dma("gpsimd", wout.t[:], w_out[l].rearrange("(c p) n -> p c n", p=128), [], [wout], wout)
                fw.dma("sync", gt.t[:], gbc[2 + l], [], [gt], gt)
                xsrc = x_in if l == 0 else xres
                yv = y_scr.rearrange("(c p) t -> p c t", p=128)
                for tb in range(NTB):
                    y = yT[tb % 2]
                    fw.dma("sync", y.t[:], yv[:, :, tb * 512:(tb + 1) * 512], [], [y], y)
                    for tq in range(4):
                        tt = tb * 4 + tq
                        i = tt % 2
                        x = xb[i]
                        fw.dma("sync", x.t[:], xsrc[tt * 128:(tt + 1) * 128, :], [], [x], x)
                        for nb in range(4):
                            pb = nps()
                            for c in range(16):
                                PE("matmul", [y, wout], [pb], inc=(c == 15), out=pb.t[:],
                                   lhsT=y.t[:, c, tq * 128:(tq + 1) * 128], rhs=wout.t[:, c, nb * 512:(nb + 1) * 512],
                                   start=(c == 0), stop=(c == 15))
                            V("tensor_tensor", [pb, x], [x], out=x.t[:, nb * 512:(nb + 1) * 512], in0=pb.t[:],
                              in1=x.t[:, nb * 512:(nb + 1) * 512], op=ALU.add)
                        fw.dma("sync", xres[tt * 128:(tt + 1) * 128, :], x.t[:], [x], [], x)
                        norm_to_hT(x, gt, hb[i], junk, ssq[i], rstd[i], hTt[i], 0,
                                   dst=hT_scr[:, :, tt * 128:(tt + 1) * 128])
                fw.barrier()

        def phase_xattn(l):
            fw.phase = "phase_xattn"
            last = (l == nl - 1)
            SC = 128.0 ** -0.5
            with ExitStack() as st:
                wq = sb(st, "p7_wq", [128, 16, 512], BF16)
                wo = sb(st, "p7_wo", [128, 4, D], BF16)
                gt = sb(st, "p7_g", [128, D], F32)
                h2T = [sb(st, f"p7_h2T{i}", [128, 16, 512], BF16) for i in range(2)]
                q2 = sb(st, "p7_q2", [128, 4, 512], BF16)
                Eb = [sb(st, f"p7_E{i}", [128, 512], BF16) for i in range(4)]
                rden = [sb(st, f"p7_rd{i}", [128, 512], F32) for i in range(2)]
                o2 = [sb(st, f"p7_o2{i}", [128, 4, 512], BF16) for i in range(2)]
                xb = [sb(st, f"p7_x{i}", [128, D], F32) for i in range(2)]
                ob_ = [sb(st, f"p7_ob{i}", [128, D], F32) for i in range(2)]
                hb = [sb(st, f"p7_h{i}", [128, D], BF16) for i in range(2)]
                junk = sb(st, "p7_junk", [128, D], BF16)
                ssq = [sb(st, f"p7_ssq{i}", [128, 1], F32) for i in range(2)]
                rstd = [sb(st, f"p7_rstd{i}", [128, 1], F32) for i in range(2)]
                hTt = [sb(st, f"p7_hT{i}", [128, 16, 128], BF16) for i in range(2)]
                fw.dma("gpsimd", wq.t[:], w_q[l].rearrange("(c p) n -> p c n", p=128), [], [wq], wq)
                fw.dma("gpsimd", wo.t[:], w_o[l].rearrange("(c p) n -> p c n", p=128), [], [wo], wo)
                fw.dma("sync", gt.t[:], gbc[6] if last else gbc[l + 1], [], [gt], gt)
                ne = 0
                for tb in range(NTB):
                    t0 = tb * 512
                    hT = h2T[tb % 2]
                    fw.dma("sync", hT.t[:], hT_scr[:, :, t0:t0 + 512], [], [hT], hT)
                    for h in range(4):
                        pb = nps()
                        for c in range(16):
                            PE("matmul", [wq, hT], [pb], inc=(c == 15), out=pb.t[:],
                               lhsT=wq.t[:, c, h * 128:(h + 1) * 128], rhs=hT.t[:, c, :], start=(c == 0),
                               stop=(c == 15))
                        V("tensor_copy", [pb], [q2], out=q2.t[:, h, :], in_=pb.t[:])
                    o2t = o2[tb % 2]
                    for h in range(4):
                        es = []
                        for kt in range(2):
                            pb = nps()
                            PE("matmul", [k2T, q2], [pb], out=pb.t[:], lhsT=k2T.t[:, h, kt * 128:(kt + 1) * 128],
                               rhs=q2.t[:, h, :], start=True, stop=True)
                            e = Eb[ne % 4]
                            ne += 1
                            A("activation", [pb], [e], out=e.t[:], in_=pb.t[:], func=AF.Exp, scale=SC)
                            es.append(e)
                        po = nps()
                        for kt in range(2):
                            PE("matmul", [v2, es[kt]], [po], out=po.t[:], lhsT=v2.t[:, kt, h * 128:(h + 1) * 128],
                               rhs=es[kt].t[:], start=(kt == 0), stop=(kt == 1))
                        pd = nps()
                        for kt in range(2):
                            PE("matmul", [ones_bf, es[kt]], [pd], out=pd.t[:], lhsT=ones_bf.t[:], rhs=es[kt].t[:],
                               start=(kt == 0), stop=(kt == 1))
                        rd = rden[h % 2]
                        V("reciprocal", [pd], [rd], out=rd.t[:], in_=pd.t[:])
                        V("tensor_tensor", [po, rd], [o2t], out=o2t.t[:, h, :], in0=po.t[:], in1=rd.t[:], op=ALU.mult)
                    for tq in range(4):
                        tt = tb * 4 + tq
                        i = tt % 2
                        x = xb[i]
                        fw.dma("sync", x.t[:], xres[tt * 128:(tt + 1) * 128, :], [], [x], x)
                        for nb in range(4):
                            pb = nps()
                            for hh in range(4):
                                PE("matmul", [o2t, wo], [pb], inc=(hh == 3), out=pb.t[:],
                                   lhsT=o2t.t[:, hh, tq * 128:(tq + 1) * 128], rhs=wo.t[:, hh, nb * 512:(nb + 1) * 512],
                                   start=(hh == 0), stop=(hh == 3))
                            V("tensor_tensor", [pb, x], [x], out=x.t[:, nb * 512:(nb + 1) * 512], in0=pb.t[:],
                              in1=x.t[:, nb * 512:(nb + 1) * 512], op=ALU.add)
                        if last:
                            norm_stats(x, junk, ssq[i], rstd[i], ceps6)
                            V("scalar_tensor_tensor", [x, rstd[i], gt], [ob_[i]], out=ob_[i].t[:], in0=x.t[:],
                              scalar=rstd[i].t[:, 0:1], in1=gt.t[:], op0=ALU.mult, op1=ALU.mult)
                            tokf = fw.dma("sync", out[tt * 128:(tt + 1) * 128, :], ob_[i].t[:], [ob_[i]], [], ob_[i])
                            final_toks.append(tokf)
                        else:
                            fw.dma("sync", xres[tt * 128:(tt + 1) * 128, :], x.t[:], [x], [], x)
                            norm_to_hT(x, gt, hb[i], junk, ssq[i], rstd[i], hTt[i], 0,
                                       dst=hT_scr[:, :, tt * 128:(tt + 1) * 128])
                fw.barrier()

        final_toks = []
        for l in range(nl):
            phase_inproj(l)
            if stop == f"p2_{l}":
                return nc
            phase_attn(l)
            if stop == f"p4_{l}":
                return nc
            phase_memkv(l)
            phase_wout(l)
            if stop == f"p6_{l}":
                return nc
            phase_xattn(l)
            if stop == f"p7_{l}":
                return nc
    return nc


_CACHE = {}


def _layout_inputs(inp, T, nl=NL):
    f = np.float32
    W = np.asarray(inp["w_in"], f)
    w_in = np.empty((nl, NCH, 128, 16, 256), f)
    for l in range(nl):
        for ci, (kind, ca, cb, aux) in enumerate(CHUNKS):
            blk = np.concatenate([W[l][:, ca:ca + 128], W[l][:, cb:cb + 128]], axis=1)
            w_in[l, ci] = blk.reshape(16, 128, 256).transpose(1, 0, 2)
    gl = [inp["mix_norm_g"][0], inp["mix_norm_g"][1], inp["xattn_norm_g"][0], inp["xattn_norm_g"][1],
          inp["mem_norm_g"][0], inp["mem_norm_g"][1], inp["final_norm_g"]]
    gbc = np.stack([np.broadcast_to(np.asarray(g, f)[None, :], (128, D)) for g in gl]).copy()

    def cols(v):
        v = np.asarray(v, f)
        return v.reshape(-1, 128).T

    pp = np.zeros((nl, 128, NPP), f)
    for l in range(nl):
        dw = np.asarray(inp["conv_dw_w"][l], f)
        pp[l, :, PP_DW:PP_DW + 124] = dw.T.reshape(4, 128, CK).transpose(1, 0, 2).reshape(128, 124)
        pp[l, :, PP_DWB:PP_DWB + 4] = cols(inp["conv_dw_b"][l])
        pp[l, :, PP_LNG:PP_LNG + 4] = cols(inp["conv_ln_g"][l])
        pp[l, :, PP_LNB:PP_LNB + 4] = cols(inp["conv_ln_b"][l])
        lw = np.asarray(inp["lru_conv_w"][l], f)
        pp[l, :, PP_LCW:PP_LCW + 16] = lw.T.reshape(4, 128, 4).transpose(1, 0, 2).reshape(128, 16)
        pp[l, :, PP_LCB:PP_LCB + 4] = cols(inp["lru_conv_b"][l])
        pp[l, :, PP_BA:PP_BA + 4] = cols(inp["lru_ba"][l])
        pp[l, :, PP_BX:PP_BX + 4] = cols(inp["lru_bx"][l])
        pp[l, :, PP_LAM:PP_LAM + 4] = cols(inp["lru_lambda"][l])
        pp[l, :, PP_GNC:PP_GNC + 4] = cols(inp["out_norm_conv"][l])
        pp[l, :, PP_GNA:PP_GNA + 8] = cols(inp["out_norm_attn"][l])
        pp[l, :, PP_GNL:PP_GNL + 4] = cols(inp["out_norm_lru"][l])
    common = {
        "w_in": w_in,
        "w_out": np.ascontiguousarray(inp["w_out"], f),
        "wq": np.ascontiguousarray(inp["xattn_wq"], f),
        "wkv": np.ascontiguousarray(inp["xattn_wkv"], f),
        "wo": np.ascontiguousarray(inp["xattn_wo"], f),
        "pw": np.ascontiguousarray(inp["conv_pw_w"], f),
        "wa": np.ascontiguousarray(np.asarray(inp["lru_wa"], f).reshape(nl, 512, 128)),
        "wx": np.ascontiguousarray(np.asarray(inp["lru_wx"], f).reshape(nl, 512, 128)),
        "gbc": gbc,
        "pp": pp,
    }
    return common


def kernel(**inputs):
    x = np.asarray(inputs["x"], np.float32)
    mem = np.asarray(inputs["mem"], np.float32)
    B, S, _ = x.shape
    common = _layout_inputs(inputs, S)
    key = ("full", S)
    if key not in _CACHE:
        _CACHE[key] = build_program(S)
    nc = _CACHE[key]
    in_maps = []
    for c in range(8):
        b = c % B
        m = dict(common)
        m["x"] = np.ascontiguousarray(x[b])
        m["mem"] = np.ascontiguousarray(mem[b])
        in_maps.append(m)
    res = run_bass_kernel_spmd(nc, in_maps, core_ids=list(range(8)))
    return np.stack([res.results[b]["out"] for b in range(B)], axis=0)
```

```python
import numpy as np
from contextlib import ExitStack
import concourse.bass as bass
import concourse.mybir as mybir
from concourse.bass_utils import run_bass_kernel_spmd

F32 = mybir.dt.float32
BF16 = mybir.dt.bfloat16
AF = mybir.ActivationFunctionType
ALU = mybir.AluOpType

D = 2048
NL = 2
MEM = 256
INW = 6656
CK = 31
PAD = 32
SEM_LIMIT = 16000
SAME_ENG_SYNC = True

PP_DW = 0
PP_DWB = 124
PP_LNG = 128
PP_LNB = 132
PP_LCW = 136
PP_LCB = 152
PP_BA = 156
PP_BX = 160
PP_LAM = 164
PP_GNC = 168
PP_GNA = 172
PP_GNL = 180
NPP = 184

def _chunks():
    ch = []
    for j in range(4):
        ch.append(("vg", j * 128, 512 + j * 128, j))
    for j in range(2):
        ch.append(("gate", 1024 + j * 256, 1024 + j * 256 + 128, 0 + j * 256))
    for j in range(4):
        ch.append(("gate", 4608 + j * 256, 4608 + j * 256 + 128, 512 + j * 256))
    for j in range(2):
        ch.append(("gate", 6144 + j * 256, 6144 + j * 256 + 128, 1536 + j * 256))
    for j in range(4):
        ch.append(("q", 1536 + j * 256, 1536 + j * 256 + 128, j * 256))
    for j in range(4):
        ch.append(("k", 2560 + j * 256, 2560 + j * 256 + 128, j * 256))
    for j in range(2):
        ch.append(("rx", 5632 + j * 256, 5632 + j * 256 + 128, j * 256))
    for j in range(4):
        ch.append(("v", 3584 + j * 256, 3584 + j * 256 + 128, j * 2))
    return ch


CHUNKS = _chunks()
NCH = len(CHUNKS)


class Res:
    __slots__ = ("name", "w", "rs", "t", "ds")

    def __init__(self, name, t=None):
        self.name = name
        self.w = None
        self.rs = {}
        self.t = t
        self.ds = None


class EngState:
    def __init__(self, name, eng, sem):
        self.name = name
        self.eng = eng
        self.sem = sem
        self.cnt = 0
        self.waited = {}
        self.pending = False


class FW:
    def __init__(self, nc, stack, nsem):
        self.nc = nc
        self.stack = stack
        self.semi = 0
        self.E = {}
        for n in ("tensor", "vector", "scalar", "gpsimd", "sync"):
            self.E[n] = EngState(n, getattr(nc, n), self.new_sem())
        self.phase = "top"
        self.names = {}
        self.dpool = {"sync": [], "gpsimd": []}
        self.dlive = []
        self.dall = []

    def new_sem(self):
        s = self.stack.enter_context(self.nc.semaphore(f"s{self.semi}"))
        self.semi += 1
        return s

    def _wait(self, E, tok):
        sem, cnt = tok
        k = id(sem)
        if E.waited.get(k, 0) >= cnt:
            return
        E.eng.wait_ge(sem, cnt)
        E.waited[k] = cnt

    @staticmethod
    def _deps(reads, writes):
        toks = []
        for r in reads:
            if r.w is not None:
                toks.append(r.w)
        for w in writes:
            if w.w is not None:
                toks.append(w.w)
            toks.extend(w.rs.values())
        return toks

    @staticmethod
    def _record(tok, reads, writes):
        for r in reads:
            k = id(tok[0])
            o = r.rs.get(k)
            if o is None or o[1] < tok[1]:
                r.rs[k] = tok
        for w in writes:
            w.w = tok
            w.rs = {}

    def op(self, en, meth, reads, writes, inc=True, **kw):
        E = self.E[en]
        if E.cnt >= SEM_LIMIT and not E.pending:
            E.sem = self.new_sem()
            E.cnt = 0
        for tok in self._deps(reads, writes):
            if tok[0] is E.sem and (en == "tensor" or (not SAME_ENG_SYNC and en != "gpsimd")):
                continue
            self._wait(E, tok)
        ins = getattr(E.eng, meth)(**kw)
        if self.names is not None:
            self.names[ins.ins.name] = self.phase
        tok = (E.sem, E.cnt + 1)
        if inc:
            ins.then_inc(E.sem, 1)
            E.cnt += 1
            E.pending = False
        else:
            E.pending = True
        self._record(tok, reads, writes)
        return tok

    def _dsem(self, owner, qn):
        if owner.ds is None or owner.ds[1] >= SEM_LIMIT:
            pool = self.dpool[qn]
            pool.sort(key=lambda d: -d[1])
            if pool and pool[-1][1] < SEM_LIMIT - 4000:
                owner.ds = pool.pop()
            else:
                owner.ds = [self.new_sem(), 0, qn]
                self.dall.append(owner.ds)
            self.dlive.append(owner.ds)
        assert owner.ds[2] == qn, (owner.name, qn)
        return owner.ds

    def dma(self, qn, out, in_, reads, writes, owner, **kw):
        E = self.E[qn]
        for tok in self._deps(reads, writes):
            self._wait(E, tok)
        ds = self._dsem(owner, qn)
        ins = E.eng.dma_start(out=out, in_=in_, **kw)
        ins.then_inc(ds[0], 16)
        if self.names is not None:
            self.names[ins.ins.name] = self.phase
        ds[1] += 16
        tok = (ds[0], ds[1])
        self._record(tok, reads, writes)
        return tok

    def barrier(self):
        toks = []
        for E in self.E.values():
            assert not E.pending
            if E.cnt > 0:
                toks.append((E.sem, E.cnt))
        for ds in self.dlive:
            if ds[1] > 0:
                toks.append((ds[0], ds[1]))
        for E in self.E.values():
            for tok in toks:
                if tok[0] is E.sem:
                    continue
                self._wait(E, tok)
        for d in self.dlive:
            self.dpool[d[2]].append(d)
        self.dlive = []


def build_program(T, debug=False, stop=None, nl=NL):
    nc = bass.Bass("TRN2", target_bir_lowering=False)
    nc._fw_names = {}
    NTT = T // 128
    NTB = T // 512
    SBT = min(2048, T)
    NSB = T // SBT
    dbg_kind = "ExternalOutput" if debug else "Internal"

    def din(name, shape, dt=F32):
        return nc.dram_tensor(name, list(shape), dt, kind="ExternalInput").ap()

    def dscr(name, shape, dt):
        return nc.dram_tensor(name, list(shape), dt, kind=dbg_kind).ap()

    x_in = din("x", [T, D])
    mem_in = din("mem", [MEM, D])
    w_in = din("w_in", [nl, NCH, 128, 16, 256])
    w_out = din("w_out", [nl, D, D])
    w_q = din("wq", [nl, D, 512])
    w_kv = din("wkv", [nl, D, 1024])
    w_o = din("wo", [nl, 512, D])
    w_pw = din("pw", [nl, 512, 512])
    w_a = din("wa", [nl, 512, 128])
    w_x = din("wx", [nl, 512, 128])
    gbc = din("gbc", [7, 128, D])
    pp_in = din("pp", [nl, 128, NPP])
    out = nc.dram_tensor("out", [T, D], F32, kind="ExternalOutput").ap()

    xres = dscr("xres", [T, D], F32)
    hT_scr = dscr("hT_scr", [128, 16, T], BF16)
    u_scr = dscr("u_scr", [512, PAD + T], F32)
    rx_scr = dscr("rx_scr", [512, PAD + T], F32)
    g_scr = dscr("g_scr", [D, T], BF16)
    q_scr = dscr("q_scr", [1024, T], BF16)
    k_scr = dscr("k_scr", [1024, T], BF16)
    v_scr = dscr("v_scr", [8, 128, NTT, 128], BF16)
    y_scr = dscr("y_scr", [D, T], BF16)

    with ExitStack() as top:
        fw = FW(nc, top, 150)
        fw.names = nc._fw_names
        psb = []
        for i in range(8):
            t = top.enter_context(nc.psum_tensor(f"ps{i}", [128, 512], F32))
            psb.append(Res(f"ps{i}", t))
        prot = [0]

        def nps():
            r = psb[prot[0] % 8]
            prot[0] += 1
            return r

        uniq = [0]

        def sb(st, name, shape, dt):
            uniq[0] += 1
            name = f"{name}_{uniq[0]}"
            t = st.enter_context(nc.sbuf_tensor(name, list(shape), dt))
            return Res(name, t)

        def V(meth, reads, writes, **kw):
            return fw.op("vector", meth, reads, writes, **kw)

        def A(meth, reads, writes, **kw):
            return fw.op("scalar", meth, reads, writes, **kw)

        def G(meth, reads, writes, **kw):
            return fw.op("gpsimd", meth, reads, writes, **kw)

        def PE(meth, reads, writes, inc=True, **kw):
            return fw.op("tensor", meth, reads, writes, inc=inc, **kw)

        ident = sb(top, "ident", [128, 128], BF16)
        tri = sb(top, "tri", [128, 128], BF16)
        ones_bf = sb(top, "ones_bf", [128, 128], BF16)
        ones_f = sb(top, "ones_f", [128, 128], F32)
        mask0 = sb(top, "mask0", [128, 512], F32)
        onesw = sb(top, "onesw", [128, 512], F32)
        ceps6 = sb(top, "ceps6", [128, 1], F32)
        ceps5 = sb(top, "ceps5", [128, 1], F32)
        cone = sb(top, "cone", [128, 1], F32)
        czero = sb(top, "czero", [128, 1], F32)
        zpad = sb(top, "zpad", [128, PAD], F32)
        pp = [sb(top, f"pp{l}", [128, NPP], F32) for l in range(nl)]
        k2T = sb(top, "k2T", [128, 4, MEM], BF16)
        v2 = sb(top, "v2", [128, 2, 512], BF16)

        G("memset", [], [onesw], ap=onesw.t[:], constant=1.0)
        G("memset", [], [ones_f], ap=ones_f.t[:], constant=1.0)
        G("memset", [], [ones_bf], ap=ones_bf.t[:], constant=1.0)
        G("memset", [], [ceps6], ap=ceps6.t[:], constant=1e-6)
        G("memset", [], [ceps5], ap=ceps5.t[:], constant=1e-5)
        G("memset", [], [cone], ap=cone.t[:], constant=1.0)
        G("memset", [], [czero], ap=czero.t[:], constant=0.0)
        G("memset", [], [zpad], ap=zpad.t[:], constant=0.0)
        G("affine_select", [ones_bf], [ident], out=ident.t[:], in_=ones_bf.t[:], pattern=[[1, 128]],
          compare_op=ALU.is_equal, fill=0.0, base=0, channel_multiplier=-1)
        G("affine_select", [ones_bf], [tri], out=tri.t[:], in_=ones_bf.t[:], pattern=[[-1, 128]],
          compare_op=ALU.is_ge, fill=0.0, base=0, channel_multiplier=1)
        G("affine_select", [onesw], [mask0], out=mask0.t[:], in_=onesw.t[:], pattern=[[1, 512]],
          compare_op=ALU.is_gt, fill=0.0, base=0, channel_multiplier=-1)
        for l in range(nl):
            fw.dma("sync", pp[l].t[:], pp_in[l], [], [pp[l]], pp[l])
        for j in range(4):
            fw.dma("sync", u_scr[j * 128:(j + 1) * 128, 0:PAD], zpad.t[:], [zpad], [], zpad)
            fw.dma("sync", rx_scr[j * 128:(j + 1) * 128, 0:PAD], zpad.t[:], [zpad], [], zpad)
        if debug:
            dmask = nc.dram_tensor("dbg_mask", [128, 512], F32, kind="ExternalOutput").ap()
            dtri = nc.dram_tensor("dbg_tri", [128, 128], BF16, kind="ExternalOutput").ap()
            fw.dma("sync", dmask, mask0.t[:], [mask0], [], mask0)
            fw.dma("sync", dtri, tri.t[:], [tri], [], tri)
        fw.barrier()

        def norm_stats(xb, junk, ssq, rstd, eps_c):
            A("activation", [xb], [junk, ssq], out=junk.t[:], in_=xb.t[:], func=AF.Square,
              accum_out=ssq.t[:, 0:1])
            A("activation", [ssq, eps_c], [rstd], out=rstd.t[:, 0:1], in_=ssq.t[:, 0:1], func=AF.Sqrt,
              scale=1.0 / D, bias=eps_c.t[:, 0:1])
            V("reciprocal", [rstd], [rstd], out=rstd.t[:, 0:1], in_=rstd.t[:, 0:1])

        def norm_a(xb, gt, hb, junk, ssq, rstd):
            norm_stats(xb, junk, ssq, rstd, ceps6)
            V("scalar_tensor_tensor", [xb, rstd, gt], [hb], out=hb.t[:], in0=xb.t[:], scalar=rstd.t[:, 0:1],
              in1=gt.t[:], op0=ALU.mult, op1=ALU.mult)

        def norm_b(hb, hTt, dst=None, oview=None):
            for half in range(2):
                pb = nps()
                pv = pb.t[:].bitcast(BF16)
                for k in range(8):
                    c = half * 8 + k
                    PE("transpose", [hb, ident], [pb], inc=(k == 7), out=pv[:, k * 128:(k + 1) * 128],
                       in_=hb.t[:, c * 128:(c + 1) * 128], identity=ident.t[:])
                ov = hTt.t[:, half * 8:half * 8 + 8, :] if oview is None else oview(half * 8, half * 8 + 8)
                if half == 0:
                    A("activation", [pb], [hTt], out=ov,
                      in_=pv.rearrange("p (c k) -> p c k", k=128), func=AF.Copy)
                else:
                    V("tensor_copy", [pb], [hTt], out=ov,
                      in_=pv.rearrange("p (c k) -> p c k", k=128))
            if dst is not None:
                fw.dma("sync", dst, hTt.t[:], [hTt], [], hTt)

        def norm_to_hT(xb, gt, hb, junk, ssq, rstd, hTt, col0, ncols=128, dst=None, oview=None):
            norm_a(xb, gt, hb, junk, ssq, rstd)
            norm_b(hb, hTt, dst=dst, oview=oview)

        def phase_norm0():
            fw.phase = "phase_norm0"
            with ExitStack() as st:
                gt = sb(st, "p1_g", [128, D], F32)
                xb = [sb(st, f"p1_x{i}", [128, D], F32) for i in range(4)]
                hb = [sb(st, f"p1_h{i}", [128, D], BF16) for i in range(4)]
                junk = sb(st, "p1_junk", [128, D], BF16)
                ssq = [sb(st, f"p1_ssq{i}", [128, 1], F32) for i in range(4)]
                rstd = [sb(st, f"p1_rstd{i}", [128, 1], F32) for i in range(4)]
                hTt = [sb(st, f"p1_hT{i}", [128, 16, 128], BF16) for i in range(4)]
                fw.dma("sync", gt.t[:], gbc[0], [], [gt], gt)
                for tt in range(min(3, NTT)):
                    fw.dma("sync", xb[tt % 4].t[:], x_in[tt * 128:(tt + 1) * 128, :], [], [xb[tt % 4]], xb[tt % 4])
                for tt in range(NTT):
                    i = tt % 4
                    if tt + 3 < NTT:
                        i3 = (tt + 3) % 4
                        fw.dma("sync", xb[i3].t[:], x_in[(tt + 3) * 128:(tt + 4) * 128, :], [], [xb[i3]], xb[i3])
                    norm_to_hT(xb[i], gt, hb[i], junk, ssq[i], rstd[i], hTt[i], 0,
                               dst=hT_scr[:, :, tt * 128:(tt + 1) * 128])
                fw.barrier()

        phase_norm0()
        if stop == "p1":
            return nc

        def phase_inproj(l):
            fw.phase = "phase_inproj"
            with ExitStack() as st:
                hT = sb(st, "p2_hT", [128, 16, SBT], BF16)
                wb = [sb(st, f"p2_w{i}", [128, 16, 256], BF16) for i in range(3)]
                valb = [sb(st, f"p2_val{i}", [128, 512], F32) for i in range(4)]
                sig = [sb(st, f"p2_sig{i}", [128, 512], F32) for i in range(2)]
                ub = [sb(st, f"p2_u{i}", [128, 512], F32) for i in range(3)]
                gb = [sb(st, f"p2_g{i}", [128, 512], BF16) for i in range(3)]
                vb = [sb(st, f"p2_v{i}", [128, 256], BF16) for i in range(3)]
                rot = {"w": 0, "sig": 0, "u": 0, "g": 0, "v": 0}

                def nxt(lst, key):
                    r = lst[rot[key] % len(lst)]
                    rot[key] += 1
                    return r

                ntb = SBT // 512
                for sbi in range(NSB):
                    t0 = sbi * SBT
                    fw.dma("sync", hT.t[:], hT_scr[:, :, t0:t0 + SBT], [], [hT], hT)
                    for ci, (kind, ca, cb, aux) in enumerate(CHUNKS):
                        w = nxt(wb, "w")
                        fw.dma("gpsimd", w.t[:], w_in[l, ci], [], [w], w)
                        if kind == "v":
                            for tt in range(SBT // 128):
                                pb = nps()
                                for c in range(16):
                                    PE("matmul", [hT, w], [pb], inc=(c == 15), out=pb.t[:, 0:256],
                                       lhsT=hT.t[:, c, tt * 128:(tt + 1) * 128], rhs=w.t[:, c, :],
                                       start=(c == 0), stop=(c == 15))
                                v = nxt(vb, "v")
                                if tt % 2 == 0:
                                    V("tensor_copy", [pb], [v], out=v.t[:], in_=pb.t[:, 0:256])
                                else:
                                    A("activation", [pb], [v], out=v.t[:], in_=pb.t[:, 0:256], func=AF.Copy)
                                gt = (t0 // 128) + tt
                                for hh in range(2):
                                    fw.dma("sync", v_scr[aux + hh, :, gt, :], v.t[:, hh * 128:(hh + 1) * 128],
                                           [v], [], v)
                            continue
                        for half in range(2):
                            for tb in range(ntb):
                                pb = nps()
                                for c in range(16):
                                    PE("matmul", [hT, w], [pb], inc=(c == 15), out=pb.t[:, :],
                                       lhsT=w.t[:, c, half * 128:(half + 1) * 128],
                                       rhs=hT.t[:, c, tb * 512:(tb + 1) * 512],
                                       start=(c == 0), stop=(c == 15))
                                tg = t0 + tb * 512
                                if kind == "vg":
                                    if half == 0:
                                        V("tensor_copy", [pb], [valb[tb]], out=valb[tb].t[:], in_=pb.t[:])
                                    else:
                                        sg = nxt(sig, "sig")
                                        A("activation", [pb], [sg], out=sg.t[:], in_=pb.t[:], func=AF.Sigmoid)
                                        u = nxt(ub, "u")
                                        V("tensor_tensor", [valb[tb], sg], [u], out=u.t[:], in0=valb[tb].t[:],
                                          in1=sg.t[:], op=ALU.mult)
                                        fw.dma("sync", u_scr[aux * 128:(aux + 1) * 128, PAD + tg:PAD + tg + 512],
                                               u.t[:], [u], [], u)
                                elif kind == "gate":
                                    g = nxt(gb, "g")
                                    A("activation", [pb], [g], out=g.t[:], in_=pb.t[:], func=AF.Silu)
                                    r0 = aux + half * 128
                                    fw.dma("sync", g_scr[r0:r0 + 128, tg:tg + 512], g.t[:], [g], [], g)
                                elif kind in ("q", "k"):
                                    g = nxt(gb, "g")
                                    V("tensor_copy", [pb], [g], out=g.t[:], in_=pb.t[:])
                                    r0 = aux + half * 128
                                    dst = q_scr if kind == "q" else k_scr
                                    fw.dma("sync", dst[r0:r0 + 128, tg:tg + 512], g.t[:], [g], [], g)
                                else:
                                    u = nxt(ub, "u")
                                    V("tensor_copy", [pb], [u], out=u.t[:], in_=pb.t[:])
                                    r0 = aux + half * 128
                                    fw.dma("sync", rx_scr[r0:r0 + 128, PAD + tg:PAD + tg + 512], u.t[:], [u], [], u)
                fw.barrier()

        def group_rstd(ytiles, nfeat, sqb, rstd, pb=None):
            if pb is None:
                pb = nps()
            n = len(ytiles)
            for j, yt in enumerate(ytiles):
                if isinstance(yt, tuple):
                    yt, yap = yt
                else:
                    yap = yt.t[:]
                sq = sqb[j % len(sqb)]
                A("activation", [yt], [sq], out=sq.t[:], in_=yap, func=AF.Square)
                PE("matmul", [ones_f, sq], [pb], out=pb.t[:], lhsT=ones_f.t[:], rhs=sq.t[:],
                   start=(j == 0), stop=(j == n - 1))
            A("activation", [pb, ceps6], [rstd], out=rstd.t[:], in_=pb.t[:], func=AF.Sqrt, scale=1.0 / nfeat,
              bias=ceps6.t[:, 0:1])
            V("reciprocal", [rstd], [rstd], out=rstd.t[:], in_=rstd.t[:])

        def gate_store(yt, rstd, gn_ap, grow, t0, tmp, gate, yo, preloaded=False):
            if isinstance(yt, tuple):
                yt, yap = yt
            else:
                yap = yt.t[:]
            if not preloaded:
                fw.dma("sync", gate.t[:], g_scr[grow:grow + 128, t0:t0 + 512], [], [gate], gate)
            V("tensor_tensor", [yt, rstd], [tmp], out=tmp.t[:], in0=yap, in1=rstd.t[:], op=ALU.mult)
            V("scalar_tensor_tensor", [tmp, gate], [yo], out=yo.t[:], in0=tmp.t[:], scalar=gn_ap, in1=gate.t[:],
              op0=ALU.mult, op1=ALU.mult)
            fw.dma("sync", y_scr[grow:grow + 128, t0:t0 + 512], yo.t[:], [yo], [], yo)

        def phase_conv(l):
            fw.phase = "phase_conv"
            P = pp[l]
            with ExitStack() as st:
                pw = sb(st, "p3_pw", [128, 4, 512], BF16)
                uin = [[sb(st, f"p3_u{i}_{j}", [128, 30 + 512], F32) for j in range(4)] for i in range(2)]
                acc = [sb(st, f"p3_acc{j}", [128, 512], F32) for j in range(4)]
                sqb = [sb(st, f"p3_sq{i}", [128, 512], F32) for i in range(2)]
                mt = sb(st, "p3_m", [128, 512], F32)
                msq = sb(st, "p3_msq", [128, 512], F32)
                rs = sb(st, "p3_rs", [128, 512], F32)
                xn = [sb(st, f"p3_xn{i}", [128, 512], F32) for i in range(2)]
                sbf = [sb(st, f"p3_s{j}", [128, 512], BF16) for j in range(4)]
                yb = [sb(st, f"p3_y{j}", [128, 512], F32) for j in range(4)]
                rs2 = sb(st, "p3_rs2", [128, 512], F32)
                tmp = [sb(st, f"p3_t{i}", [128, 512], F32) for i in range(2)]
                gate = [sb(st, f"p3_g{i}", [128, 512], BF16) for i in range(2)]
                yo = [sb(st, f"p3_yo{i}", [128, 512], BF16) for i in range(2)]
                fw.dma("gpsimd", pw.t[:], w_pw[l].rearrange("(c p) n -> p c n", p=128), [], [pw], pw)
                for tb in range(NTB):
                    t0 = tb * 512
                    us = uin[tb % 2]
                    for tbl in ([0, 1] if tb == 0 else [tb + 1]):
                        if tbl >= NTB:
                            continue
                        tl = tbl * 512
                        for j in range(4):
                            ul = uin[tbl % 2][j]
                            fw.dma("sync", ul.t[:], u_scr[j * 128:(j + 1) * 128, PAD + tl - 30:PAD + tl + 512],
                                   [], [ul], ul)
                    for j in range(4):
                        wcol = PP_DW + j * CK
                        V("tensor_scalar", [us[j], P], [acc[j]], out=acc[j].t[:], in0=us[j].t[:, 30:542],
                          scalar1=P.t[:, wcol + 30:wcol + 31], scalar2=P.t[:, PP_DWB + j:PP_DWB + j + 1],
                          op0=ALU.mult, op1=ALU.add)
                    for k in range(30):
                        for j in range(4):
                            wcol = PP_DW + j * CK
                            V("scalar_tensor_tensor", [us[j], P, acc[j]], [acc[j]], out=acc[j].t[:],
                              in0=us[j].t[:, k:k + 512], scalar=P.t[:, wcol + k:wcol + k + 1], in1=acc[j].t[:],
                              op0=ALU.mult, op1=ALU.add)
                    p1 = nps()
                    for j in range(4):
                        PE("matmul", [ones_f, acc[j]], [p1], out=p1.t[:], lhsT=ones_f.t[:],
                           rhs=acc[j].t[:], start=(j == 0), stop=(j == 3))
                    p2 = nps()
                    for j in range(4):
                        sq = sqb[j % 2]
                        A("activation", [acc[j]], [sq], out=sq.t[:], in_=acc[j].t[:], func=AF.Square)
                        PE("matmul", [ones_f, sq], [p2], out=p2.t[:], lhsT=ones_f.t[:],
                           rhs=sq.t[:], start=(j == 0), stop=(j == 3))
                    A("activation", [p1], [mt], out=mt.t[:], in_=p1.t[:], func=AF.Copy, scale=1.0 / 512)
                    V("tensor_tensor", [mt], [msq], out=msq.t[:], in0=mt.t[:], in1=mt.t[:], op=ALU.mult)
                    V("scalar_tensor_tensor", [p2, msq], [rs], out=rs.t[:], in0=p2.t[:], scalar=1.0 / 512,
                      in1=msq.t[:], op0=ALU.mult, op1=ALU.subtract)
                    A("activation", [rs, ceps5], [rs], out=rs.t[:], in_=rs.t[:], func=AF.Sqrt,
                      bias=ceps5.t[:, 0:1])
                    V("reciprocal", [rs], [rs], out=rs.t[:], in_=rs.t[:])
                    for j in range(4):
                        x1 = xn[j % 2]
                        V("tensor_tensor", [acc[j], mt], [x1], out=x1.t[:], in0=acc[j].t[:], in1=mt.t[:],
                          op=ALU.subtract)
                        V("tensor_tensor", [x1, rs], [x1], out=x1.t[:], in0=x1.t[:], in1=rs.t[:], op=ALU.mult)
                        A("activation", [x1, P], [sbf[j]], out=sbf[j].t[:], in_=x1.t[:], func=AF.Silu,
                          scale=P.t[:, PP_LNG + j:PP_LNG + j + 1], bias=P.t[:, PP_LNB + j:PP_LNB + j + 1])
                    for co in range(4):
                        pb = nps()
                        for ci in range(4):
                            PE("matmul", [pw, sbf[ci]], [pb], out=pb.t[:],
                               lhsT=pw.t[:, ci, co * 128:(co + 1) * 128], rhs=sbf[ci].t[:],
                               start=(ci == 0), stop=(ci == 3))
                        V("tensor_copy", [pb], [yb[co]], out=yb[co].t[:], in_=pb.t[:])
                    group_rstd(yb, 512, sqb, rs2)
                    for co in range(4):
                        gate_store(yb[co], rs2, P.t[:, PP_GNC + co:PP_GNC + co + 1], co * 128, t0,
                                   tmp[co % 2], gate[co % 2], yo[co % 2])
                fw.barrier()

        def phase_lru(l):
            fw.phase = "phase_lru"
            P = pp[l]
            with ExitStack() as st:
                wa = sb(st, "p5_wa", [128, 4, 128], BF16)
                wx = sb(st, "p5_wx", [128, 4, 128], BF16)
                kc = sb(st, "p5_kc", [128, 4], F32)
                hc = [sb(st, f"p5_hc{j}", [128, 1], F32) for j in range(4)]
                rxin = [sb(st, f"p5_rx{i}", [128, 3 + 512], F32) for i in range(8)]
                xc = [sb(st, f"p5_xc{i}", [128, 512], F32) for i in range(4)]
                xcb = [sb(st, f"p5_xcb{i}", [128, 512], BF16) for i in range(4)]
                rt = [sb(st, f"p5_r{i}", [128, 512], F32) for i in range(4)]
                it = [sb(st, f"p5_i{i}", [128, 512], F32) for i in range(4)]
                at = [sb(st, f"p5_a{i}", [128, 512], F32) for i in range(4)]
                om = [sb(st, f"p5_om{i}", [128, 512], F32) for i in range(4)]
                bt = [sb(st, f"p5_b{i}", [128, 512], F32) for i in range(4)]
                hb = [sb(st, f"p5_h{j}", [128, 512], F32) for j in range(4)]
                sqb = [sb(st, f"p5_sq{i}", [128, 512], F32) for i in range(2)]
                rs2 = sb(st, "p5_rs2", [128, 512], F32)
                tmp = [sb(st, f"p5_t{i}", [128, 512], F32) for i in range(2)]
                gate = [sb(st, f"p5_g{i}", [128, 512], BF16) for i in range(2)]
                yo = [sb(st, f"p5_yo{i}", [128, 512], BF16) for i in range(2)]
                fw.dma("gpsimd", wa.t[:], w_a[l].rearrange("(n p) e -> p n e", p=128), [], [wa], wa)
                fw.dma("gpsimd", wx.t[:], w_x[l].rearrange("(n p) e -> p n e", p=128), [], [wx], wx)
                A("activation", [P], [kc], out=kc.t[:], in_=P.t[:, PP_LAM:PP_LAM + 4], func=AF.Exp, scale=-1.0)
                A("activation", [kc, cone], [kc], out=kc.t[:], in_=kc.t[:], func=AF.Ln, bias=cone.t[:, 0:1])
                V("tensor_scalar", [kc], [kc], out=kc.t[:], in0=kc.t[:], scalar1=-8.0, scalar2=None, op0=ALU.mult)
                for j in range(4):
                    G("memset", [], [hc[j]], ap=hc[j].t[:], constant=0.0)
                for tb in range(NTB):
                    t0 = tb * 512
                    J = range(4)
                    rxs = [rxin[(tb % 2) * 4 + j] for j in J]
                    for tbl in ([0, 1] if tb == 0 else [tb + 1]):
                        if tbl >= NTB:
                            continue
                        tl = tbl * 512
                        for j in J:
                            rl = rxin[(tbl % 2) * 4 + j]
                            fw.dma("sync", rl.t[:], rx_scr[j * 128:(j + 1) * 128, PAD + tl - 3:PAD + tl + 512],
                                   [], [rl], rl)
                    for j in J:
                        wcol = PP_LCW + j * 4
                        V("tensor_scalar", [rxs[j], P], [xc[j]], out=xc[j].t[:], in0=rxs[j].t[:, 3:515],
                          scalar1=P.t[:, wcol + 3:wcol + 4], scalar2=P.t[:, PP_LCB + j:PP_LCB + j + 1],
                          op0=ALU.mult, op1=ALU.add)
                    for k in range(3):
                        for j in J:
                            wcol = PP_LCW + j * 4
                            V("scalar_tensor_tensor", [rxs[j], P, xc[j]], [xc[j]], out=xc[j].t[:],
                              in0=rxs[j].t[:, k:k + 512], scalar=P.t[:, wcol + k:wcol + k + 1], in1=xc[j].t[:],
                              op0=ALU.mult, op1=ALU.add)
                    for j in J:
                        G("tensor_copy", [xc[j]], [xcb[j]], out=xcb[j].t[:], in_=xc[j].t[:])
                    prs, pis = [], []
                    for j in J:
                        pr = psb[2 * j]
                        PE("matmul", [wa, xcb[j]], [pr], out=pr.t[:], lhsT=wa.t[:, j, :], rhs=xcb[j].t[:],
                           start=True, stop=True)
                        pi = psb[2 * j + 1]
                        PE("matmul", [wx, xcb[j]], [pi], out=pi.t[:], lhsT=wx.t[:, j, :], rhs=xcb[j].t[:],
                           start=True, stop=True)
                        prs.append(pr)
                        pis.append(pi)
                    for j in J:
                        A("activation", [prs[j], P], [rt[j]], out=rt[j].t[:], in_=prs[j].t[:], func=AF.Sigmoid,
                          bias=P.t[:, PP_BA + j:PP_BA + j + 1])
                        A("activation", [pis[j], P], [it[j]], out=it[j].t[:], in_=pis[j].t[:], func=AF.Sigmoid,
                          bias=P.t[:, PP_BX + j:PP_BX + j + 1])
                    for j in J:
                        A("activation", [rt[j], kc], [at[j]], out=at[j].t[:], in_=rt[j].t[:], func=AF.Exp,
                          scale=kc.t[:, j:j + 1])
                    for j in J:
                        V("tensor_tensor", [at[j]], [om[j]], out=om[j].t[:], in0=at[j].t[:], in1=at[j].t[:],
                          op=ALU.mult)
                    for j in J:
                        V("tensor_scalar", [om[j]], [om[j]], out=om[j].t[:], in0=om[j].t[:], scalar1=-1.0,
                          scalar2=1.0, op0=ALU.mult, op1=ALU.add)
                    for j in J:
                        V("tensor_scalar", [om[j]], [om[j]], out=om[j].t[:], in0=om[j].t[:], scalar1=1e-30,
                          scalar2=None, op0=ALU.max)
                    for j in J:
                        A("activation", [om[j]], [om[j]], out=om[j].t[:], in_=om[j].t[:], func=AF.Sqrt)
                    for j in J:
                        V("tensor_tensor", [it[j], xc[j]], [bt[j]], out=bt[j].t[:], in0=it[j].t[:], in1=xc[j].t[:],
                          op=ALU.mult)
                    for j in J:
                        V("tensor_tensor", [bt[j], om[j]], [bt[j]], out=bt[j].t[:], in0=bt[j].t[:],
                          in1=om[j].t[:], op=ALU.mult)
                    for j in J:
                        V("tensor_tensor_scan", [at[j], bt[j], hc[j]], [hb[j]], out=hb[j].t[:],
                          data0=at[j].t[:], data1=bt[j].t[:], initial=hc[j].t[:, 0:1], op0=ALU.mult,
                          op1=ALU.add)
                    for j in J:
                        V("tensor_copy", [hb[j]], [hc[j]], out=hc[j].t[:, 0:1], in_=hb[j].t[:, 511:512])
                    group_rstd(hb, 512, sqb, rs2)
                    for j in range(4):
                        gate_store(hb[j], rs2, P.t[:, PP_GNL + j:PP_GNL + j + 1], 1536 + j * 128, t0,
                                   tmp[j % 2], gate[j % 2], yo[j % 2])
                fw.barrier()

        def make_bg_pool(st):
            return {
                "f32": [sb(st, f"bg_f{i}", [128, 512], F32) for i in range(22)],
                "b16": [sb(st, f"bg_b{i}", [128, 512], BF16) for i in range(10)],
                "uin": [[sb(st, f"bg_u{i}_{j}", [128, 30 + 512], F32) for j in range(4)] for i in range(2)],
                "pw": sb(st, "bg_pw", [128, 4, 512], BF16),
                "wa": sb(st, "bg_wa", [128, 4, 128], BF16),
                "wx": sb(st, "bg_wx", [128, 4, 128], BF16),
                "kc": sb(st, "bg_kc", [128, 4], F32),
                "hc": [sb(st, f"bg_hc{j}", [128, 1], F32) for j in range(4)],
                "banks": [psb[6], psb[7]],
                "brot": [0],
            }

        def gen_conv(l, pool):
            P = pp[l]
            F = pool["f32"]
            B = pool["b16"]
            pw = pool["pw"]
            uin = pool["uin"]
            acc = F[0:4]
            sqb = F[4:6]
            mt, msq, rs, rs2 = F[6], F[7], F[8], F[9]
            xn = F[10:12]
            yb = F[12:16]
            tmp = F[16:18]
            sbf = B[0:4]
            gate = B[4:8]
            yo = B[8:10]

            def bps():
                r = pool["banks"][pool["brot"][0] % 2]
                pool["brot"][0] += 1
                return r

            fw.dma("gpsimd", pw.t[:], w_pw[l].rearrange("(c p) n -> p c n", p=128), [], [pw], pw)
            for tb in range(NTB):
                t0 = tb * 512
                us = uin[tb % 2]
                for tbl in ([0, 1] if tb == 0 else [tb + 1]):
                    if tbl >= NTB:
                        continue
                    tl = tbl * 512
                    for j in range(4):
                        ul = uin[tbl % 2][j]
                        fw.dma("sync", ul.t[:], u_scr[j * 128:(j + 1) * 128, PAD + tl - 30:PAD + tl + 512],
                               [], [ul], ul)
                for j in range(4):
                    fw.dma("sync", gate[j].t[:], g_scr[j * 128:(j + 1) * 128, t0:t0 + 512], [], [gate[j]], gate[j])
                yield
                for j in range(4):
                    wcol = PP_DW + j * CK
                    V("tensor_scalar", [us[j], P], [acc[j]], out=acc[j].t[:], in0=us[j].t[:, 30:542],
                      scalar1=P.t[:, wcol + 30:wcol + 31], scalar2=P.t[:, PP_DWB + j:PP_DWB + j + 1],
                      op0=ALU.mult, op1=ALU.add)
                yield
                for k in range(30):
                    for j in range(4):
                        wcol = PP_DW + j * CK
                        V("scalar_tensor_tensor", [us[j], P, acc[j]], [acc[j]], out=acc[j].t[:],
                          in0=us[j].t[:, k:k + 512], scalar=P.t[:, wcol + k:wcol + k + 1], in1=acc[j].t[:],
                          op0=ALU.mult, op1=ALU.add)
                        yield
                p1 = bps()
                for j in range(4):
                    PE("matmul", [ones_f, acc[j]], [p1], out=p1.t[:], lhsT=ones_f.t[:], rhs=acc[j].t[:],
                       start=(j == 0), stop=(j == 3))
                p2 = bps()
                for j in range(4):
                    sq = sqb[j % 2]
                    A("activation", [acc[j]], [sq], out=sq.t[:], in_=acc[j].t[:], func=AF.Square)
                    PE("matmul", [ones_f, sq], [p2], out=p2.t[:], lhsT=ones_f.t[:], rhs=sq.t[:],
                       start=(j == 0), stop=(j == 3))
                A("activation", [p1], [mt], out=mt.t[:], in_=p1.t[:], func=AF.Copy, scale=1.0 / 512)
                V("tensor_tensor", [mt], [msq], out=msq.t[:], in0=mt.t[:], in1=mt.t[:], op=ALU.mult)
                V("scalar_tensor_tensor", [p2, msq], [rs], out=rs.t[:], in0=p2.t[:], scalar=1.0 / 512,
                  in1=msq.t[:], op0=ALU.mult, op1=ALU.subtract)
                A("activation", [rs, ceps5], [rs], out=rs.t[:], in_=rs.t[:], func=AF.Sqrt, bias=ceps5.t[:, 0:1])
                V("reciprocal", [rs], [rs], out=rs.t[:], in_=rs.t[:])
                yield
                for j in range(4):
                    x1 = xn[j % 2]
                    V("tensor_tensor", [acc[j], mt], [x1], out=x1.t[:], in0=acc[j].t[:], in1=mt.t[:],
                      op=ALU.subtract)
                    V("tensor_tensor", [x1, rs], [x1], out=x1.t[:], in0=x1.t[:], in1=rs.t[:], op=ALU.mult)
                    A("activation", [x1, P], [sbf[j]], out=sbf[j].t[:], in_=x1.t[:], func=AF.Silu,
                      scale=P.t[:, PP_LNG + j:PP_LNG + j + 1], bias=P.t[:, PP_LNB + j:PP_LNB + j + 1])
                    yield
                for co in range(4):
                    pb = bps()
                    for ci in range(4):
                        PE("matmul", [pw, sbf[ci]], [pb], out=pb.t[:], lhsT=pw.t[:, ci, co * 128:(co + 1) * 128],
                           rhs=sbf[ci].t[:], start=(ci == 0), stop=(ci == 3))
                    V("tensor_copy", [pb], [yb[co]], out=yb[co].t[:], in_=pb.t[:])
                    yield
                group_rstd(yb, 512, sqb, rs2, pb=bps())
                yield
                for co in range(4):
                    gate_store(yb[co], rs2, P.t[:, PP_GNC + co:PP_GNC + co + 1], co * 128, t0, tmp[co % 2],
                               gate[co], yo[co % 2], preloaded=True)
                    yield

        def gen_lru(l, pool):
            P = pp[l]
            F = pool["f32"]
            B = pool["b16"]
            wa, wx, kc, hc = pool["wa"], pool["wx"], pool["kc"], pool["hc"]
            rxin = [pool["uin"][0][j] for j in range(4)] + [pool["uin"][1][j] for j in range(4)]
            xc = F[0:4]
            rt = F[4:8]
            it = F[8:12]
            om = F[12:16]
            hb = F[16:20]
            sqb = F[20:22]
            rs2 = F[4]
            tmp = F[5:7]
            xcb = B[0:4]
            gate = B[4:8]
            yo = B[8:10]
            J = range(4)

            def bps():
                r = pool["banks"][pool["brot"][0] % 2]
                pool["brot"][0] += 1
                return r

            fw.dma("gpsimd", wa.t[:], w_a[l].rearrange("(n p) e -> p n e", p=128), [], [wa], wa)
            fw.dma("gpsimd", wx.t[:], w_x[l].rearrange("(n p) e -> p n e", p=128), [], [wx], wx)
            A("activation", [P], [kc], out=kc.t[:], in_=P.t[:, PP_LAM:PP_LAM + 4], func=AF.Exp, scale=-1.0)
            A("activation", [kc, cone], [kc], out=kc.t[:], in_=kc.t[:], func=AF.Ln, bias=cone.t[:, 0:1])
            V("tensor_scalar", [kc], [kc], out=kc.t[:], in0=kc.t[:], scalar1=-8.0, scalar2=None, op0=ALU.mult)
            for j in J:
                G("memset", [], [hc[j]], ap=hc[j].t[:], constant=0.0)
            yield
            for tb in range(NTB):
                t0 = tb * 512
                rxs = [rxin[(tb % 2) * 4 + j] for j in J]
                for tbl in ([0, 1] if tb == 0 else [tb + 1]):
                    if tbl >= NTB:
                        continue
                    tl = tbl * 512
                    for j in J:
                        rl = rxin[(tbl % 2) * 4 + j]
                        fw.dma("sync", rl.t[:, 0:515], rx_scr[j * 128:(j + 1) * 128, PAD + tl - 3:PAD + tl + 512],
                               [], [rl], rl)
                for j in J:
                    fw.dma("sync", gate[j].t[:], g_scr[1536 + j * 128:1536 + (j + 1) * 128, t0:t0 + 512], [],
                           [gate[j]], gate[j])
                yield
                for j in J:
                    wcol = PP_LCW + j * 4
                    V("tensor_scalar", [rxs[j], P], [xc[j]], out=xc[j].t[:], in0=rxs[j].t[:, 3:515],
                      scalar1=P.t[:, wcol + 3:wcol + 4], scalar2=P.t[:, PP_LCB + j:PP_LCB + j + 1],
                      op0=ALU.mult, op1=ALU.add)
                yield
                for k in range(3):
                    for j in J:
                        wcol = PP_LCW + j * 4
                        V("scalar_tensor_tensor", [rxs[j], P, xc[j]], [xc[j]], out=xc[j].t[:],
                          in0=rxs[j].t[:, k:k + 512], scalar=P.t[:, wcol + k:wcol + k + 1], in1=xc[j].t[:],
                          op0=ALU.mult, op1=ALU.add)
                    yield
                for j in J:
                    A("activation", [xc[j]], [xcb[j]], out=xcb[j].t[:], in_=xc[j].t[:], func=AF.Copy)
                yield
                for j in J:
                    pr = bps()
                    PE("matmul", [wa, xcb[j]], [pr], out=pr.t[:], lhsT=wa.t[:, j, :], rhs=xcb[j].t[:],
                       start=True, stop=True)
                    A("activation", [pr, P], [rt[j]], out=rt[j].t[:], in_=pr.t[:], func=AF.Sigmoid,
                      bias=P.t[:, PP_BA + j:PP_BA + j + 1])
                    pi = bps()
                    PE("matmul", [wx, xcb[j]], [pi], out=pi.t[:], lhsT=wx.t[:, j, :], rhs=xcb[j].t[:],
                       start=True, stop=True)
                    A("activation", [pi, P], [it[j]], out=it[j].t[:], in_=pi.t[:], func=AF.Sigmoid,
                      bias=P.t[:, PP_BX + j:PP_BX + j + 1])
                    yield
                for j in J:
                    A("activation", [rt[j], kc], [rt[j]], out=rt[j].t[:], in_=rt[j].t[:], func=AF.Exp,
                      scale=kc.t[:, j:j + 1])
                yield
                for j in J:
                    V("tensor_tensor", [rt[j]], [om[j]], out=om[j].t[:], in0=rt[j].t[:], in1=rt[j].t[:],
                      op=ALU.mult)
                yield
                for j in J:
                    V("tensor_scalar", [om[j]], [om[j]], out=om[j].t[:], in0=om[j].t[:], scalar1=-1.0,
                      scalar2=1.0, op0=ALU.mult, op1=ALU.add)
                yield
                for j in J:
                    V("tensor_scalar", [om[j]], [om[j]], out=om[j].t[:], in0=om[j].t[:], scalar1=1e-30,
                      scalar2=None, op0=ALU.max)
                yield
                for j in J:
                    A("activation", [om[j]], [om[j]], out=om[j].t[:], in_=om[j].t[:], func=AF.Sqrt)
                yield
                for j in J:
                    V("tensor_tensor", [it[j], xc[j]], [it[j]], out=it[j].t[:], in0=it[j].t[:], in1=xc[j].t[:],
                      op=ALU.mult)
                yield
                for j in J:
                    V("tensor_tensor", [it[j], om[j]], [it[j]], out=it[j].t[:], in0=it[j].t[:], in1=om[j].t[:],
                      op=ALU.mult)
                yield
                for j in J:
                    V("tensor_tensor_scan", [rt[j], it[j], hc[j]], [hb[j]], out=hb[j].t[:], data0=rt[j].t[:],
                      data1=it[j].t[:], initial=hc[j].t[:, 0:1], op0=ALU.mult, op1=ALU.add)
                    yield
                for j in J:
                    V("tensor_copy", [hb[j]], [hc[j]], out=hc[j].t[:, 0:1], in_=hb[j].t[:, 511:512])
                yield
                group_rstd(hb, 512, sqb, rs2, pb=bps())
                yield
                for j in J:
                    gate_store(hb[j], rs2, P.t[:, PP_GNL + j:PP_GNL + j + 1], 1536 + j * 128, t0, tmp[j % 2],
                               gate[j], yo[j % 2], preloaded=True)
                    yield

        def phase_attn(l):
            fw.phase = "phase_attn"
            P = pp[l]
            SC = 128.0 ** -0.5
            with ExitStack() as st:
                qb = [sb(st, f"p4_q{i}", [128, 512], BF16) for i in range(2)]
                kb = [sb(st, f"p4_k{i}", [128, T], BF16) for i in range(2)]
                vb = [sb(st, f"p4_v{i}", [128, NTT, 128], BF16) for i in range(3)]
                eb = [sb(st, f"p4_e{i}", [128, 512], F32) for i in range(4)]
                spb = [sb(st, f"p4_sp{i}", [128, 512], BF16) for i in range(4)]
                gbf = [sb(st, f"p4_gb{i}", [128, 512], F32) for i in range(2)]
                ab = [sb(st, f"p4_a{i}", [128, 512], BF16) for i in range(3)]
                Sb = [sb(st, f"p4_S{i}", [128, 512], BF16) for i in range(2)]
                ob = [sb(st, f"p4_o{i}", [128, 8, 512], F32) for i in range(2)]
                sqb = [sb(st, f"p4_sq{i}", [128, 512], F32) for i in range(2)]
                rs2 = sb(st, "p4_rs2", [128, 512], F32)
                tmp = [sb(st, f"p4_t{i}", [128, 512], F32) for i in range(2)]
                gate = [sb(st, f"p4_g{i}", [128, 512], BF16) for i in range(2)]
                yo = [sb(st, f"p4_yo{i}", [128, 512], BF16) for i in range(2)]
                zps = psb[0:3]
                fps = [psb[3]]
                ops = psb[4:6]
                pool = make_bg_pool(st)

                def bg_all():
                    yield from gen_conv(l, pool)
                    yield from gen_lru(l, pool)

                bgen = bg_all()

                def bg_bps():
                    r = pool["banks"][pool["brot"][0] % 2]
                    pool["brot"][0] += 1
                    return r
                BG_ITEMS = NTB * 150 + 1 + NTB * 45
                bg_state = {"acc": 0.0, "done": False}
                chains = []
                tiles = []
                for QB in range(NTB):
                    for h in range(8):
                        nk = 4 * (QB + 1)
                        ci = len(chains)
                        chains.append((QB, h, nk))
                        for i in range(nk):
                            kt = nk - 1 - i
                            o = max(0, kt * 128 - QB * 512)
                            tiles.append((ci, i, kt, o))
                N = len(tiles)

                def load_chain(ci):
                    QB, h, nk = chains[ci]
                    p = ci % 2
                    fw.dma("sync", qb[p].t[:], q_scr[h * 128:(h + 1) * 128, QB * 512:(QB + 1) * 512], [], [qb[p]],
                           qb[p])
                    fw.dma("sync", kb[p].t[:, 0:nk * 128], k_scr[h * 128:(h + 1) * 128, 0:nk * 128], [], [kb[p]],
                           kb[p])

                def load_v(ci):
                    QB, h, nk = chains[ci]
                    p3 = ci % 3
                    fw.dma("sync", vb[p3].t[:, 0:nk, :], v_scr[h, :, 0:nk, :], [], [vb[p3]], vb[p3])

                def st0(n):
                    ci, i, kt, o = tiles[n]
                    p = ci % 2
                    if i == 0:
                        if ci == 0:
                            load_chain(0)
                        load_v(ci)
                        if ci + 1 < len(chains):
                            load_chain(ci + 1)
                    z = zps[n % 3]
                    PE("matmul", [kb[p], qb[p]], [z], out=z.t[:, o:512], lhsT=kb[p].t[:, kt * 128:(kt + 1) * 128],
                       rhs=qb[p].t[:, o:512], start=True, stop=True)

                def st1(n):
                    ci, i, kt, o = tiles[n]
                    QB, h, nk = chains[ci]
                    z = zps[n % 3]
                    e = eb[n % 4]
                    sp = spb[n % 4]
                    A("activation", [z], [e], out=e.t[:, o:512], in_=z.t[:, o:512], func=AF.Exp, scale=SC)
                    if kt >= 4 * QB:
                        V("tensor_tensor", [e, mask0], [e], out=e.t[:, o:512], in0=e.t[:, o:512],
                          in1=mask0.t[:, 0:512 - o], op=ALU.mult)
                    A("activation", [e, cone], [sp], out=sp.t[:, o:512], in_=e.t[:, o:512], func=AF.Ln,
                      bias=cone.t[:, 0:1])

                def st2(n):
                    ci, i, kt, o = tiles[n]
                    QB, h, nk = chains[ci]
                    sp = spb[n % 4]
                    S = Sb[ci % 2]
                    f = fps[0]
                    if i == 0:
                        G("memset", [], [S], ap=S.t[:], constant=0.0)
                    PE("matmul", [tri, sp], [f], out=f.t[:, o:512], lhsT=tri.t[:], rhs=sp.t[:, o:512], start=True,
                       stop=(i == 0))
                    if i > 0:
                        PE("matmul", [ones_bf, S], [f], out=f.t[:, o:512], lhsT=ones_bf.t[:], rhs=S.t[:, o:512],
                           start=False, stop=True)
                    if i < nk - 1:
                        V("tensor_tensor", [S, sp], [S], out=S.t[:, o:512], in0=S.t[:, o:512], in1=sp.t[:, o:512],
                          op=ALU.add)
                    g = gbf[n % 2]
                    A("activation", [f], [g], out=g.t[:, o:512], in_=f.t[:, o:512], func=AF.Exp, scale=-1.0)
                    a = ab[n % 3]
                    e = eb[n % 4]
                    if o > 0:
                        G("memset", [], [a], ap=a.t[:, 0:o], constant=0.0)
                    V("tensor_tensor", [e, g], [a], out=a.t[:, o:512], in0=e.t[:, o:512], in1=g.t[:, o:512],
                      op=ALU.mult)

                def st3(n):
                    ci, i, kt, o = tiles[n]
                    QB, h, nk = chains[ci]
                    p = ci % 3
                    a = ab[n % 3]
                    op_ = ops[ci % 2]
                    PE("matmul", [vb[p], a], [op_], out=op_.t[:], lhsT=vb[p].t[:, kt, :], rhs=a.t[:],
                       start=(i == 0), stop=(i == nk - 1))
                    if i == nk - 1:
                        o_ = ob[QB % 2]
                        V("tensor_copy", [op_], [o_], out=o_.t[:, h, :], in_=op_.t[:])
                        if h == 7:
                            group_rstd([(o_, o_.t[:, hh, :]) for hh in range(8)], 1024, sqb, rs2, pb=bg_bps())
                            for hh in range(8):
                                gate_store((o_, o_.t[:, hh, :]), rs2, P.t[:, PP_GNA + hh:PP_GNA + hh + 1],
                                           512 + hh * 128, QB * 512, tmp[hh % 2], gate[hh % 2], yo[hh % 2])

                for s_ in range(N + 6):
                    if s_ < N:
                        st0(s_)
                    if 0 <= s_ - 2 < N:
                        st1(s_ - 2)
                    if 0 <= s_ - 4 < N:
                        st2(s_ - 4)
                    if 0 <= s_ - 6 < N:
                        st3(s_ - 6)
                    if not bg_state["done"]:
                        bg_state["acc"] += BG_ITEMS * 1.1 / N
                        while bg_state["acc"] >= 1.0 and not bg_state["done"]:
                            bg_state["acc"] -= 1.0
                            try:
                                next(bgen)
                            except StopIteration:
                                bg_state["done"] = True
                for _ in bgen:
                    pass
                fw.barrier()

        def phase_memkv(l):
            fw.phase = "phase_memkv"
            with ExitStack() as st:
                wkv = sb(st, "p0_wkv", [128, 16, 1024], BF16)
                gt = sb(st, "p0_g", [128, D], F32)
                xb = [sb(st, f"p0_x{i}", [128, D], F32) for i in range(2)]
                hb = [sb(st, f"p0_h{i}", [128, D], BF16) for i in range(2)]
                junk = sb(st, "p0_junk", [128, D], BF16)
                ssq = [sb(st, f"p0_ssq{i}", [128, 1], F32) for i in range(2)]
                rstd = [sb(st, f"p0_rstd{i}", [128, 1], F32) for i in range(2)]
                memT = sb(st, "p0_memT", [128, 16, MEM], BF16)
                fw.dma("gpsimd", wkv.t[:], w_kv[l].rearrange("(c p) n -> p c n", p=128), [], [wkv], wkv)
                fw.dma("sync", gt.t[:], gbc[4 + l], [], [gt], gt)
                for kt in range(2):
                    fw.dma("sync", xb[kt].t[:], mem_in[kt * 128:(kt + 1) * 128, :], [], [xb[kt]], xb[kt])
                    norm_to_hT(xb[kt], gt, hb[kt], junk, ssq[kt], rstd[kt], memT, 0,
                               oview=lambda c0, c1, kt=kt: memT.t[:, c0:c1, kt * 128:(kt + 1) * 128])
                for h in range(4):
                    pb = nps()
                    for c in range(16):
                        PE("matmul", [wkv, memT], [pb], inc=(c == 15), out=pb.t[:, 0:MEM],
                           lhsT=wkv.t[:, c, h * 128:(h + 1) * 128], rhs=memT.t[:, c, :], start=(c == 0),
                           stop=(c == 15))
                    V("tensor_copy", [pb], [k2T], out=k2T.t[:, h, :], in_=pb.t[:, 0:MEM])
                for kt in range(2):
                    pb = nps()
                    for c in range(16):
                        PE("matmul", [wkv, memT], [pb], inc=(c == 15), out=pb.t[:],
                           lhsT=memT.t[:, c, kt * 128:(kt + 1) * 128], rhs=wkv.t[:, c, 512:1024],
                           start=(c == 0), stop=(c == 15))
                    V("tensor_copy", [pb], [v2], out=v2.t[:, kt, :], in_=pb.t[:])
                fw.barrier()

        def phase_wout(l):
            fw.phase = "phase_wout"
            with ExitStack() as st:
                wout = sb(st, "p6_wout", [128, 16, D], BF16)
                gt = sb(st, "p6_g", [128, D], F32)
                yT = [sb(st, f"p6_yT{i}", [128, 16, 512], BF16) for i in range(2)]
                xb = [sb(st, f"p6_x{i}", [128, D], F32) for i in range(3)]
                hb = [sb(st, f"p6_h{i}", [128, D], BF16) for i in range(2)]
                junk = sb(st, "p6_junk", [128, D], BF16)
                ssq = [sb(st, f"p6_ssq{i}", [128, 1], F32) for i in range(2)]
                rstd = [sb(st, f"p6_rstd{i}", [128, 1], F32) for i in range(2)]
                hTt = [sb(st, f"p6_hT{i}", [128, 16, 128], BF16) for i in range(2)]
                fw.dma("gpsimd", wout.t[:], w_out[l].rearrange("(c p) n -> p c n", p=128), [], [wout], wout)
                fw.dma("sync", gt.t[:], gbc[2 + l], [], [gt], gt)
                xsrc = x_in if l == 0 else xres
                yv = y_scr.rearrange("(c p) t -> p c t", p=128)
                fw.dma("sync", yT[0].t[:], yv[:, :, 0:512], [], [yT[0]], yT[0])
                fw.dma("sync", xb[0].t[:], xsrc[0:128, :], [], [xb[0]], xb[0])
                pending = None
                for tb in range(NTB):
                    y = yT[tb % 2]
                    if tb + 1 < NTB:
                        yn = yT[(tb + 1) % 2]
                        fw.dma("sync", yn.t[:], yv[:, :, (tb + 1) * 512:(tb + 2) * 512], [], [yn], yn)
                    for tq in range(4):
                        tt = tb * 4 + tq
                        i = tt % 2
                        x = xb[tt % 3]
                        if tt + 1 < NTT:
                            xn_ = xb[(tt + 1) % 3]
                            fw.dma("sync", xn_.t[:], xsrc[(tt + 1) * 128:(tt + 2) * 128, :], [], [xn_], xn_)
                        for nb in range(4):
                            pb = nps()
                            for c in range(16):
                                PE("matmul", [y, wout], [pb], inc=(c == 15), out=pb.t[:],
                                   lhsT=y.t[:, c, tq * 128:(tq + 1) * 128], rhs=wout.t[:, c, nb * 512:(nb + 1) * 512],
                                   start=(c == 0), stop=(c == 15))
                            V("tensor_tensor", [pb, x], [x], out=x.t[:, nb * 512:(nb + 1) * 512], in0=pb.t[:],
                              in1=x.t[:, nb * 512:(nb + 1) * 512], op=ALU.add)
                        if pending is not None:
                            pending()
                        fw.dma("sync", xres[tt * 128:(tt + 1) * 128, :], x.t[:], [x], [], x)
                        norm_a(x, gt, hb[i], junk, ssq[i], rstd[i])
                        pending = (lambda i=i, tt=tt: norm_b(hb[i], hTt[i], dst=hT_scr[:, :, tt * 128:(tt + 1) * 128]))
                pending()
                fw.barrier()

        def phase_xattn(l):
            fw.phase = "phase_xattn"
            last = (l == nl - 1)
            SC = 128.0 ** -0.5
            with ExitStack() as st:
                wq = sb(st, "p7_wq", [128, 16, 512], BF16)
                wo = sb(st, "p7_wo", [128, 4, D], BF16)
                gt = sb(st, "p7_g", [128, D], F32)
                h2T = [sb(st, f"p7_h2T{i}", [128, 16, 512], BF16) for i in range(2)]
                q2 = [sb(st, f"p7_q2{i}", [128, 4, 512], BF16) for i in range(2)]
                Eb = [sb(st, f"p7_E{i}", [128, 512], BF16) for i in range(4)]
                rden = [sb(st, f"p7_rd{i}", [128, 512], F32) for i in range(2)]
                o2 = [sb(st, f"p7_o2{i}", [128, 4, 512], BF16) for i in range(2)]
                xb = [sb(st, f"p7_x{i}", [128, D], F32) for i in range(3)]
                ob_ = [sb(st, f"p7_ob{i}", [128, D], F32) for i in range(2)]
                hb = [sb(st, f"p7_h{i}", [128, D], BF16) for i in range(2)]
                junk = sb(st, "p7_junk", [128, D], BF16)
                ssq = [sb(st, f"p7_ssq{i}", [128, 1], F32) for i in range(2)]
                rstd = [sb(st, f"p7_rstd{i}", [128, 1], F32) for i in range(2)]
                hTt = [sb(st, f"p7_hT{i}", [128, 16, 128], BF16) for i in range(2)]
                fw.dma("gpsimd", wq.t[:], w_q[l].rearrange("(c p) n -> p c n", p=128), [], [wq], wq)
                fw.dma("gpsimd", wo.t[:], w_o[l].rearrange("(c p) n -> p c n", p=128), [], [wo], wo)
                fw.dma("sync", gt.t[:], gbc[6] if last else gbc[l + 1], [], [gt], gt)
                fw.dma("sync", h2T[0].t[:], hT_scr[:, :, 0:512], [], [h2T[0]], h2T[0])
                state = {"ne": 0, "pending": None}

                def S1(tb):
                    hT = h2T[tb % 2]
                    if tb + 1 < NTB:
                        hn = h2T[(tb + 1) % 2]
                        fw.dma("sync", hn.t[:], hT_scr[:, :, (tb + 1) * 512:(tb + 2) * 512], [], [hn], hn)
                    q = q2[tb % 2]
                    for h in range(4):
                        pb = nps()
                        for c in range(16):
                            PE("matmul", [wq, hT], [pb], inc=(c == 15), out=pb.t[:],
                               lhsT=wq.t[:, c, h * 128:(h + 1) * 128], rhs=hT.t[:, c, :], start=(c == 0),
                               stop=(c == 15))
                        if h % 2 == 0:
                            V("tensor_copy", [pb], [q], out=q.t[:, h, :], in_=pb.t[:])
                        else:
                            A("activation", [pb], [q], out=q.t[:, h, :], in_=pb.t[:], func=AF.Copy)

                def S2(tb):
                    q = q2[tb % 2]
                    o2t = o2[tb % 2]
                    for h in range(4):
                        es = []
                        for kt in range(2):
                            pb = nps()
                            PE("matmul", [k2T, q], [pb], out=pb.t[:], lhsT=k2T.t[:, h, kt * 128:(kt + 1) * 128],
                               rhs=q.t[:, h, :], start=True, stop=True)
                            e = Eb[state["ne"] % 4]
                            state["ne"] += 1
                            A("activation", [pb], [e], out=e.t[:], in_=pb.t[:], func=AF.Exp, scale=SC)
                            es.append(e)
                        po = nps()
                        for kt in range(2):
                            PE("matmul", [v2, es[kt]], [po], out=po.t[:], lhsT=v2.t[:, kt, h * 128:(h + 1) * 128],
                               rhs=es[kt].t[:], start=(kt == 0), stop=(kt == 1))
                        pd = nps()
                        for kt in range(2):
                            PE("matmul", [ones_bf, es[kt]], [pd], out=pd.t[:], lhsT=ones_bf.t[:], rhs=es[kt].t[:],
                               start=(kt == 0), stop=(kt == 1))
                        rd = rden[h % 2]
                        V("reciprocal", [pd], [rd], out=rd.t[:], in_=pd.t[:])
                        V("tensor_tensor", [po, rd], [o2t], out=o2t.t[:, h, :], in0=po.t[:], in1=rd.t[:], op=ALU.mult)

                def S3(tb):
                    o2t = o2[tb % 2]
                    for tq in range(4):
                        tt = tb * 4 + tq
                        i = tt % 2
                        x = xb[tt % 3]
                        if tt == 0:
                            fw.dma("sync", x.t[:], xres[0:128, :], [], [x], x)
                        if tt + 1 < NTT:
                            xn_ = xb[(tt + 1) % 3]
                            fw.dma("sync", xn_.t[:], xres[(tt + 1) * 128:(tt + 2) * 128, :], [], [xn_], xn_)
                        for nb in range(4):
                            pb = nps()
                            for hh in range(4):
                                PE("matmul", [o2t, wo], [pb], inc=(hh == 3), out=pb.t[:],
                                   lhsT=o2t.t[:, hh, tq * 128:(tq + 1) * 128], rhs=wo.t[:, hh, nb * 512:(nb + 1) * 512],
                                   start=(hh == 0), stop=(hh == 3))
                            V("tensor_tensor", [pb, x], [x], out=x.t[:, nb * 512:(nb + 1) * 512], in0=pb.t[:],
                              in1=x.t[:, nb * 512:(nb + 1) * 512], op=ALU.add)
                        if state["pending"] is not None:
                            state["pending"]()
                            state["pending"] = None
                        if last:
                            norm_stats(x, junk, ssq[i], rstd[i], ceps6)
                            V("scalar_tensor_tensor", [x, rstd[i], gt], [ob_[i]], out=ob_[i].t[:], in0=x.t[:],
                              scalar=rstd[i].t[:, 0:1], in1=gt.t[:], op0=ALU.mult, op1=ALU.mult)
                            tokf = fw.dma("sync", out[tt * 128:(tt + 1) * 128, :], ob_[i].t[:], [ob_[i]], [], ob_[i])
                            final_toks.append(tokf)
                        else:
                            fw.dma("sync", xres[tt * 128:(tt + 1) * 128, :], x.t[:], [x], [], x)
                            norm_a(x, gt, hb[i], junk, ssq[i], rstd[i])
                            state["pending"] = (lambda i=i, tt=tt: norm_b(
                                hb[i], hTt[i], dst=hT_scr[:, :, tt * 128:(tt + 1) * 128]))

                for it_ in range(NTB + 2):
                    if it_ < NTB:
                        S1(it_)
                    if 0 <= it_ - 1 < NTB:
                        S2(it_ - 1)
                    if 0 <= it_ - 2 < NTB:
                        S3(it_ - 2)
                if state["pending"] is not None:
                    state["pending"]()
                fw.barrier()

        final_toks = []
        for l in range(nl):
            phase_inproj(l)
            if stop == f"p2_{l}":
                return nc
            phase_attn(l)
            if stop == f"p4_{l}":
                return nc
            phase_memkv(l)
            phase_wout(l)
            if stop == f"p6_{l}":
                return nc
            phase_xattn(l)
            if stop == f"p7_{l}":
                return nc
    return nc


_CACHE = {}


def _layout_inputs(inp, T, nl=NL):
    f = np.float32
    W = np.asarray(inp["w_in"], f)
    w_in = np.empty((nl, NCH, 128, 16, 256), f)
    for l in range(nl):
        for ci, (kind, ca, cb, aux) in enumerate(CHUNKS):
            blk = np.concatenate([W[l][:, ca:ca + 128], W[l][:, cb:cb + 128]], axis=1)
            w_in[l, ci] = blk.reshape(16, 128, 256).transpose(1, 0, 2)
    gl = [inp["mix_norm_g"][0], inp["mix_norm_g"][1], inp["xattn_norm_g"][0], inp["xattn_norm_g"][1],
          inp["mem_norm_g"][0], inp["mem_norm_g"][1], inp["final_norm_g"]]
    gbc = np.stack([np.broadcast_to(np.asarray(g, f)[None, :], (128, D)) for g in gl]).copy()

    def cols(v):
        v = np.asarray(v, f)
        return v.reshape(-1, 128).T

    pp = np.zeros((nl, 128, NPP), f)
    for l in range(nl):
        dw = np.asarray(inp["conv_dw_w"][l], f)
        pp[l, :, PP_DW:PP_DW + 124] = dw.T.reshape(4, 128, CK).transpose(1, 0, 2).reshape(128, 124)
        pp[l, :, PP_DWB:PP_DWB + 4] = cols(inp["conv_dw_b"][l])
        pp[l, :, PP_LNG:PP_LNG + 4] = cols(inp["conv_ln_g"][l])
        pp[l, :, PP_LNB:PP_LNB + 4] = cols(inp["conv_ln_b"][l])
        lw = np.asarray(inp["lru_conv_w"][l], f)
        pp[l, :, PP_LCW:PP_LCW + 16] = lw.T.reshape(4, 128, 4).transpose(1, 0, 2).reshape(128, 16)
        pp[l, :, PP_LCB:PP_LCB + 4] = cols(inp["lru_conv_b"][l])
        pp[l, :, PP_BA:PP_BA + 4] = cols(inp["lru_ba"][l])
        pp[l, :, PP_BX:PP_BX + 4] = cols(inp["lru_bx"][l])
        pp[l, :, PP_LAM:PP_LAM + 4] = cols(inp["lru_lambda"][l])
        pp[l, :, PP_GNC:PP_GNC + 4] = cols(inp["out_norm_conv"][l])
        pp[l, :, PP_GNA:PP_GNA + 8] = cols(inp["out_norm_attn"][l])
        pp[l, :, PP_GNL:PP_GNL + 4] = cols(inp["out_norm_lru"][l])
    common = {
        "w_in": w_in,
        "w_out": np.ascontiguousarray(inp["w_out"], f),
        "wq": np.ascontiguousarray(inp["xattn_wq"], f),
        "wkv": np.ascontiguousarray(inp["xattn_wkv"], f),
        "wo": np.ascontiguousarray(inp["xattn_wo"], f),
        "pw": np.ascontiguousarray(inp["conv_pw_w"], f),
        "wa": np.ascontiguousarray(np.asarray(inp["lru_wa"], f).reshape(nl, 512, 128)),
        "wx": np.ascontiguousarray(np.asarray(inp["lru_wx"], f).reshape(nl, 512, 128)),
        "gbc": gbc,
        "pp": pp,
    }
    return common


def kernel(**inputs):
    x = np.asarray(inputs["x"], np.float32)
    mem = np.asarray(inputs["mem"], np.float32)
    B, S, _ = x.shape
    common = _layout_inputs(inputs, S)
    key = ("full", S)
    if key not in _CACHE:
        _CACHE[key] = build_program(S)
    nc = _CACHE[key]
    in_maps = []
    for c in range(8):
        b = c % B
        m = dict(common)
        m["x"] = np.ascontiguousarray(x[b])
        m["mem"] = np.ascontiguousarray(mem[b])
        in_maps.append(m)
    res = run_bass_kernel_spmd(nc, in_maps, core_ids=list(range(8)))
    return np.stack([res.results[b]["out"] for b in range(B)], axis=0)
```

```python
import numpy as np
from contextlib import ExitStack
import concourse.bass as bass
import concourse.mybir as mybir
from concourse.bass_utils import run_bass_kernel_spmd

F32 = mybir.dt.float32
BF16 = mybir.dt.bfloat16
AF = mybir.ActivationFunctionType
ALU = mybir.AluOpType

D = 2048
NL = 2
MEM = 256
INW = 6656
CK = 31
PAD = 32
SEM_LIMIT = 16000
SAME_ENG_SYNC = True

PP_DW = 0
PP_DWB = 124
PP_LNG = 128
PP_LNB = 132
PP_LCW = 136
PP_LCB = 152
PP_BA = 156
PP_BX = 160
PP_LAM = 164
PP_GNC = 168
PP_GNA = 172
PP_GNL = 180
NPP = 184

def _chunks():
    ch = []
    for j in range(4):
        ch.append(("vg", j * 128, 512 + j * 128, j))
    for j in range(2):
        ch.append(("gate", 1024 + j * 256, 1024 + j * 256 + 128, 0 + j * 256))
    for j in range(2):
        ch.append(("rx", 5632 + j * 256, 5632 + j * 256 + 128, j * 256))
    for j in range(2):
        ch.append(("gate", 6144 + j * 256, 6144 + j * 256 + 128, 1536 + j * 256))
    for j in range(4):
        ch.append(("gate", 4608 + j * 256, 4608 + j * 256 + 128, 512 + j * 256))
    for j in range(4):
        ch.append(("q", 1536 + j * 256, 1536 + j * 256 + 128, j * 256))
    for j in range(4):
        ch.append(("k", 2560 + j * 256, 2560 + j * 256 + 128, j * 256))
    for j in range(4):
        ch.append(("v", 3584 + j * 256, 3584 + j * 256 + 128, j * 2))
    return ch


CHUNKS = _chunks()
NCH = len(CHUNKS)
BG_READY = 10


class Res:
    __slots__ = ("name", "w", "rs", "t", "ds")

    def __init__(self, name, t=None):
        self.name = name
        self.w = None
        self.rs = {}
        self.t = t
        self.ds = None


class EngState:
    def __init__(self, name, eng, sem):
        self.name = name
        self.eng = eng
        self.sem = sem
        self.cnt = 0
        self.waited = {}
        self.pending = False


class FW:
    def __init__(self, nc, stack, nsem):
        self.nc = nc
        self.stack = stack
        self.semi = 0
        self.E = {}
        for n in ("tensor", "vector", "scalar", "gpsimd", "sync"):
            self.E[n] = EngState(n, getattr(nc, n), self.new_sem())
        self.phase = "top"
        self.names = {}
        self.dpool = {"sync": [], "gpsimd": []}
        self.dlive = []
        self.dall = []

    def new_sem(self):
        s = self.stack.enter_context(self.nc.semaphore(f"s{self.semi}"))
        self.semi += 1
        return s

    def _wait(self, E, tok):
        sem, cnt = tok
        k = id(sem)
        if E.waited.get(k, 0) >= cnt:
            return
        E.eng.wait_ge(sem, cnt)
        E.waited[k] = cnt

    @staticmethod
    def _deps(reads, writes):
        toks = []
        for r in reads:
            if r.w is not None:
                toks.append(r.w)
        for w in writes:
            if w.w is not None:
                toks.append(w.w)
            toks.extend(w.rs.values())
        return toks

    @staticmethod
    def _record(tok, reads, writes):
        for r in reads:
            k = id(tok[0])
            o = r.rs.get(k)
            if o is None or o[1] < tok[1]:
                r.rs[k] = tok
        for w in writes:
            w.w = tok
            w.rs = {}

    def op(self, en, meth, reads, writes, inc=True, **kw):
        E = self.E[en]
        if E.cnt >= SEM_LIMIT and not E.pending:
            E.sem = self.new_sem()
            E.cnt = 0
        for tok in self._deps(reads, writes):
            if tok[0] is E.sem and (en == "tensor" or (not SAME_ENG_SYNC and en != "gpsimd")):
                continue
            self._wait(E, tok)
        ins = getattr(E.eng, meth)(**kw)
        if self.names is not None:
            self.names[ins.ins.name] = self.phase
        tok = (E.sem, E.cnt + 1)
        if inc:
            ins.then_inc(E.sem, 1)
            E.cnt += 1
            E.pending = False
        else:
            E.pending = True
        self._record(tok, reads, writes)
        return tok

    def _dsem(self, owner, qn):
        if owner.ds is None or owner.ds[1] >= SEM_LIMIT:
            pool = self.dpool[qn]
            pool.sort(key=lambda d: -d[1])
            if pool and pool[-1][1] < SEM_LIMIT - 4000:
                owner.ds = pool.pop()
            else:
                owner.ds = [self.new_sem(), 0, qn]
                self.dall.append(owner.ds)
            self.dlive.append(owner.ds)
        assert owner.ds[2] == qn, (owner.name, qn)
        return owner.ds

    def dma(self, qn, out, in_, reads, writes, owner, **kw):
        E = self.E[qn]
        for tok in self._deps(reads, writes):
            self._wait(E, tok)
        ds = self._dsem(owner, qn)
        ins = E.eng.dma_start(out=out, in_=in_, **kw)
        ins.then_inc(ds[0], 16)
        if self.names is not None:
            self.names[ins.ins.name] = self.phase
        ds[1] += 16
        tok = (ds[0], ds[1])
        self._record(tok, reads, writes)
        return tok

    def barrier(self):
        toks = []
        for E in self.E.values():
            assert not E.pending
            if E.cnt > 0:
                toks.append((E.sem, E.cnt))
        for ds in self.dlive:
            if ds[1] > 0:
                toks.append((ds[0], ds[1]))
        for E in self.E.values():
            for tok in toks:
                if tok[0] is E.sem:
                    continue
                self._wait(E, tok)
        for d in self.dlive:
            self.dpool[d[2]].append(d)
        self.dlive = []


def build_program(T, debug=False, stop=None, nl=NL):
    nc = bass.Bass("TRN2", target_bir_lowering=False)
    nc._fw_names = {}
    NTT = T // 128
    NTB = T // 512
    SBT = min(2048, T)
    NSB = T // SBT
    dbg_kind = "ExternalOutput" if debug else "Internal"

    def din(name, shape, dt=F32):
        return nc.dram_tensor(name, list(shape), dt, kind="ExternalInput").ap()

    def dscr(name, shape, dt):
        return nc.dram_tensor(name, list(shape), dt, kind=dbg_kind).ap()

    x_in = din("x", [T, D])
    mem_in = din("mem", [MEM, D])
    w_in = din("w_in", [nl, NCH, 128, 16, 256])
    w_out = din("w_out", [nl, D, D])
    w_q = din("wq", [nl, D, 512])
    w_kv = din("wkv", [nl, D, 1024])
    w_o = din("wo", [nl, 512, D])
    w_pw = din("pw", [nl, 512, 512])
    w_a = din("wa", [nl, 512, 128])
    w_x = din("wx", [nl, 512, 128])
    gbc = din("gbc", [7, 128, D])
    pp_in = din("pp", [nl, 128, NPP])
    out = nc.dram_tensor("out", [T, D], F32, kind="ExternalOutput").ap()

    xres = dscr("xres", [T, D], F32)
    hT_scr = dscr("hT_scr", [128, 16, T], BF16)
    u_scr = dscr("u_scr", [512, PAD + T], F32)
    rx_scr = dscr("rx_scr", [512, PAD + T], F32)
    g_scr = dscr("g_scr", [D, T], BF16)
    q_scr = dscr("q_scr", [1024, T], BF16)
    k_scr = dscr("k_scr", [1024, T], BF16)
    v_scr = dscr("v_scr", [8, 128, NTT, 128], BF16)
    y_scr = dscr("y_scr", [D, T], BF16)

    with ExitStack() as top:
        fw = FW(nc, top, 150)
        fw.names = nc._fw_names
        psb = []
        for i in range(8):
            t = top.enter_context(nc.psum_tensor(f"ps{i}", [128, 512], F32))
            psb.append(Res(f"ps{i}", t))
        prot = [0]

        def nps():
            r = psb[prot[0] % 8]
            prot[0] += 1
            return r

        uniq = [0]
        dres = {}

        def DR(*key):
            r = dres.get(key)
            if r is None:
                r = dres[key] = Res(str(key))
            return r


        def sb(st, name, shape, dt):
            uniq[0] += 1
            name = f"{name}_{uniq[0]}"
            t = st.enter_context(nc.sbuf_tensor(name, list(shape), dt))
            return Res(name, t)

        def V(meth, reads, writes, **kw):
            return fw.op("vector", meth, reads, writes, **kw)

        def A(meth, reads, writes, **kw):
            return fw.op("scalar", meth, reads, writes, **kw)

        def G(meth, reads, writes, **kw):
            return fw.op("gpsimd", meth, reads, writes, **kw)

        def PE(meth, reads, writes, inc=True, **kw):
            return fw.op("tensor", meth, reads, writes, inc=inc, **kw)

        ident = sb(top, "ident", [128, 128], BF16)
        tri = sb(top, "tri", [128, 128], BF16)
        ones_bf = sb(top, "ones_bf", [128, 128], BF16)
        ones_f = sb(top, "ones_f", [128, 128], F32)
        mask0 = sb(top, "mask0", [128, 512], F32)
        onesw = sb(top, "onesw", [128, 512], F32)
        ceps6 = sb(top, "ceps6", [128, 1], F32)
        ceps5 = sb(top, "ceps5", [128, 1], F32)
        cone = sb(top, "cone", [128, 1], F32)
        czero = sb(top, "czero", [128, 1], F32)
        zpad = sb(top, "zpad", [128, PAD], F32)
        pp = [sb(top, f"pp{l}", [128, NPP], F32) for l in range(nl)]
        k2T = sb(top, "k2T", [128, 4, MEM], BF16)
        v2 = sb(top, "v2", [128, 2, 512], BF16)

        G("memset", [], [onesw], ap=onesw.t[:], constant=1.0)
        G("memset", [], [ones_f], ap=ones_f.t[:], constant=1.0)
        G("memset", [], [ones_bf], ap=ones_bf.t[:], constant=1.0)
        G("memset", [], [ceps6], ap=ceps6.t[:], constant=1e-6)
        G("memset", [], [ceps5], ap=ceps5.t[:], constant=1e-5)
        G("memset", [], [cone], ap=cone.t[:], constant=1.0)
        G("memset", [], [czero], ap=czero.t[:], constant=0.0)
        G("memset", [], [zpad], ap=zpad.t[:], constant=0.0)
        G("affine_select", [ones_bf], [ident], out=ident.t[:], in_=ones_bf.t[:], pattern=[[1, 128]],
          compare_op=ALU.is_equal, fill=0.0, base=0, channel_multiplier=-1)
        G("affine_select", [ones_bf], [tri], out=tri.t[:], in_=ones_bf.t[:], pattern=[[-1, 128]],
          compare_op=ALU.is_ge, fill=0.0, base=0, channel_multiplier=1)
        G("affine_select", [onesw], [mask0], out=mask0.t[:], in_=onesw.t[:], pattern=[[1, 512]],
          compare_op=ALU.is_gt, fill=0.0, base=0, channel_multiplier=-1)
        for l in range(nl):
            fw.dma("sync", pp[l].t[:], pp_in[l], [], [pp[l]], pp[l])
        for j in range(4):
            fw.dma("sync", u_scr[j * 128:(j + 1) * 128, 0:PAD], zpad.t[:], [zpad], [], zpad)
            fw.dma("sync", rx_scr[j * 128:(j + 1) * 128, 0:PAD], zpad.t[:], [zpad], [], zpad)
        if debug:
            dmask = nc.dram_tensor("dbg_mask", [128, 512], F32, kind="ExternalOutput").ap()
            dtri = nc.dram_tensor("dbg_tri", [128, 128], BF16, kind="ExternalOutput").ap()
            fw.dma("sync", dmask, mask0.t[:], [mask0], [], mask0)
            fw.dma("sync", dtri, tri.t[:], [tri], [], tri)
        fw.barrier()

        def norm_stats(xb, junk, ssq, rstd, eps_c):
            A("activation", [xb], [junk, ssq], out=junk.t[:], in_=xb.t[:], func=AF.Square,
              accum_out=ssq.t[:, 0:1])
            A("activation", [ssq, eps_c], [rstd], out=rstd.t[:, 0:1], in_=ssq.t[:, 0:1], func=AF.Sqrt,
              scale=1.0 / D, bias=eps_c.t[:, 0:1])
            V("reciprocal", [rstd], [rstd], out=rstd.t[:, 0:1], in_=rstd.t[:, 0:1])

        def norm_a(xb, gt, hb, junk, ssq, rstd):
            norm_stats(xb, junk, ssq, rstd, ceps6)
            V("scalar_tensor_tensor", [xb, rstd, gt], [hb], out=hb.t[:], in0=xb.t[:], scalar=rstd.t[:, 0:1],
              in1=gt.t[:], op0=ALU.mult, op1=ALU.mult)

        def norm_b(hb, hTt, dst=None, oview=None):
            for half in range(2):
                pb = nps()
                pv = pb.t[:].bitcast(BF16)
                for k in range(8):
                    c = half * 8 + k
                    PE("transpose", [hb, ident], [pb], inc=(k == 7), out=pv[:, k * 128:(k + 1) * 128],
                       in_=hb.t[:, c * 128:(c + 1) * 128], identity=ident.t[:])
                ov = hTt.t[:, half * 8:half * 8 + 8, :] if oview is None else oview(half * 8, half * 8 + 8)
                if half == 0:
                    A("activation", [pb], [hTt], out=ov,
                      in_=pv.rearrange("p (c k) -> p c k", k=128), func=AF.Copy)
                else:
                    V("tensor_copy", [pb], [hTt], out=ov,
                      in_=pv.rearrange("p (c k) -> p c k", k=128))
            if dst is not None:
                fw.dma("sync", dst, hTt.t[:], [hTt], [], hTt)

        def norm_to_hT(xb, gt, hb, junk, ssq, rstd, hTt, col0, ncols=128, dst=None, oview=None):
            norm_a(xb, gt, hb, junk, ssq, rstd)
            norm_b(hb, hTt, dst=dst, oview=oview)

        def phase_norm0():
            fw.phase = "phase_norm0"
            with ExitStack() as st:
                gt = sb(st, "p1_g", [128, D], F32)
                xb = [sb(st, f"p1_x{i}", [128, D], F32) for i in range(4)]
                hb = [sb(st, f"p1_h{i}", [128, D], BF16) for i in range(4)]
                junk = sb(st, "p1_junk", [128, D], BF16)
                ssq = [sb(st, f"p1_ssq{i}", [128, 1], F32) for i in range(4)]
                rstd = [sb(st, f"p1_rstd{i}", [128, 1], F32) for i in range(4)]
                hTt = [sb(st, f"p1_hT{i}", [128, 16, 128], BF16) for i in range(4)]
                fw.dma("sync", gt.t[:], gbc[0], [], [gt], gt)
                for tt in range(min(3, NTT)):
                    fw.dma("sync", xb[tt % 4].t[:], x_in[tt * 128:(tt + 1) * 128, :], [], [xb[tt % 4]], xb[tt % 4])
                for tt in range(NTT):
                    i = tt % 4
                    if tt + 3 < NTT:
                        i3 = (tt + 3) % 4
                        fw.dma("sync", xb[i3].t[:], x_in[(tt + 3) * 128:(tt + 4) * 128, :], [], [xb[i3]], xb[i3])
                    norm_to_hT(xb[i], gt, hb[i], junk, ssq[i], rstd[i], hTt[i], 0,
                               dst=hT_scr[:, :, tt * 128:(tt + 1) * 128])
                fw.barrier()

        phase_norm0()
        if stop == "p1":
            return nc

        def phase_inproj(l):
            fw.phase = "phase_inproj"
            with ExitStack() as st:
                hT = sb(st, "p2_hT", [128, 16, SBT], BF16)
                wb = [sb(st, f"p2_w{i}", [128, 16, 256], BF16) for i in range(3)]
                valb = [sb(st, f"p2_val{i}", [128, 512], F32) for i in range(4)]
                sig = [sb(st, f"p2_sig{i}", [128, 512], F32) for i in range(2)]
                ub = [sb(st, f"p2_u{i}", [128, 512], F32) for i in range(3)]
                gb = [sb(st, f"p2_g{i}", [128, 512], BF16) for i in range(3)]
                vb = [sb(st, f"p2_v{i}", [128, 256], BF16) for i in range(3)]
                rot = {"w": 0, "sig": 0, "u": 0, "g": 0, "v": 0}

                def nxt(lst, key):
                    r = lst[rot[key] % len(lst)]
                    rot[key] += 1
                    return r

                ntb = SBT // 512
                pool = make_bg_pool(st)
                prot6 = [0]

                def nps():
                    r = psb[prot6[0] % 6]
                    prot6[0] += 1
                    return r

                bgs = {"gen": None, "done": True, "rate": 0.0, "acc": 0.0}

                def bg_pull():
                    if bgs["done"]:
                        return
                    bgs["acc"] += bgs["rate"]
                    while bgs["acc"] >= 1.0 and not bgs["done"]:
                        bgs["acc"] -= 1.0
                        try:
                            next(bgs["gen"])
                        except StopIteration:
                            bgs["done"] = True

                def bg_drain():
                    if bgs["gen"] is not None and not bgs["done"]:
                        for _ in bgs["gen"]:
                            pass
                    bgs["done"] = True

                for sbi in range(NSB):
                    t0 = sbi * SBT
                    fw.dma("sync", hT.t[:], hT_scr[:, :, t0:t0 + SBT], [], [hT], hT)
                    for ci, (kind, ca, cb, aux) in enumerate(CHUNKS):
                        if ci == BG_READY:
                            bg_drain()
                            tbs = list(range(t0 // 512, (t0 + SBT) // 512))

                            def bg_all(tbs=tbs, first=(sbi == 0)):
                                yield from gen_conv(l, pool, tbs, first)
                                yield from gen_lru(l, pool, tbs, first)

                            bgs["gen"] = bg_all()
                            bgs["done"] = False
                            n_groups = (NCH - BG_READY - 4) * 2 * ntb + 4 * (SBT // 128)
                            bgs["rate"] = len(tbs) * 200.0 / n_groups
                            bgs["acc"] = 0.0
                        w = nxt(wb, "w")
                        fw.dma("gpsimd", w.t[:], w_in[l, ci], [], [w], w)
                        if kind == "v":
                            for tt in range(SBT // 128):
                                pb = nps()
                                for c in range(16):
                                    PE("matmul", [hT, w], [pb], inc=(c == 15), out=pb.t[:, 0:256],
                                       lhsT=hT.t[:, c, tt * 128:(tt + 1) * 128], rhs=w.t[:, c, :],
                                       start=(c == 0), stop=(c == 15))
                                v = nxt(vb, "v")
                                if tt % 2 == 0:
                                    V("tensor_copy", [pb], [v], out=v.t[:], in_=pb.t[:, 0:256])
                                else:
                                    A("activation", [pb], [v], out=v.t[:], in_=pb.t[:, 0:256], func=AF.Copy)
                                gt = (t0 // 128) + tt
                                for hh in range(2):
                                    fw.dma("sync", v_scr[aux + hh, :, gt, :], v.t[:, hh * 128:(hh + 1) * 128],
                                           [v], [], v)
                                bg_pull()
                            continue
                        for half in range(2):
                            for tb in range(ntb):
                                pb = nps()
                                for c in range(16):
                                    PE("matmul", [hT, w], [pb], inc=(c == 15), out=pb.t[:, :],
                                       lhsT=w.t[:, c, half * 128:(half + 1) * 128],
                                       rhs=hT.t[:, c, tb * 512:(tb + 1) * 512],
                                       start=(c == 0), stop=(c == 15))
                                tg = t0 + tb * 512
                                if kind == "vg":
                                    if half == 0:
                                        V("tensor_copy", [pb], [valb[tb]], out=valb[tb].t[:], in_=pb.t[:])
                                    else:
                                        sg = nxt(sig, "sig")
                                        A("activation", [pb], [sg], out=sg.t[:], in_=pb.t[:], func=AF.Sigmoid)
                                        u = nxt(ub, "u")
                                        V("tensor_tensor", [valb[tb], sg], [u], out=u.t[:], in0=valb[tb].t[:],
                                          in1=sg.t[:], op=ALU.mult)
                                        fw.dma("sync", u_scr[aux * 128:(aux + 1) * 128, PAD + tg:PAD + tg + 512],
                                               u.t[:], [u], [DR("u", aux, tg // 512)], u)
                                elif kind == "gate":
                                    g = nxt(gb, "g")
                                    A("activation", [pb], [g], out=g.t[:], in_=pb.t[:], func=AF.Silu)
                                    r0 = aux + half * 128
                                    fw.dma("sync", g_scr[r0:r0 + 128, tg:tg + 512], g.t[:], [g],
                                           [DR("g", r0 // 128, tg // 512)], g)
                                elif kind in ("q", "k"):
                                    g = nxt(gb, "g")
                                    V("tensor_copy", [pb], [g], out=g.t[:], in_=pb.t[:])
                                    r0 = aux + half * 128
                                    dst = q_scr if kind == "q" else k_scr
                                    fw.dma("sync", dst[r0:r0 + 128, tg:tg + 512], g.t[:], [g], [], g)
                                else:
                                    u = nxt(ub, "u")
                                    V("tensor_copy", [pb], [u], out=u.t[:], in_=pb.t[:])
                                    r0 = aux + half * 128
                                    fw.dma("sync", rx_scr[r0:r0 + 128, PAD + tg:PAD + tg + 512], u.t[:], [u],
                                           [DR("rx", r0 // 128, tg // 512)], u)
                                bg_pull()
                bg_drain()
                fw.barrier()

        def group_rstd(ytiles, nfeat, sqb, rstd, pb=None):
            if pb is None:
                pb = nps()
            n = len(ytiles)
            for j, yt in enumerate(ytiles):
                if isinstance(yt, tuple):
                    yt, yap = yt
                else:
                    yap = yt.t[:]
                sq = sqb[j % len(sqb)]
                A("activation", [yt], [sq], out=sq.t[:], in_=yap, func=AF.Square)
                PE("matmul", [ones_f, sq], [pb], out=pb.t[:], lhsT=ones_f.t[:], rhs=sq.t[:],
                   start=(j == 0), stop=(j == n - 1))
            A("activation", [pb, ceps6], [rstd], out=rstd.t[:], in_=pb.t[:], func=AF.Sqrt, scale=1.0 / nfeat,
              bias=ceps6.t[:, 0:1])
            V("reciprocal", [rstd], [rstd], out=rstd.t[:], in_=rstd.t[:])

        def gate_store(yt, rstd, gn_ap, grow, t0, tmp, gate, yo, preloaded=False):
            if isinstance(yt, tuple):
                yt, yap = yt
            else:
                yap = yt.t[:]
            if not preloaded:
                fw.dma("sync", gate.t[:], g_scr[grow:grow + 128, t0:t0 + 512], [], [gate], gate)
            V("tensor_tensor", [yt, rstd], [tmp], out=tmp.t[:], in0=yap, in1=rstd.t[:], op=ALU.mult)
            V("scalar_tensor_tensor", [tmp, gate], [yo], out=yo.t[:], in0=tmp.t[:], scalar=gn_ap, in1=gate.t[:],
              op0=ALU.mult, op1=ALU.mult)
            fw.dma("sync", y_scr[grow:grow + 128, t0:t0 + 512], yo.t[:], [yo], [], yo)

        def phase_conv(l):
            fw.phase = "phase_conv"
            P = pp[l]
            with ExitStack() as st:
                pw = sb(st, "p3_pw", [128, 4, 512], BF16)
                uin = [[sb(st, f"p3_u{i}_{j}", [128, 30 + 512], F32) for j in range(4)] for i in range(2)]
                acc = [sb(st, f"p3_acc{j}", [128, 512], F32) for j in range(4)]
                sqb = [sb(st, f"p3_sq{i}", [128, 512], F32) for i in range(2)]
                mt = sb(st, "p3_m", [128, 512], F32)
                msq = sb(st, "p3_msq", [128, 512], F32)
                rs = sb(st, "p3_rs", [128, 512], F32)
                xn = [sb(st, f"p3_xn{i}", [128, 512], F32) for i in range(2)]
                sbf = [sb(st, f"p3_s{j}", [128, 512], BF16) for j in range(4)]
                yb = [sb(st, f"p3_y{j}", [128, 512], F32) for j in range(4)]
                rs2 = sb(st, "p3_rs2", [128, 512], F32)
                tmp = [sb(st, f"p3_t{i}", [128, 512], F32) for i in range(2)]
                gate = [sb(st, f"p3_g{i}", [128, 512], BF16) for i in range(2)]
                yo = [sb(st, f"p3_yo{i}", [128, 512], BF16) for i in range(2)]
                fw.dma("gpsimd", pw.t[:], w_pw[l].rearrange("(c p) n -> p c n", p=128), [], [pw], pw)
                for tb in range(NTB):
                    t0 = tb * 512
                    us = uin[tb % 2]
                    for tbl in ([0, 1] if tb == 0 else [tb + 1]):
                        if tbl >= NTB:
                            continue
                        tl = tbl * 512
                        for j in range(4):
                            ul = uin[tbl % 2][j]
                            fw.dma("sync", ul.t[:], u_scr[j * 128:(j + 1) * 128, PAD + tl - 30:PAD + tl + 512],
                                   [], [ul], ul)
                    for j in range(4):
                        wcol = PP_DW + j * CK
                        V("tensor_scalar", [us[j], P], [acc[j]], out=acc[j].t[:], in0=us[j].t[:, 30:542],
                          scalar1=P.t[:, wcol + 30:wcol + 31], scalar2=P.t[:, PP_DWB + j:PP_DWB + j + 1],
                          op0=ALU.mult, op1=ALU.add)
                    for k in range(30):
                        for j in range(4):
                            wcol = PP_DW + j * CK
                            V("scalar_tensor_tensor", [us[j], P, acc[j]], [acc[j]], out=acc[j].t[:],
                              in0=us[j].t[:, k:k + 512], scalar=P.t[:, wcol + k:wcol + k + 1], in1=acc[j].t[:],
                              op0=ALU.mult, op1=ALU.add)
                    p1 = nps()
                    for j in range(4):
                        PE("matmul", [ones_f, acc[j]], [p1], out=p1.t[:], lhsT=ones_f.t[:],
                           rhs=acc[j].t[:], start=(j == 0), stop=(j == 3))
                    p2 = nps()
                    for j in range(4):
                        sq = sqb[j % 2]
                        A("activation", [acc[j]], [sq], out=sq.t[:], in_=acc[j].t[:], func=AF.Square)
                        PE("matmul", [ones_f, sq], [p2], out=p2.t[:], lhsT=ones_f.t[:],
                           rhs=sq.t[:], start=(j == 0), stop=(j == 3))
                    A("activation", [p1], [mt], out=mt.t[:], in_=p1.t[:], func=AF.Copy, scale=1.0 / 512)
                    V("tensor_tensor", [mt], [msq], out=msq.t[:], in0=mt.t[:], in1=mt.t[:], op=ALU.mult)
                    V("scalar_tensor_tensor", [p2, msq], [rs], out=rs.t[:], in0=p2.t[:], scalar=1.0 / 512,
                      in1=msq.t[:], op0=ALU.mult, op1=ALU.subtract)
                    A("activation", [rs, ceps5], [rs], out=rs.t[:], in_=rs.t[:], func=AF.Sqrt,
                      bias=ceps5.t[:, 0:1])
                    V("reciprocal", [rs], [rs], out=rs.t[:], in_=rs.t[:])
                    for j in range(4):
                        x1 = xn[j % 2]
                        V("tensor_tensor", [acc[j], mt], [x1], out=x1.t[:], in0=acc[j].t[:], in1=mt.t[:],
                          op=ALU.subtract)
                        V("tensor_tensor", [x1, rs], [x1], out=x1.t[:], in0=x1.t[:], in1=rs.t[:], op=ALU.mult)
                        A("activation", [x1, P], [sbf[j]], out=sbf[j].t[:], in_=x1.t[:], func=AF.Silu,
                          scale=P.t[:, PP_LNG + j:PP_LNG + j + 1], bias=P.t[:, PP_LNB + j:PP_LNB + j + 1])
                    for co in range(4):
                        pb = nps()
                        for ci in range(4):
                            PE("matmul", [pw, sbf[ci]], [pb], out=pb.t[:],
                               lhsT=pw.t[:, ci, co * 128:(co + 1) * 128], rhs=sbf[ci].t[:],
                               start=(ci == 0), stop=(ci == 3))
                        V("tensor_copy", [pb], [yb[co]], out=yb[co].t[:], in_=pb.t[:])
                    group_rstd(yb, 512, sqb, rs2)
                    for co in range(4):
                        gate_store(yb[co], rs2, P.t[:, PP_GNC + co:PP_GNC + co + 1], co * 128, t0,
                                   tmp[co % 2], gate[co % 2], yo[co % 2])
                fw.barrier()

        def phase_lru(l):
            fw.phase = "phase_lru"
            P = pp[l]
            with ExitStack() as st:
                wa = sb(st, "p5_wa", [128, 4, 128], BF16)
                wx = sb(st, "p5_wx", [128, 4, 128], BF16)
                kc = sb(st, "p5_kc", [128, 4], F32)
                hc = [sb(st, f"p5_hc{j}", [128, 1], F32) for j in range(4)]
                rxin = [sb(st, f"p5_rx{i}", [128, 3 + 512], F32) for i in range(8)]
                xc = [sb(st, f"p5_xc{i}", [128, 512], F32) for i in range(4)]
                xcb = [sb(st, f"p5_xcb{i}", [128, 512], BF16) for i in range(4)]
                rt = [sb(st, f"p5_r{i}", [128, 512], F32) for i in range(4)]
                it = [sb(st, f"p5_i{i}", [128, 512], F32) for i in range(4)]
                at = [sb(st, f"p5_a{i}", [128, 512], F32) for i in range(4)]
                om = [sb(st, f"p5_om{i}", [128, 512], F32) for i in range(4)]
                bt = [sb(st, f"p5_b{i}", [128, 512], F32) for i in range(4)]
                hb = [sb(st, f"p5_h{j}", [128, 512], F32) for j in range(4)]
                sqb = [sb(st, f"p5_sq{i}", [128, 512], F32) for i in range(2)]
                rs2 = sb(st, "p5_rs2", [128, 512], F32)
                tmp = [sb(st, f"p5_t{i}", [128, 512], F32) for i in range(2)]
                gate = [sb(st, f"p5_g{i}", [128, 512], BF16) for i in range(2)]
                yo = [sb(st, f"p5_yo{i}", [128, 512], BF16) for i in range(2)]
                fw.dma("gpsimd", wa.t[:], w_a[l].rearrange("(n p) e -> p n e", p=128), [], [wa], wa)
                fw.dma("gpsimd", wx.t[:], w_x[l].rearrange("(n p) e -> p n e", p=128), [], [wx], wx)
                A("activation", [P], [kc], out=kc.t[:], in_=P.t[:, PP_LAM:PP_LAM + 4], func=AF.Exp, scale=-1.0)
                A("activation", [kc, cone], [kc], out=kc.t[:], in_=kc.t[:], func=AF.Ln, bias=cone.t[:, 0:1])
                V("tensor_scalar", [kc], [kc], out=kc.t[:], in0=kc.t[:], scalar1=-8.0, scalar2=None, op0=ALU.mult)
                for j in range(4):
                    G("memset", [], [hc[j]], ap=hc[j].t[:], constant=0.0)
                for tb in range(NTB):
                    t0 = tb * 512
                    J = range(4)
                    rxs = [rxin[(tb % 2) * 4 + j] for j in J]
                    for tbl in ([0, 1] if tb == 0 else [tb + 1]):
                        if tbl >= NTB:
                            continue
                        tl = tbl * 512
                        for j in J:
                            rl = rxin[(tbl % 2) * 4 + j]
                            fw.dma("sync", rl.t[:], rx_scr[j * 128:(j + 1) * 128, PAD + tl - 3:PAD + tl + 512],
                                   [], [rl], rl)
                    for j in J:
                        wcol = PP_LCW + j * 4
                        V("tensor_scalar", [rxs[j], P], [xc[j]], out=xc[j].t[:], in0=rxs[j].t[:, 3:515],
                          scalar1=P.t[:, wcol + 3:wcol + 4], scalar2=P.t[:, PP_LCB + j:PP_LCB + j + 1],
                          op0=ALU.mult, op1=ALU.add)
                    for k in range(3):
                        for j in J:
                            wcol = PP_LCW + j * 4
                            V("scalar_tensor_tensor", [rxs[j], P, xc[j]], [xc[j]], out=xc[j].t[:],
                              in0=rxs[j].t[:, k:k + 512], scalar=P.t[:, wcol + k:wcol + k + 1], in1=xc[j].t[:],
                              op0=ALU.mult, op1=ALU.add)
                    for j in J:
                        G("tensor_copy", [xc[j]], [xcb[j]], out=xcb[j].t[:], in_=xc[j].t[:])
                    prs, pis = [], []
                    for j in J:
                        pr = psb[2 * j]
                        PE("matmul", [wa, xcb[j]], [pr], out=pr.t[:], lhsT=wa.t[:, j, :], rhs=xcb[j].t[:],
                           start=True, stop=True)
                        pi = psb[2 * j + 1]
                        PE("matmul", [wx, xcb[j]], [pi], out=pi.t[:], lhsT=wx.t[:, j, :], rhs=xcb[j].t[:],
                           start=True, stop=True)
                        prs.append(pr)
                        pis.append(pi)
                    for j in J:
                        A("activation", [prs[j], P], [rt[j]], out=rt[j].t[:], in_=prs[j].t[:], func=AF.Sigmoid,
                          bias=P.t[:, PP_BA + j:PP_BA + j + 1])
                        A("activation", [pis[j], P], [it[j]], out=it[j].t[:], in_=pis[j].t[:], func=AF.Sigmoid,
                          bias=P.t[:, PP_BX + j:PP_BX + j + 1])
                    for j in J:
                        A("activation", [rt[j], kc], [at[j]], out=at[j].t[:], in_=rt[j].t[:], func=AF.Exp,
                          scale=kc.t[:, j:j + 1])
                    for j in J:
                        V("tensor_tensor", [at[j]], [om[j]], out=om[j].t[:], in0=at[j].t[:], in1=at[j].t[:],
                          op=ALU.mult)
                    for j in J:
                        V("tensor_scalar", [om[j]], [om[j]], out=om[j].t[:], in0=om[j].t[:], scalar1=-1.0,
                          scalar2=1.0, op0=ALU.mult, op1=ALU.add)
                    for j in J:
                        V("tensor_scalar", [om[j]], [om[j]], out=om[j].t[:], in0=om[j].t[:], scalar1=1e-30,
                          scalar2=None, op0=ALU.max)
                    for j in J:
                        A("activation", [om[j]], [om[j]], out=om[j].t[:], in_=om[j].t[:], func=AF.Sqrt)
                    for j in J:
                        V("tensor_tensor", [it[j], xc[j]], [bt[j]], out=bt[j].t[:], in0=it[j].t[:], in1=xc[j].t[:],
                          op=ALU.mult)
                    for j in J:
                        V("tensor_tensor", [bt[j], om[j]], [bt[j]], out=bt[j].t[:], in0=bt[j].t[:],
                          in1=om[j].t[:], op=ALU.mult)
                    for j in J:
                        V("tensor_tensor_scan", [at[j], bt[j], hc[j]], [hb[j]], out=hb[j].t[:],
                          data0=at[j].t[:], data1=bt[j].t[:], initial=hc[j].t[:, 0:1], op0=ALU.mult,
                          op1=ALU.add)
                    for j in J:
                        V("tensor_copy", [hb[j]], [hc[j]], out=hc[j].t[:, 0:1], in_=hb[j].t[:, 511:512])
                    group_rstd(hb, 512, sqb, rs2)
                    for j in range(4):
                        gate_store(hb[j], rs2, P.t[:, PP_GNL + j:PP_GNL + j + 1], 1536 + j * 128, t0,
                                   tmp[j % 2], gate[j % 2], yo[j % 2])
                fw.barrier()

        def make_bg_pool(st):
            return {
                "f32": [sb(st, f"bg_f{i}", [128, 512], F32) for i in range(22)],
                "b16": [sb(st, f"bg_b{i}", [128, 512], BF16) for i in range(10)],
                "uin": [[sb(st, f"bg_u{i}_{j}", [128, 30 + 512], F32) for j in range(4)] for i in range(2)],
                "pw": sb(st, "bg_pw", [128, 4, 512], BF16),
                "wa": sb(st, "bg_wa", [128, 4, 128], BF16),
                "wx": sb(st, "bg_wx", [128, 4, 128], BF16),
                "kc": sb(st, "bg_kc", [128, 4], F32),
                "hc": [sb(st, f"bg_hc{j}", [128, 1], F32) for j in range(4)],
                "banks": [psb[6], psb[7]],
                "brot": [0],
            }

        def gen_conv(l, pool, tbs, first):
            P = pp[l]
            F = pool["f32"]
            B = pool["b16"]
            pw = pool["pw"]
            uin = pool["uin"]
            acc = F[0:4]
            sqb = F[4:6]
            mt, msq, rs, rs2 = F[6], F[7], F[8], F[9]
            xn = F[10:12]
            yb = F[12:16]
            tmp = F[16:18]
            sbf = B[0:4]
            gate = B[4:8]
            yo = B[8:10]

            def bps():
                r = pool["banks"][pool["brot"][0] % 2]
                pool["brot"][0] += 1
                return r

            if first:
                fw.dma("gpsimd", pw.t[:], w_pw[l].rearrange("(c p) n -> p c n", p=128), [], [pw], pw)
            for tb in tbs:
                t0 = tb * 512
                us = uin[tb % 2]
                for tbl in ([tb, tb + 1] if tb == tbs[0] else [tb + 1]):
                    if tbl not in tbs:
                        continue
                    tl = tbl * 512
                    for j in range(4):
                        ul = uin[tbl % 2][j]
                        rd = [DR("u", j, tbl)] + ([DR("u", j, tbl - 1)] if tbl > 0 else [])
                        fw.dma("sync", ul.t[:], u_scr[j * 128:(j + 1) * 128, PAD + tl - 30:PAD + tl + 512],
                               rd, [ul], ul)
                for j in range(4):
                    fw.dma("sync", gate[j].t[:], g_scr[j * 128:(j + 1) * 128, t0:t0 + 512], [DR("g", j, tb)],
                           [gate[j]], gate[j])
                yield
                for j in range(4):
                    wcol = PP_DW + j * CK
                    V("tensor_scalar", [us[j], P], [acc[j]], out=acc[j].t[:], in0=us[j].t[:, 30:542],
                      scalar1=P.t[:, wcol + 30:wcol + 31], scalar2=P.t[:, PP_DWB + j:PP_DWB + j + 1],
                      op0=ALU.mult, op1=ALU.add)
                yield
                for k in range(30):
                    for j in range(4):
                        wcol = PP_DW + j * CK
                        V("scalar_tensor_tensor", [us[j], P, acc[j]], [acc[j]], out=acc[j].t[:],
                          in0=us[j].t[:, k:k + 512], scalar=P.t[:, wcol + k:wcol + k + 1], in1=acc[j].t[:],
                          op0=ALU.mult, op1=ALU.add)
                        yield
                p1 = bps()
                for j in range(4):
                    PE("matmul", [ones_f, acc[j]], [p1], out=p1.t[:], lhsT=ones_f.t[:], rhs=acc[j].t[:],
                       start=(j == 0), stop=(j == 3))
                p2 = bps()
                for j in range(4):
                    sq = sqb[j % 2]
                    A("activation", [acc[j]], [sq], out=sq.t[:], in_=acc[j].t[:], func=AF.Square)
                    PE("matmul", [ones_f, sq], [p2], out=p2.t[:], lhsT=ones_f.t[:], rhs=sq.t[:],
                       start=(j == 0), stop=(j == 3))
                A("activation", [p1], [mt], out=mt.t[:], in_=p1.t[:], func=AF.Copy, scale=1.0 / 512)
                V("tensor_tensor", [mt], [msq], out=msq.t[:], in0=mt.t[:], in1=mt.t[:], op=ALU.mult)
                V("scalar_tensor_tensor", [p2, msq], [rs], out=rs.t[:], in0=p2.t[:], scalar=1.0 / 512,
                  in1=msq.t[:], op0=ALU.mult, op1=ALU.subtract)
                A("activation", [rs, ceps5], [rs], out=rs.t[:], in_=rs.t[:], func=AF.Sqrt, bias=ceps5.t[:, 0:1])
                V("reciprocal", [rs], [rs], out=rs.t[:], in_=rs.t[:])
                yield
                for j in range(4):
                    x1 = xn[j % 2]
                    V("tensor_tensor", [acc[j], mt], [x1], out=x1.t[:], in0=acc[j].t[:], in1=mt.t[:],
                      op=ALU.subtract)
                    V("tensor_tensor", [x1, rs], [x1], out=x1.t[:], in0=x1.t[:], in1=rs.t[:], op=ALU.mult)
                    A("activation", [x1, P], [sbf[j]], out=sbf[j].t[:], in_=x1.t[:], func=AF.Silu,
                      scale=P.t[:, PP_LNG + j:PP_LNG + j + 1], bias=P.t[:, PP_LNB + j:PP_LNB + j + 1])
                    yield
                for co in range(4):
                    pb = bps()
                    for ci in range(4):
                        PE("matmul", [pw, sbf[ci]], [pb], out=pb.t[:], lhsT=pw.t[:, ci, co * 128:(co + 1) * 128],
                           rhs=sbf[ci].t[:], start=(ci == 0), stop=(ci == 3))
                    V("tensor_copy", [pb], [yb[co]], out=yb[co].t[:], in_=pb.t[:])
                    yield
                group_rstd(yb, 512, sqb, rs2, pb=bps())
                yield
                for co in range(4):
                    gate_store(yb[co], rs2, P.t[:, PP_GNC + co:PP_GNC + co + 1], co * 128, t0, tmp[co % 2],
                               gate[co], yo[co % 2], preloaded=True)
                    yield

        def gen_lru(l, pool, tbs, first):
            P = pp[l]
            F = pool["f32"]
            B = pool["b16"]
            wa, wx, kc, hc = pool["wa"], pool["wx"], pool["kc"], pool["hc"]
            rxin = [pool["uin"][0][j] for j in range(4)] + [pool["uin"][1][j] for j in range(4)]
            xc = F[0:4]
            rt = F[4:8]
            it = F[8:12]
            om = F[12:16]
            hb = F[16:20]
            sqb = F[20:22]
            rs2 = F[4]
            tmp = F[5:7]
            xcb = B[0:4]
            gate = B[4:8]
            yo = B[8:10]
            J = range(4)

            def bps():
                r = pool["banks"][pool["brot"][0] % 2]
                pool["brot"][0] += 1
                return r

            if first:
                fw.dma("gpsimd", wa.t[:], w_a[l].rearrange("(n p) e -> p n e", p=128), [], [wa], wa)
                fw.dma("gpsimd", wx.t[:], w_x[l].rearrange("(n p) e -> p n e", p=128), [], [wx], wx)
                A("activation", [P], [kc], out=kc.t[:], in_=P.t[:, PP_LAM:PP_LAM + 4], func=AF.Exp, scale=-1.0)
                A("activation", [kc, cone], [kc], out=kc.t[:], in_=kc.t[:], func=AF.Ln, bias=cone.t[:, 0:1])
                V("tensor_scalar", [kc], [kc], out=kc.t[:], in0=kc.t[:], scalar1=-8.0, scalar2=None, op0=ALU.mult)
                for j in J:
                    G("memset", [], [hc[j]], ap=hc[j].t[:], constant=0.0)
            yield
            for tb in tbs:
                t0 = tb * 512
                rxs = [rxin[(tb % 2) * 4 + j] for j in J]
                for tbl in ([tb, tb + 1] if tb == tbs[0] else [tb + 1]):
                    if tbl not in tbs:
                        continue
                    tl = tbl * 512
                    for j in J:
                        rl = rxin[(tbl % 2) * 4 + j]
                        rd = [DR("rx", j, tbl)] + ([DR("rx", j, tbl - 1)] if tbl > 0 else [])
                        fw.dma("sync", rl.t[:, 0:515], rx_scr[j * 128:(j + 1) * 128, PAD + tl - 3:PAD + tl + 512],
                               rd, [rl], rl)
                for j in J:
                    fw.dma("sync", gate[j].t[:], g_scr[1536 + j * 128:1536 + (j + 1) * 128, t0:t0 + 512],
                           [DR("g", 12 + j, tb)], [gate[j]], gate[j])
                yield
                for j in J:
                    wcol = PP_LCW + j * 4
                    V("tensor_scalar", [rxs[j], P], [xc[j]], out=xc[j].t[:], in0=rxs[j].t[:, 3:515],
                      scalar1=P.t[:, wcol + 3:wcol + 4], scalar2=P.t[:, PP_LCB + j:PP_LCB + j + 1],
                      op0=ALU.mult, op1=ALU.add)
                yield
                for k in range(3):
                    for j in J:
                        wcol = PP_LCW + j * 4
                        V("scalar_tensor_tensor", [rxs[j], P, xc[j]], [xc[j]], out=xc[j].t[:],
                          in0=rxs[j].t[:, k:k + 512], scalar=P.t[:, wcol + k:wcol + k + 1], in1=xc[j].t[:],
                          op0=ALU.mult, op1=ALU.add)
                    yield
                for j in J:
                    A("activation", [xc[j]], [xcb[j]], out=xcb[j].t[:], in_=xc[j].t[:], func=AF.Copy)
                yield
                for j in J:
                    pr = bps()
                    PE("matmul", [wa, xcb[j]], [pr], out=pr.t[:], lhsT=wa.t[:, j, :], rhs=xcb[j].t[:],
                       start=True, stop=True)
                    A("activation", [pr, P], [rt[j]], out=rt[j].t[:], in_=pr.t[:], func=AF.Sigmoid,
                      bias=P.t[:, PP_BA + j:PP_BA + j + 1])
                    pi = bps()
                    PE("matmul", [wx, xcb[j]], [pi], out=pi.t[:], lhsT=wx.t[:, j, :], rhs=xcb[j].t[:],
                       start=True, stop=True)
                    A("activation", [pi, P], [it[j]], out=it[j].t[:], in_=pi.t[:], func=AF.Sigmoid,
                      bias=P.t[:, PP_BX + j:PP_BX + j + 1])
                    yield
                for j in J:
                    A("activation", [rt[j], kc], [rt[j]], out=rt[j].t[:], in_=rt[j].t[:], func=AF.Exp,
                      scale=kc.t[:, j:j + 1])
                yield
                for j in J:
                    V("tensor_tensor", [rt[j]], [om[j]], out=om[j].t[:], in0=rt[j].t[:], in1=rt[j].t[:],
                      op=ALU.mult)
                yield
                for j in J:
                    V("tensor_scalar", [om[j]], [om[j]], out=om[j].t[:], in0=om[j].t[:], scalar1=-1.0,
                      scalar2=1.0, op0=ALU.mult, op1=ALU.add)
                yield
                for j in J:
                    V("tensor_scalar", [om[j]], [om[j]], out=om[j].t[:], in0=om[j].t[:], scalar1=1e-30,
                      scalar2=None, op0=ALU.max)
                yield
                for j in J:
                    A("activation", [om[j]], [om[j]], out=om[j].t[:], in_=om[j].t[:], func=AF.Sqrt)
                yield
                for j in J:
                    V("tensor_tensor", [it[j], xc[j]], [it[j]], out=it[j].t[:], in0=it[j].t[:], in1=xc[j].t[:],
                      op=ALU.mult)
                yield
                for j in J:
                    V("tensor_tensor", [it[j], om[j]], [it[j]], out=it[j].t[:], in0=it[j].t[:], in1=om[j].t[:],
                      op=ALU.mult)
                yield
                for j in J:
                    V("tensor_tensor_scan", [rt[j], it[j], hc[j]], [hb[j]], out=hb[j].t[:], data0=rt[j].t[:],
                      data1=it[j].t[:], initial=hc[j].t[:, 0:1], op0=ALU.mult, op1=ALU.add)
                    yield
                for j in J:
                    V("tensor_copy", [hb[j]], [hc[j]], out=hc[j].t[:, 0:1], in_=hb[j].t[:, 511:512])
                yield
                group_rstd(hb, 512, sqb, rs2, pb=bps())
                yield
                for j in J:
                    gate_store(hb[j], rs2, P.t[:, PP_GNL + j:PP_GNL + j + 1], 1536 + j * 128, t0, tmp[j % 2],
                               gate[j], yo[j % 2], preloaded=True)
                    yield

        def phase_attn(l):
            fw.phase = "phase_attn"
            P = pp[l]
            SC = 128.0 ** -0.5
            with ExitStack() as st:
                qb = [sb(st, f"p4_q{i}", [128, 512], BF16) for i in range(2)]
                kb = [sb(st, f"p4_k{i}", [128, T], BF16) for i in range(2)]
                vb = [sb(st, f"p4_v{i}", [128, NTT, 128], BF16) for i in range(3)]
                eb = [sb(st, f"p4_e{i}", [128, 512], F32) for i in range(4)]
                spb = [sb(st, f"p4_sp{i}", [128, 512], BF16) for i in range(4)]
                gbf = [sb(st, f"p4_gb{i}", [128, 512], F32) for i in range(2)]
                ab = [sb(st, f"p4_a{i}", [128, 512], BF16) for i in range(3)]
                Sb = [sb(st, f"p4_S{i}", [128, 512], BF16) for i in range(2)]
                ob = [sb(st, f"p4_o{i}", [128, 8, 512], F32) for i in range(2)]
                sqb = [sb(st, f"p4_sq{i}", [128, 512], F32) for i in range(2)]
                rs2 = sb(st, "p4_rs2", [128, 512], F32)
                tmp = [sb(st, f"p4_t{i}", [128, 512], F32) for i in range(2)]
                gate = [sb(st, f"p4_g{i}", [128, 512], BF16) for i in range(2)]
                yo = [sb(st, f"p4_yo{i}", [128, 512], BF16) for i in range(2)]
                zps = psb[0:3]
                fps = [psb[3]]
                ops = psb[4:6]
                chains = []
                tiles = []
                for QB in range(NTB):
                    for h in range(8):
                        nk = 4 * (QB + 1)
                        ci = len(chains)
                        chains.append((QB, h, nk))
                        for i in range(nk):
                            kt = nk - 1 - i
                            o = max(0, kt * 128 - QB * 512)
                            tiles.append((ci, i, kt, o))
                N = len(tiles)

                def load_chain(ci):
                    QB, h, nk = chains[ci]
                    p = ci % 2
                    fw.dma("sync", qb[p].t[:], q_scr[h * 128:(h + 1) * 128, QB * 512:(QB + 1) * 512], [], [qb[p]],
                           qb[p])
                    fw.dma("sync", kb[p].t[:, 0:nk * 128], k_scr[h * 128:(h + 1) * 128, 0:nk * 128], [], [kb[p]],
                           kb[p])

                def load_v(ci):
                    QB, h, nk = chains[ci]
                    p3 = ci % 3
                    fw.dma("sync", vb[p3].t[:, 0:nk, :], v_scr[h, :, 0:nk, :], [], [vb[p3]], vb[p3])

                def st0(n):
                    ci, i, kt, o = tiles[n]
                    p = ci % 2
                    if i == 0:
                        if ci == 0:
                            load_chain(0)
                        load_v(ci)
                        if ci + 1 < len(chains):
                            load_chain(ci + 1)
                    z = zps[n % 3]
                    PE("matmul", [kb[p], qb[p]], [z], out=z.t[:, o:512], lhsT=kb[p].t[:, kt * 128:(kt + 1) * 128],
                       rhs=qb[p].t[:, o:512], start=True, stop=True)

                def st1(n):
                    ci, i, kt, o = tiles[n]
                    QB, h, nk = chains[ci]
                    z = zps[n % 3]
                    e = eb[n % 4]
                    sp = spb[n % 4]
                    A("activation", [z], [e], out=e.t[:, o:512], in_=z.t[:, o:512], func=AF.Exp, scale=SC)
                    if kt >= 4 * QB:
                        V("tensor_tensor", [e, mask0], [e], out=e.t[:, o:512], in0=e.t[:, o:512],
                          in1=mask0.t[:, 0:512 - o], op=ALU.mult)
                    A("activation", [e, cone], [sp], out=sp.t[:, o:512], in_=e.t[:, o:512], func=AF.Ln,
                      bias=cone.t[:, 0:1])

                def st2(n):
                    ci, i, kt, o = tiles[n]
                    QB, h, nk = chains[ci]
                    sp = spb[n % 4]
                    S = Sb[ci % 2]
                    f = fps[0]
                    if i == 0:
                        G("memset", [], [S], ap=S.t[:], constant=0.0)
                    PE("matmul", [tri, sp], [f], out=f.t[:, o:512], lhsT=tri.t[:], rhs=sp.t[:, o:512], start=True,
                       stop=(i == 0))
                    if i > 0:
                        PE("matmul", [ones_bf, S], [f], out=f.t[:, o:512], lhsT=ones_bf.t[:], rhs=S.t[:, o:512],
                           start=False, stop=True)
                    if i < nk - 1:
                        V("tensor_tensor", [S, sp], [S], out=S.t[:, o:512], in0=S.t[:, o:512], in1=sp.t[:, o:512],
                          op=ALU.add)
                    g = gbf[n % 2]
                    A("activation", [f], [g], out=g.t[:, o:512], in_=f.t[:, o:512], func=AF.Exp, scale=-1.0)
                    a = ab[n % 3]
                    e = eb[n % 4]
                    if o > 0:
                        G("memset", [], [a], ap=a.t[:, 0:o], constant=0.0)
                    V("tensor_tensor", [e, g], [a], out=a.t[:, o:512], in0=e.t[:, o:512], in1=g.t[:, o:512],
                      op=ALU.mult)

                def st3(n):
                    ci, i, kt, o = tiles[n]
                    QB, h, nk = chains[ci]
                    p = ci % 3
                    a = ab[n % 3]
                    op_ = ops[ci % 2]
                    PE("matmul", [vb[p], a], [op_], out=op_.t[:], lhsT=vb[p].t[:, kt, :], rhs=a.t[:],
                       start=(i == 0), stop=(i == nk - 1))
                    if i == nk - 1:
                        o_ = ob[QB % 2]
                        V("tensor_copy", [op_], [o_], out=o_.t[:, h, :], in_=op_.t[:])
                        if h == 7:
                            group_rstd([(o_, o_.t[:, hh, :]) for hh in range(8)], 1024, sqb, rs2, pb=psb[6])
                            for hh in range(8):
                                gate_store((o_, o_.t[:, hh, :]), rs2, P.t[:, PP_GNA + hh:PP_GNA + hh + 1],
                                           512 + hh * 128, QB * 512, tmp[hh % 2], gate[hh % 2], yo[hh % 2])

                for s_ in range(N + 6):
                    if s_ < N:
                        st0(s_)
                    if 0 <= s_ - 2 < N:
                        st1(s_ - 2)
                    if 0 <= s_ - 4 < N:
                        st2(s_ - 4)
                    if 0 <= s_ - 6 < N:
                        st3(s_ - 6)
                fw.barrier()

        def phase_memkv(l):
            fw.phase = "phase_memkv"
            with ExitStack() as st:
                wkv = sb(st, "p0_wkv", [128, 16, 1024], BF16)
                gt = sb(st, "p0_g", [128, D], F32)
                xb = [sb(st, f"p0_x{i}", [128, D], F32) for i in range(2)]
                hb = [sb(st, f"p0_h{i}", [128, D], BF16) for i in range(2)]
                junk = sb(st, "p0_junk", [128, D], BF16)
                ssq = [sb(st, f"p0_ssq{i}", [128, 1], F32) for i in range(2)]
                rstd = [sb(st, f"p0_rstd{i}", [128, 1], F32) for i in range(2)]
                memT = sb(st, "p0_memT", [128, 16, MEM], BF16)
                fw.dma("gpsimd", wkv.t[:], w_kv[l].rearrange("(c p) n -> p c n", p=128), [], [wkv], wkv)
                fw.dma("sync", gt.t[:], gbc[4 + l], [], [gt], gt)
                for kt in range(2):
                    fw.dma("sync", xb[kt].t[:], mem_in[kt * 128:(kt + 1) * 128, :], [], [xb[kt]], xb[kt])
                    norm_to_hT(xb[kt], gt, hb[kt], junk, ssq[kt], rstd[kt], memT, 0,
                               oview=lambda c0, c1, kt=kt: memT.t[:, c0:c1, kt * 128:(kt + 1) * 128])
                for h in range(4):
                    pb = nps()
                    for c in range(16):
                        PE("matmul", [wkv, memT], [pb], inc=(c == 15), out=pb.t[:, 0:MEM],
                           lhsT=wkv.t[:, c, h * 128:(h + 1) * 128], rhs=memT.t[:, c, :], start=(c == 0),
                           stop=(c == 15))
                    V("tensor_copy", [pb], [k2T], out=k2T.t[:, h, :], in_=pb.t[:, 0:MEM])
                for kt in range(2):
                    pb = nps()
                    for c in range(16):
                        PE("matmul", [wkv, memT], [pb], inc=(c == 15), out=pb.t[:],
                           lhsT=memT.t[:, c, kt * 128:(kt + 1) * 128], rhs=wkv.t[:, c, 512:1024],
                           start=(c == 0), stop=(c == 15))
                    V("tensor_copy", [pb], [v2], out=v2.t[:, kt, :], in_=pb.t[:])
                fw.barrier()

        def phase_wout(l):
            fw.phase = "phase_wout"
            with ExitStack() as st:
                wout = sb(st, "p6_wout", [128, 16, D], BF16)
                gt = sb(st, "p6_g", [128, D], F32)
                yT = [sb(st, f"p6_yT{i}", [128, 16, 512], BF16) for i in range(2)]
                xb = [sb(st, f"p6_x{i}", [128, D], F32) for i in range(3)]
                hb = [sb(st, f"p6_h{i}", [128, D], BF16) for i in range(2)]
                junk = sb(st, "p6_junk", [128, D], BF16)
                ssq = [sb(st, f"p6_ssq{i}", [128, 1], F32) for i in range(2)]
                rstd = [sb(st, f"p6_rstd{i}", [128, 1], F32) for i in range(2)]
                hTt = [sb(st, f"p6_hT{i}", [128, 16, 128], BF16) for i in range(2)]
                fw.dma("gpsimd", wout.t[:], w_out[l].rearrange("(c p) n -> p c n", p=128), [], [wout], wout)
                fw.dma("sync", gt.t[:], gbc[2 + l], [], [gt], gt)
                xsrc = x_in if l == 0 else xres
                yv = y_scr.rearrange("(c p) t -> p c t", p=128)
                fw.dma("sync", yT[0].t[:], yv[:, :, 0:512], [], [yT[0]], yT[0])
                fw.dma("sync", xb[0].t[:], xsrc[0:128, :], [], [xb[0]], xb[0])
                pending = None
                for tb in range(NTB):
                    y = yT[tb % 2]
                    if tb + 1 < NTB:
                        yn = yT[(tb + 1) % 2]
                        fw.dma("sync", yn.t[:], yv[:, :, (tb + 1) * 512:(tb + 2) * 512], [], [yn], yn)
                    for tq in range(4):
                        tt = tb * 4 + tq
                        i = tt % 2
                        x = xb[tt % 3]
                        if tt + 1 < NTT:
                            xn_ = xb[(tt + 1) % 3]
                            fw.dma("sync", xn_.t[:], xsrc[(tt + 1) * 128:(tt + 2) * 128, :], [], [xn_], xn_)
                        for nb in range(4):
                            pb = nps()
                            for c in range(16):
                                PE("matmul", [y, wout], [pb], inc=(c == 15), out=pb.t[:],
                                   lhsT=y.t[:, c, tq * 128:(tq + 1) * 128], rhs=wout.t[:, c, nb * 512:(nb + 1) * 512],
                                   start=(c == 0), stop=(c == 15))
                            V("tensor_tensor", [pb, x], [x], out=x.t[:, nb * 512:(nb + 1) * 512], in0=pb.t[:],
                              in1=x.t[:, nb * 512:(nb + 1) * 512], op=ALU.add)
                        if pending is not None:
                            pending()
                        fw.dma("sync", xres[tt * 128:(tt + 1) * 128, :], x.t[:], [x], [], x)
                        norm_a(x, gt, hb[i], junk, ssq[i], rstd[i])
                        pending = (lambda i=i, tt=tt: norm_b(hb[i], hTt[i], dst=hT_scr[:, :, tt * 128:(tt + 1) * 128]))
                pending()
                fw.barrier()

        def phase_xattn(l):
            fw.phase = "phase_xattn"
            last = (l == nl - 1)
            SC = 128.0 ** -0.5
            with ExitStack() as st:
                wq = sb(st, "p7_wq", [128, 16, 512], BF16)
                wo = sb(st, "p7_wo", [128, 4, D], BF16)
                gt = sb(st, "p7_g", [128, D], F32)
                h2T = [sb(st, f"p7_h2T{i}", [128, 16, 512], BF16) for i in range(2)]
                q2 = [sb(st, f"p7_q2{i}", [128, 4, 512], BF16) for i in range(2)]
                Eb = [sb(st, f"p7_E{i}", [128, 512], BF16) for i in range(4)]
                rden = [sb(st, f"p7_rd{i}", [128, 512], F32) for i in range(2)]
                o2 = [sb(st, f"p7_o2{i}", [128, 4, 512], BF16) for i in range(2)]
                xb = [sb(st, f"p7_x{i}", [128, D], F32) for i in range(3)]
                ob_ = [sb(st, f"p7_ob{i}", [128, D], F32) for i in range(2)]
                hb = [sb(st, f"p7_h{i}", [128, D], BF16) for i in range(2)]
                junk = sb(st, "p7_junk", [128, D], BF16)
                ssq = [sb(st, f"p7_ssq{i}", [128, 1], F32) for i in range(2)]
                rstd = [sb(st, f"p7_rstd{i}", [128, 1], F32) for i in range(2)]
                hTt = [sb(st, f"p7_hT{i}", [128, 16, 128], BF16) for i in range(2)]
                fw.dma("gpsimd", wq.t[:], w_q[l].rearrange("(c p) n -> p c n", p=128), [], [wq], wq)
                fw.dma("gpsimd", wo.t[:], w_o[l].rearrange("(c p) n -> p c n", p=128), [], [wo], wo)
                fw.dma("sync", gt.t[:], gbc[6] if last else gbc[l + 1], [], [gt], gt)
                fw.dma("sync", h2T[0].t[:], hT_scr[:, :, 0:512], [], [h2T[0]], h2T[0])
                state = {"ne": 0, "pending": None}

                def S1(tb):
                    hT = h2T[tb % 2]
                    if tb + 1 < NTB:
                        hn = h2T[(tb + 1) % 2]
                        fw.dma("sync", hn.t[:], hT_scr[:, :, (tb + 1) * 512:(tb + 2) * 512], [], [hn], hn)
                    q = q2[tb % 2]
                    for h in range(4):
                        pb = nps()
                        for c in range(16):
                            PE("matmul", [wq, hT], [pb], inc=(c == 15), out=pb.t[:],
                               lhsT=wq.t[:, c, h * 128:(h + 1) * 128], rhs=hT.t[:, c, :], start=(c == 0),
                               stop=(c == 15))
                        if h % 2 == 0:
                            V("tensor_copy", [pb], [q], out=q.t[:, h, :], in_=pb.t[:])
                        else:
                            A("activation", [pb], [q], out=q.t[:, h, :], in_=pb.t[:], func=AF.Copy)

                def S2(tb):
                    q = q2[tb % 2]
                    o2t = o2[tb % 2]
                    for h in range(4):
                        es = []
                        for kt in range(2):
                            pb = nps()
                            PE("matmul", [k2T, q], [pb], out=pb.t[:], lhsT=k2T.t[:, h, kt * 128:(kt + 1) * 128],
                               rhs=q.t[:, h, :], start=True, stop=True)
                            e = Eb[state["ne"] % 4]
                            state["ne"] += 1
                            A("activation", [pb], [e], out=e.t[:], in_=pb.t[:], func=AF.Exp, scale=SC)
                            es.append(e)
                        po = nps()
                        for kt in range(2):
                            PE("matmul", [v2, es[kt]], [po], out=po.t[:], lhsT=v2.t[:, kt, h * 128:(h + 1) * 128],
                               rhs=es[kt].t[:], start=(kt == 0), stop=(kt == 1))
                        pd = nps()
                        for kt in range(2):
                            PE("matmul", [ones_bf, es[kt]], [pd], out=pd.t[:], lhsT=ones_bf.t[:], rhs=es[kt].t[:],
                               start=(kt == 0), stop=(kt == 1))
                        rd = rden[h % 2]
                        V("reciprocal", [pd], [rd], out=rd.t[:], in_=pd.t[:])
                        V("tensor_tensor", [po, rd], [o2t], out=o2t.t[:, h, :], in0=po.t[:], in1=rd.t[:], op=ALU.mult)

                def S3(tb):
                    o2t = o2[tb % 2]
                    for tq in range(4):
                        tt = tb * 4 + tq
                        i = tt % 2
                        x = xb[tt % 3]
                        if tt == 0:
                            fw.dma("sync", x.t[:], xres[0:128, :], [], [x], x)
                        if tt + 1 < NTT:
                            xn_ = xb[(tt + 1) % 3]
                            fw.dma("sync", xn_.t[:], xres[(tt + 1) * 128:(tt + 2) * 128, :], [], [xn_], xn_)
                        for nb in range(4):
                            pb = nps()
                            for hh in range(4):
                                PE("matmul", [o2t, wo], [pb], inc=(hh == 3), out=pb.t[:],
                                   lhsT=o2t.t[:, hh, tq * 128:(tq + 1) * 128], rhs=wo.t[:, hh, nb * 512:(nb + 1) * 512],
                                   start=(hh == 0), stop=(hh == 3))
                            V("tensor_tensor", [pb, x], [x], out=x.t[:, nb * 512:(nb + 1) * 512], in0=pb.t[:],
                              in1=x.t[:, nb * 512:(nb + 1) * 512], op=ALU.add)
                        if state["pending"] is not None:
                            state["pending"]()
                            state["pending"] = None
                        if last:
                            norm_stats(x, junk, ssq[i], rstd[i], ceps6)
                            V("scalar_tensor_tensor", [x, rstd[i], gt], [ob_[i]], out=ob_[i].t[:], in0=x.t[:],
                              scalar=rstd[i].t[:, 0:1], in1=gt.t[:], op0=ALU.mult, op1=ALU.mult)
                            tokf = fw.dma("sync", out[tt * 128:(tt + 1) * 128, :], ob_[i].t[:], [ob_[i]], [], ob_[i])
                            final_toks.append(tokf)
                        else:
                            fw.dma("sync", xres[tt * 128:(tt + 1) * 128, :], x.t[:], [x], [], x)
                            norm_a(x, gt, hb[i], junk, ssq[i], rstd[i])
                            state["pending"] = (lambda i=i, tt=tt: norm_b(
                                hb[i], hTt[i], dst=hT_scr[:, :, tt * 128:(tt + 1) * 128]))

                for it_ in range(NTB + 2):
                    if it_ < NTB:
                        S1(it_)
                    if 0 <= it_ - 1 < NTB:
                        S2(it_ - 1)
                    if 0 <= it_ - 2 < NTB:
                        S3(it_ - 2)
                if state["pending"] is not None:
                    state["pending"]()
                fw.barrier()

        final_toks = []
        for l in range(nl):
            phase_inproj(l)
            if stop == f"p2_{l}":
                return nc
            phase_attn(l)
            if stop == f"p4_{l}":
                return nc
            phase_memkv(l)
            phase_wout(l)
            if stop == f"p6_{l}":
                return nc
            phase_xattn(l)
            if stop == f"p7_{l}":
                return nc
    return nc


_CACHE = {}


def _layout_inputs(inp, T, nl=NL):
    f = np.float32
    W = np.asarray(inp["w_in"], f)
    w_in = np.empty((nl, NCH, 128, 16, 256), f)
    for l in range(nl):
        for ci, (kind, ca, cb, aux) in enumerate(CHUNKS):
            blk = np.concatenate([W[l][:, ca:ca + 128], W[l][:, cb:cb + 128]], axis=1)
            w_in[l, ci] = blk.reshape(16, 128, 256).transpose(1, 0, 2)
    gl = [inp["mix_norm_g"][0], inp["mix_norm_g"][1], inp["xattn_norm_g"][0], inp["xattn_norm_g"][1],
          inp["mem_norm_g"][0], inp["mem_norm_g"][1], inp["final_norm_g"]]
    gbc = np.stack([np.broadcast_to(np.asarray(g, f)[None, :], (128, D)) for g in gl]).copy()

    def cols(v):
        v = np.asarray(v, f)
        return v.reshape(-1, 128).T

    pp = np.zeros((nl, 128, NPP), f)
    for l in range(nl):
        dw = np.asarray(inp["conv_dw_w"][l], f)
        pp[l, :, PP_DW:PP_DW + 124] = dw.T.reshape(4, 128, CK).transpose(1, 0, 2).reshape(128, 124)
        pp[l, :, PP_DWB:PP_DWB + 4] = cols(inp["conv_dw_b"][l])
        pp[l, :, PP_LNG:PP_LNG + 4] = cols(inp["conv_ln_g"][l])
        pp[l, :, PP_LNB:PP_LNB + 4] = cols(inp["conv_ln_b"][l])
        lw = np.asarray(inp["lru_conv_w"][l], f)
        pp[l, :, PP_LCW:PP_LCW + 16] = lw.T.reshape(4, 128, 4).transpose(1, 0, 2).reshape(128, 16)
        pp[l, :, PP_LCB:PP_LCB + 4] = cols(inp["lru_conv_b"][l])
        pp[l, :, PP_BA:PP_BA + 4] = cols(inp["lru_ba"][l])
        pp[l, :, PP_BX:PP_BX + 4] = cols(inp["lru_bx"][l])
        pp[l, :, PP_LAM:PP_LAM + 4] = cols(inp["lru_lambda"][l])
        pp[l, :, PP_GNC:PP_GNC + 4] = cols(inp["out_norm_conv"][l])
        pp[l, :, PP_GNA:PP_GNA + 8] = cols(inp["out_norm_attn"][l])
        pp[l, :, PP_GNL:PP_GNL + 4] = cols(inp["out_norm_lru"][l])
    common = {
        "w_in": w_in,
        "w_out": np.ascontiguousarray(inp["w_out"], f),
        "wq": np.ascontiguousarray(inp["xattn_wq"], f),
        "wkv": np.ascontiguousarray(inp["xattn_wkv"], f),
        "wo": np.ascontiguousarray(inp["xattn_wo"], f),
        "pw": np.ascontiguousarray(inp["conv_pw_w"], f),
        "wa": np.ascontiguousarray(np.asarray(inp["lru_wa"], f).reshape(nl, 512, 128)),
        "wx": np.ascontiguousarray(np.asarray(inp["lru_wx"], f).reshape(nl, 512, 128)),
        "gbc": gbc,
        "pp": pp,
    }
    return common


def kernel(**inputs):
    x = np.asarray(inputs["x"], np.float32)
    mem = np.asarray(inputs["mem"], np.float32)
    B, S, _ = x.shape
    common = _layout_inputs(inputs, S)
    key = ("full", S)
    if key not in _CACHE:
        _CACHE[key] = build_program(S)
    nc = _CACHE[key]
    in_maps = []
    for c in range(8):
        b = c % B
        m = dict(common)
        m["x"] = np.ascontiguousarray(x[b])
        m["mem"] = np.ascontiguousarray(mem[b])
        in_maps.append(m)
    res = run_bass_kernel_spmd(nc, in_maps, core_ids=list(range(8)))
    return np.stack([res.results[b]["out"] for b in range(B)], axis=0)
```

```python
import numpy as np
from contextlib import ExitStack
import concourse.bass as bass
import concourse.mybir as mybir
from concourse.bass_utils import run_bass_kernel_spmd

F32 = mybir.dt.float32
BF16 = mybir.dt.bfloat16
AF = mybir.ActivationFunctionType
ALU = mybir.AluOpType

D = 2048
NL = 2
MEM = 256
INW = 6656
CK = 31
PAD = 32
SEM_LIMIT = 16000
SAME_ENG_SYNC = True

PP_DW = 0
PP_DWB = 124
PP_LNG = 128
PP_LNB = 132
PP_LCW = 136
PP_LCB = 152
PP_BA = 156
PP_BX = 160
PP_LAM = 164
PP_GNC = 168
PP_GNA = 172
PP_GNL = 180
NPP = 184

def _chunks():
    ch = []
    for j in range(4):
        ch.append(("vg", j * 128, 512 + j * 128, j))
    for j in range(2):
        ch.append(("gate", 1024 + j * 256, 1024 + j * 256 + 128, 0 + j * 256))
    for j in range(2):
        ch.append(("rx", 5632 + j * 256, 5632 + j * 256 + 128, j * 256))
    for j in range(2):
        ch.append(("gate", 6144 + j * 256, 6144 + j * 256 + 128, 1536 + j * 256))
    for j in range(4):
        ch.append(("gate", 4608 + j * 256, 4608 + j * 256 + 128, 512 + j * 256))
    for j in range(4):
        ch.append(("q", 1536 + j * 256, 1536 + j * 256 + 128, j * 256))
    for j in range(4):
        ch.append(("k", 2560 + j * 256, 2560 + j * 256 + 128, j * 256))
    for j in range(4):
        ch.append(("v", 3584 + j * 256, 3584 + j * 256 + 128, j * 2))
    return ch


CHUNKS = _chunks()
NCH = len(CHUNKS)
BG_READY = 10


class Res:
    __slots__ = ("name", "w", "rs", "t", "ds")

    def __init__(self, name, t=None):
        self.name = name
        self.w = None
        self.rs = {}
        self.t = t
        self.ds = None


class EngState:
    def __init__(self, name, eng, sem):
        self.name = name
        self.eng = eng
        self.sem = sem
        self.cnt = 0
        self.waited = {}
        self.pending = False


class FW:
    def __init__(self, nc, stack, nsem):
        self.nc = nc
        self.stack = stack
        self.semi = 0
        self.E = {}
        for n in ("tensor", "vector", "scalar", "gpsimd", "sync"):
            self.E[n] = EngState(n, getattr(nc, n), self.new_sem())
        self.phase = "top"
        self.names = {}
        self.dpool = {"sync": [], "gpsimd": []}
        self.dlive = []
        self.dall = []

    def new_sem(self):
        s = self.stack.enter_context(self.nc.semaphore(f"s{self.semi}"))
        self.semi += 1
        return s

    def _wait(self, E, tok):
        sem, cnt = tok
        k = id(sem)
        if E.waited.get(k, 0) >= cnt:
            return
        E.eng.wait_ge(sem, cnt)
        E.waited[k] = cnt

    @staticmethod
    def _deps(reads, writes):
        toks = []
        for r in reads:
            if r.w is not None:
                toks.append(r.w)
        for w in writes:
            if w.w is not None:
                toks.append(w.w)
            toks.extend(w.rs.values())
        return toks

    @staticmethod
    def _record(tok, reads, writes):
        for r in reads:
            k = id(tok[0])
            o = r.rs.get(k)
            if o is None or o[1] < tok[1]:
                r.rs[k] = tok
        for w in writes:
            w.w = tok
            w.rs = {}

    def op(self, en, meth, reads, writes, inc=True, **kw):
        E = self.E[en]
        if E.cnt >= SEM_LIMIT and not E.pending:
            E.sem = self.new_sem()
            E.cnt = 0
        for tok in self._deps(reads, writes):
            if tok[0] is E.sem and (en == "tensor" or (not SAME_ENG_SYNC and en != "gpsimd")):
                continue
            self._wait(E, tok)
        ins = getattr(E.eng, meth)(**kw)
        if self.names is not None:
            self.names[ins.ins.name] = self.phase
        tok = (E.sem, E.cnt + 1)
        if inc:
            ins.then_inc(E.sem, 1)
            E.cnt += 1
            E.pending = False
        else:
            E.pending = True
        self._record(tok, reads, writes)
        return tok

    def _dsem(self, owner, qn):
        if owner.ds is None or owner.ds[1] >= SEM_LIMIT:
            pool = self.dpool[qn]
            pool.sort(key=lambda d: -d[1])
            if pool and pool[-1][1] < SEM_LIMIT - 4000:
                owner.ds = pool.pop()
            else:
                owner.ds = [self.new_sem(), 0, qn]
                self.dall.append(owner.ds)
            self.dlive.append(owner.ds)
        assert owner.ds[2] == qn, (owner.name, qn)
        return owner.ds

    def dma(self, qn, out, in_, reads, writes, owner, **kw):
        E = self.E[qn]
        for tok in self._deps(reads, writes):
            self._wait(E, tok)
        ds = self._dsem(owner, qn)
        ins = E.eng.dma_start(out=out, in_=in_, **kw)
        ins.then_inc(ds[0], 16)
        if self.names is not None:
            self.names[ins.ins.name] = self.phase
        ds[1] += 16
        tok = (ds[0], ds[1])
        self._record(tok, reads, writes)
        return tok

    def barrier(self):
        toks = []
        for E in self.E.values():
            assert not E.pending
            if E.cnt > 0:
                toks.append((E.sem, E.cnt))
        for ds in self.dlive:
            if ds[1] > 0:
                toks.append((ds[0], ds[1]))
        for E in self.E.values():
            for tok in toks:
                if tok[0] is E.sem:
                    continue
                self._wait(E, tok)
        for d in self.dlive:
            self.dpool[d[2]].append(d)
        self.dlive = []


def build_program(T, debug=False, stop=None, nl=NL):
    nc = bass.Bass("TRN2", target_bir_lowering=False)
    nc._fw_names = {}
    NTT = T // 128
    NTB = T // 512
    SBT = min(2048, T)
    NSB = T // SBT
    dbg_kind = "ExternalOutput" if debug else "Internal"

    def din(name, shape, dt=F32):
        return nc.dram_tensor(name, list(shape), dt, kind="ExternalInput").ap()

    def dscr(name, shape, dt):
        return nc.dram_tensor(name, list(shape), dt, kind=dbg_kind).ap()

    x_in = din("x", [T, D])
    mem_in = din("mem", [MEM, D])
    w_in = din("w_in", [nl, NCH, 128, 16, 256])
    w_out = din("w_out", [nl, D, D])
    w_q = din("wq", [nl, D, 512])
    w_kv = din("wkv", [nl, D, 1024])
    w_o = din("wo", [nl, 512, D])
    w_pw = din("pw", [nl, 512, 512])
    w_a = din("wa", [nl, 512, 128])
    w_x = din("wx", [nl, 512, 128])
    gbc = din("gbc", [7, 128, D])
    pp_in = din("pp", [nl, 128, NPP])
    out = nc.dram_tensor("out", [T, D], F32, kind="ExternalOutput").ap()

    xres = dscr("xres", [T, D], F32)
    hT_scr = dscr("hT_scr", [128, 16, T], BF16)
    u_scr = dscr("u_scr", [512, PAD + T], F32)
    rx_scr = dscr("rx_scr", [512, PAD + T], F32)
    g_scr = dscr("g_scr", [D, T], BF16)
    q_scr = dscr("q_scr", [1024, T], BF16)
    k_scr = dscr("k_scr", [1024, T], BF16)
    v_scr = dscr("v_scr", [8, 128, NTT, 128], BF16)
    y_scr = dscr("y_scr", [D, T], BF16)

    with ExitStack() as top:
        fw = FW(nc, top, 150)
        fw.names = nc._fw_names
        psb = []
        for i in range(8):
            t = top.enter_context(nc.psum_tensor(f"ps{i}", [128, 512], F32))
            psb.append(Res(f"ps{i}", t))
        prot = [0]

        def nps():
            r = psb[prot[0] % 8]
            prot[0] += 1
            return r

        uniq = [0]
        dres = {}

        def DR(*key):
            r = dres.get(key)
            if r is None:
                r = dres[key] = Res(str(key))
            return r


        def sb(st, name, shape, dt):
            uniq[0] += 1
            name = f"{name}_{uniq[0]}"
            t = st.enter_context(nc.sbuf_tensor(name, list(shape), dt))
            return Res(name, t)

        def V(meth, reads, writes, **kw):
            return fw.op("vector", meth, reads, writes, **kw)

        def A(meth, reads, writes, **kw):
            return fw.op("scalar", meth, reads, writes, **kw)

        def G(meth, reads, writes, **kw):
            return fw.op("gpsimd", meth, reads, writes, **kw)

        def PE(meth, reads, writes, inc=True, **kw):
            return fw.op("tensor", meth, reads, writes, inc=inc, **kw)

        ident = sb(top, "ident", [128, 128], BF16)
        tri = sb(top, "tri", [128, 128], BF16)
        ones_bf = sb(top, "ones_bf", [128, 128], BF16)
        ones_f = sb(top, "ones_f", [128, 128], F32)
        mask0 = sb(top, "mask0", [128, 512], F32)
        onesw = sb(top, "onesw", [128, 512], F32)
        ceps6 = sb(top, "ceps6", [128, 1], F32)
        ceps5 = sb(top, "ceps5", [128, 1], F32)
        cone = sb(top, "cone", [128, 1], F32)
        czero = sb(top, "czero", [128, 1], F32)
        zpad = sb(top, "zpad", [128, PAD], F32)
        pp = [sb(top, f"pp{l}", [128, NPP], F32) for l in range(nl)]
        k2T = sb(top, "k2T", [128, 4, MEM], BF16)
        v2 = sb(top, "v2", [128, 2, 512], BF16)

        G("memset", [], [onesw], ap=onesw.t[:], constant=1.0)
        G("memset", [], [ones_f], ap=ones_f.t[:], constant=1.0)
        G("memset", [], [ones_bf], ap=ones_bf.t[:], constant=1.0)
        G("memset", [], [ceps6], ap=ceps6.t[:], constant=1e-6)
        G("memset", [], [ceps5], ap=ceps5.t[:], constant=1e-5)
        G("memset", [], [cone], ap=cone.t[:], constant=1.0)
        G("memset", [], [czero], ap=czero.t[:], constant=0.0)
        G("memset", [], [zpad], ap=zpad.t[:], constant=0.0)
        G("affine_select", [ones_bf], [ident], out=ident.t[:], in_=ones_bf.t[:], pattern=[[1, 128]],
          compare_op=ALU.is_equal, fill=0.0, base=0, channel_multiplier=-1)
        G("affine_select", [ones_bf], [tri], out=tri.t[:], in_=ones_bf.t[:], pattern=[[-1, 128]],
          compare_op=ALU.is_ge, fill=0.0, base=0, channel_multiplier=1)
        G("affine_select", [onesw], [mask0], out=mask0.t[:], in_=onesw.t[:], pattern=[[1, 512]],
          compare_op=ALU.is_gt, fill=0.0, base=0, channel_multiplier=-1)
        for l in range(nl):
            fw.dma("sync", pp[l].t[:], pp_in[l], [], [pp[l]], pp[l])
        for j in range(4):
            fw.dma("sync", u_scr[j * 128:(j + 1) * 128, 0:PAD], zpad.t[:], [zpad], [], zpad)
            fw.dma("sync", rx_scr[j * 128:(j + 1) * 128, 0:PAD], zpad.t[:], [zpad], [], zpad)
        if debug:
            dmask = nc.dram_tensor("dbg_mask", [128, 512], F32, kind="ExternalOutput").ap()
            dtri = nc.dram_tensor("dbg_tri", [128, 128], BF16, kind="ExternalOutput").ap()
            fw.dma("sync", dmask, mask0.t[:], [mask0], [], mask0)
            fw.dma("sync", dtri, tri.t[:], [tri], [], tri)
        fw.barrier()

        def norm_stats(xb, junk, ssq, rstd, eps_c):
            A("activation", [xb], [junk, ssq], out=junk.t[:], in_=xb.t[:], func=AF.Square,
              accum_out=ssq.t[:, 0:1])
            A("activation", [ssq, eps_c], [rstd], out=rstd.t[:, 0:1], in_=ssq.t[:, 0:1], func=AF.Sqrt,
              scale=1.0 / D, bias=eps_c.t[:, 0:1])
            V("reciprocal", [rstd], [rstd], out=rstd.t[:, 0:1], in_=rstd.t[:, 0:1])

        def norm_a(xb, gt, hb, junk, ssq, rstd):
            norm_stats(xb, junk, ssq, rstd, ceps6)
            V("scalar_tensor_tensor", [xb, rstd, gt], [hb], out=hb.t[:], in0=xb.t[:], scalar=rstd.t[:, 0:1],
              in1=gt.t[:], op0=ALU.mult, op1=ALU.mult)

        def norm_b(hb, hTt, dst=None, oview=None):
            for half in range(2):
                pb = nps()
                pv = pb.t[:].bitcast(BF16)
                for k in range(8):
                    c = half * 8 + k
                    PE("transpose", [hb, ident], [pb], inc=(k == 7), out=pv[:, k * 128:(k + 1) * 128],
                       in_=hb.t[:, c * 128:(c + 1) * 128], identity=ident.t[:])
                ov = hTt.t[:, half * 8:half * 8 + 8, :] if oview is None else oview(half * 8, half * 8 + 8)
                if half == 0:
                    A("activation", [pb], [hTt], out=ov,
                      in_=pv.rearrange("p (c k) -> p c k", k=128), func=AF.Copy)
                else:
                    V("tensor_copy", [pb], [hTt], out=ov,
                      in_=pv.rearrange("p (c k) -> p c k", k=128))
            if dst is not None:
                fw.dma("sync", dst, hTt.t[:], [hTt], [], hTt)

        def norm_to_hT(xb, gt, hb, junk, ssq, rstd, hTt, col0, ncols=128, dst=None, oview=None):
            norm_a(xb, gt, hb, junk, ssq, rstd)
            norm_b(hb, hTt, dst=dst, oview=oview)

        def phase_norm0():
            fw.phase = "phase_norm0"
            with ExitStack() as st:
                gt = sb(st, "p1_g", [128, D], F32)
                xb = [sb(st, f"p1_x{i}", [128, D], F32) for i in range(4)]
                hb = [sb(st, f"p1_h{i}", [128, D], BF16) for i in range(4)]
                junk = sb(st, "p1_junk", [128, D], BF16)
                ssq = [sb(st, f"p1_ssq{i}", [128, 1], F32) for i in range(4)]
                rstd = [sb(st, f"p1_rstd{i}", [128, 1], F32) for i in range(4)]
                hTt = [sb(st, f"p1_hT{i}", [128, 16, 128], BF16) for i in range(4)]
                fw.dma("sync", gt.t[:], gbc[0], [], [gt], gt)
                for tt in range(min(3, NTT)):
                    fw.dma("sync", xb[tt % 4].t[:], x_in[tt * 128:(tt + 1) * 128, :], [], [xb[tt % 4]], xb[tt % 4])
                for tt in range(NTT):
                    i = tt % 4
                    if tt + 3 < NTT:
                        i3 = (tt + 3) % 4
                        fw.dma("sync", xb[i3].t[:], x_in[(tt + 3) * 128:(tt + 4) * 128, :], [], [xb[i3]], xb[i3])
                    norm_to_hT(xb[i], gt, hb[i], junk, ssq[i], rstd[i], hTt[i], 0,
                               dst=hT_scr[:, :, tt * 128:(tt + 1) * 128])
                fw.barrier()

        phase_norm0()
        if stop == "p1":
            return nc

        def phase_inproj(l):
            fw.phase = "phase_inproj"
            with ExitStack() as st:
                hT = sb(st, "p2_hT", [128, 16, SBT], BF16)
                wb = [sb(st, f"p2_w{i}", [128, 16, 256], BF16) for i in range(3)]
                valb = [sb(st, f"p2_val{i}", [128, 512], F32) for i in range(4)]
                sig = [sb(st, f"p2_sig{i}", [128, 512], F32) for i in range(2)]
                ub = [sb(st, f"p2_u{i}", [128, 512], F32) for i in range(3)]
                gb = [sb(st, f"p2_g{i}", [128, 512], BF16) for i in range(3)]
                vb = [sb(st, f"p2_v{i}", [128, 256], BF16) for i in range(3)]
                rot = {"w": 0, "sig": 0, "u": 0, "g": 0, "v": 0}

                def nxt(lst, key):
                    r = lst[rot[key] % len(lst)]
                    rot[key] += 1
                    return r

                ntb = SBT // 512
                pool = make_bg_pool(st)
                prot6 = [0]

                def nps():
                    r = psb[prot6[0] % 6]
                    prot6[0] += 1
                    return r

                bgs = {"gen": None, "done": True, "rate": 0.0, "acc": 0.0}

                def bg_pull():
                    if bgs["done"]:
                        return
                    bgs["acc"] += bgs["rate"]
                    while bgs["acc"] >= 1.0 and not bgs["done"]:
                        bgs["acc"] -= 1.0
                        try:
                            next(bgs["gen"])
                        except StopIteration:
                            bgs["done"] = True

                def bg_drain():
                    if bgs["gen"] is not None and not bgs["done"]:
                        for _ in bgs["gen"]:
                            pass
                    bgs["done"] = True

                for sbi in range(NSB):
                    t0 = sbi * SBT
                    fw.dma("sync", hT.t[:], hT_scr[:, :, t0:t0 + SBT], [], [hT], hT)
                    for ci, (kind, ca, cb, aux) in enumerate(CHUNKS):
                        if ci == BG_READY:
                            bg_drain()
                            tbs = list(range(t0 // 512, (t0 + SBT) // 512))

                            def bg_all(tbs=tbs, first=(sbi == 0)):
                                yield from gen_conv(l, pool, tbs, first)
                                yield from gen_lru(l, pool, tbs, first)

                            bgs["gen"] = bg_all()
                            bgs["done"] = False
                            n_groups = (NCH - BG_READY - 4) * 2 * ntb + 4 * (SBT // 128)
                            bgs["rate"] = len(tbs) * 200.0 / n_groups
                            bgs["acc"] = 0.0
                        w = nxt(wb, "w")
                        fw.dma("gpsimd", w.t[:], w_in[l, ci], [], [w], w)
                        if kind == "v":
                            for tt in range(SBT // 128):
                                pb = nps()
                                for c in range(16):
                                    PE("matmul", [hT, w], [pb], inc=(c == 15), out=pb.t[:, 0:256],
                                       lhsT=hT.t[:, c, tt * 128:(tt + 1) * 128], rhs=w.t[:, c, :],
                                       start=(c == 0), stop=(c == 15))
                                v = nxt(vb, "v")
                                A("activation", [pb], [v], out=v.t[:], in_=pb.t[:, 0:256], func=AF.Copy)
                                gt = (t0 // 128) + tt
                                for hh in range(2):
                                    fw.dma("sync", v_scr[aux + hh, :, gt, :], v.t[:, hh * 128:(hh + 1) * 128],
                                           [v], [], v)
                                bg_pull()
                            continue
                        for half in range(2):
                            for tb in range(ntb):
                                pb = nps()
                                for c in range(16):
                                    PE("matmul", [hT, w], [pb], inc=(c == 15), out=pb.t[:, :],
                                       lhsT=w.t[:, c, half * 128:(half + 1) * 128],
                                       rhs=hT.t[:, c, tb * 512:(tb + 1) * 512],
                                       start=(c == 0), stop=(c == 15))
                                tg = t0 + tb * 512
                                if kind == "vg":
                                    if half == 0:
                                        A("activation", [pb], [valb[tb]], out=valb[tb].t[:], in_=pb.t[:], func=AF.Copy)
                                    else:
                                        sg = nxt(sig, "sig")
                                        A("activation", [pb], [sg], out=sg.t[:], in_=pb.t[:], func=AF.Sigmoid)
                                        u = nxt(ub, "u")
                                        V("tensor_tensor", [valb[tb], sg], [u], out=u.t[:], in0=valb[tb].t[:],
                                          in1=sg.t[:], op=ALU.mult)
                                        fw.dma("sync", u_scr[aux * 128:(aux + 1) * 128, PAD + tg:PAD + tg + 512],
                                               u.t[:], [u], [DR("u", aux, tg // 512)], u)
                                elif kind == "gate":
                                    g = nxt(gb, "g")
                                    A("activation", [pb], [g], out=g.t[:], in_=pb.t[:], func=AF.Silu)
                                    r0 = aux + half * 128
                                    fw.dma("sync", g_scr[r0:r0 + 128, tg:tg + 512], g.t[:], [g],
                                           [DR("g", r0 // 128, tg // 512)], g)
                                elif kind in ("q", "k"):
                                    g = nxt(gb, "g")
                                    A("activation", [pb], [g], out=g.t[:], in_=pb.t[:], func=AF.Copy)
                                    r0 = aux + half * 128
                                    dst = q_scr if kind == "q" else k_scr
                                    fw.dma("sync", dst[r0:r0 + 128, tg:tg + 512], g.t[:], [g], [], g)
                                else:
                                    u = nxt(ub, "u")
                                    A("activation", [pb], [u], out=u.t[:], in_=pb.t[:], func=AF.Copy)
                                    r0 = aux + half * 128
                                    fw.dma("sync", rx_scr[r0:r0 + 128, PAD + tg:PAD + tg + 512], u.t[:], [u],
                                           [DR("rx", r0 // 128, tg // 512)], u)
                                bg_pull()
                bg_drain()
                fw.barrier()

        def group_rstd(ytiles, nfeat, sqb, rstd, pb=None):
            if pb is None:
                pb = nps()
            n = len(ytiles)
            for j, yt in enumerate(ytiles):
                if isinstance(yt, tuple):
                    yt, yap = yt
                else:
                    yap = yt.t[:]
                sq = sqb[j % len(sqb)]
                A("activation", [yt], [sq], out=sq.t[:], in_=yap, func=AF.Square)
                PE("matmul", [ones_f, sq], [pb], out=pb.t[:], lhsT=ones_f.t[:], rhs=sq.t[:],
                   start=(j == 0), stop=(j == n - 1))
            A("activation", [pb, ceps6], [rstd], out=rstd.t[:], in_=pb.t[:], func=AF.Sqrt, scale=1.0 / nfeat,
              bias=ceps6.t[:, 0:1])
            V("reciprocal", [rstd], [rstd], out=rstd.t[:], in_=rstd.t[:])

        def gate_store(yt, rstd, gn_ap, grow, t0, tmp, gate, yo, preloaded=False):
            if isinstance(yt, tuple):
                yt, yap = yt
            else:
                yap = yt.t[:]
            if not preloaded:
                fw.dma("sync", gate.t[:], g_scr[grow:grow + 128, t0:t0 + 512], [], [gate], gate)
            V("tensor_tensor", [yt, rstd], [tmp], out=tmp.t[:], in0=yap, in1=rstd.t[:], op=ALU.mult)
            V("scalar_tensor_tensor", [tmp, gate], [yo], out=yo.t[:], in0=tmp.t[:], scalar=gn_ap, in1=gate.t[:],
              op0=ALU.mult, op1=ALU.mult)
            fw.dma("sync", y_scr[grow:grow + 128, t0:t0 + 512], yo.t[:], [yo], [], yo)

        def phase_conv(l):
            fw.phase = "phase_conv"
            P = pp[l]
            with ExitStack() as st:
                pw = sb(st, "p3_pw", [128, 4, 512], BF16)
                uin = [[sb(st, f"p3_u{i}_{j}", [128, 30 + 512], F32) for j in range(4)] for i in range(2)]
                acc = [sb(st, f"p3_acc{j}", [128, 512], F32) for j in range(4)]
                sqb = [sb(st, f"p3_sq{i}", [128, 512], F32) for i in range(2)]
                mt = sb(st, "p3_m", [128, 512], F32)
                msq = sb(st, "p3_msq", [128, 512], F32)
                rs = sb(st, "p3_rs", [128, 512], F32)
                xn = [sb(st, f"p3_xn{i}", [128, 512], F32) for i in range(2)]
                sbf = [sb(st, f"p3_s{j}", [128, 512], BF16) for j in range(4)]
                yb = [sb(st, f"p3_y{j}", [128, 512], F32) for j in range(4)]
                rs2 = sb(st, "p3_rs2", [128, 512], F32)
                tmp = [sb(st, f"p3_t{i}", [128, 512], F32) for i in range(2)]
                gate = [sb(st, f"p3_g{i}", [128, 512], BF16) for i in range(2)]
                yo = [sb(st, f"p3_yo{i}", [128, 512], BF16) for i in range(2)]
                fw.dma("gpsimd", pw.t[:], w_pw[l].rearrange("(c p) n -> p c n", p=128), [], [pw], pw)
                for tb in range(NTB):
                    t0 = tb * 512
                    us = uin[tb % 2]
                    for tbl in ([0, 1] if tb == 0 else [tb + 1]):
                        if tbl >= NTB:
                            continue
                        tl = tbl * 512
                        for j in range(4):
                            ul = uin[tbl % 2][j]
                            fw.dma("sync", ul.t[:], u_scr[j * 128:(j + 1) * 128, PAD + tl - 30:PAD + tl + 512],
                                   [], [ul], ul)
                    for j in range(4):
                        wcol = PP_DW + j * CK
                        V("tensor_scalar", [us[j], P], [acc[j]], out=acc[j].t[:], in0=us[j].t[:, 30:542],
                          scalar1=P.t[:, wcol + 30:wcol + 31], scalar2=P.t[:, PP_DWB + j:PP_DWB + j + 1],
                          op0=ALU.mult, op1=ALU.add)
                    for k in range(30):
                        for j in range(4):
                            wcol = PP_DW + j * CK
                            V("scalar_tensor_tensor", [us[j], P, acc[j]], [acc[j]], out=acc[j].t[:],
                              in0=us[j].t[:, k:k + 512], scalar=P.t[:, wcol + k:wcol + k + 1], in1=acc[j].t[:],
                              op0=ALU.mult, op1=ALU.add)
                    p1 = nps()
                    for j in range(4):
                        PE("matmul", [ones_f, acc[j]], [p1], out=p1.t[:], lhsT=ones_f.t[:],
                           rhs=acc[j].t[:], start=(j == 0), stop=(j == 3))
                    p2 = nps()
                    for j in range(4):
                        sq = sqb[j % 2]
                        A("activation", [acc[j]], [sq], out=sq.t[:], in_=acc[j].t[:], func=AF.Square)
                        PE("matmul", [ones_f, sq], [p2], out=p2.t[:], lhsT=ones_f.t[:],
                           rhs=sq.t[:], start=(j == 0), stop=(j == 3))
                    A("activation", [p1], [mt], out=mt.t[:], in_=p1.t[:], func=AF.Copy, scale=1.0 / 512)
                    V("tensor_tensor", [mt], [msq], out=msq.t[:], in0=mt.t[:], in1=mt.t[:], op=ALU.mult)
                    V("scalar_tensor_tensor", [p2, msq], [rs], out=rs.t[:], in0=p2.t[:], scalar=1.0 / 512,
                      in1=msq.t[:], op0=ALU.mult, op1=ALU.subtract)
                    A("activation", [rs, ceps5], [rs], out=rs.t[:], in_=rs.t[:], func=AF.Sqrt,
                      bias=ceps5.t[:, 0:1])
                    V("reciprocal", [rs], [rs], out=rs.t[:], in_=rs.t[:])
                    for j in range(4):
                        x1 = xn[j % 2]
                        V("tensor_tensor", [acc[j], mt], [x1], out=x1.t[:], in0=acc[j].t[:], in1=mt.t[:],
                          op=ALU.subtract)
                        V("tensor_tensor", [x1, rs], [x1], out=x1.t[:], in0=x1.t[:], in1=rs.t[:], op=ALU.mult)
                        A("activation", [x1, P], [sbf[j]], out=sbf[j].t[:], in_=x1.t[:], func=AF.Silu,
                          scale=P.t[:, PP_LNG + j:PP_LNG + j + 1], bias=P.t[:, PP_LNB + j:PP_LNB + j + 1])
                    for co in range(4):
                        pb = nps()
                        for ci in range(4):
                            PE("matmul", [pw, sbf[ci]], [pb], out=pb.t[:],
                               lhsT=pw.t[:, ci, co * 128:(co + 1) * 128], rhs=sbf[ci].t[:],
                               start=(ci == 0), stop=(ci == 3))
                        V("tensor_copy", [pb], [yb[co]], out=yb[co].t[:], in_=pb.t[:])
                    group_rstd(yb, 512, sqb, rs2)
                    for co in range(4):
                        gate_store(yb[co], rs2, P.t[:, PP_GNC + co:PP_GNC + co + 1], co * 128, t0,
                                   tmp[co % 2], gate[co % 2], yo[co % 2])
                fw.barrier()

        def phase_lru(l):
            fw.phase = "phase_lru"
            P = pp[l]
            with ExitStack() as st:
                wa = sb(st, "p5_wa", [128, 4, 128], BF16)
                wx = sb(st, "p5_wx", [128, 4, 128], BF16)
                kc = sb(st, "p5_kc", [128, 4], F32)
                hc = [sb(st, f"p5_hc{j}", [128, 1], F32) for j in range(4)]
                rxin = [sb(st, f"p5_rx{i}", [128, 3 + 512], F32) for i in range(8)]
                xc = [sb(st, f"p5_xc{i}", [128, 512], F32) for i in range(4)]
                xcb = [sb(st, f"p5_xcb{i}", [128, 512], BF16) for i in range(4)]
                rt = [sb(st, f"p5_r{i}", [128, 512], F32) for i in range(4)]
                it = [sb(st, f"p5_i{i}", [128, 512], F32) for i in range(4)]
                at = [sb(st, f"p5_a{i}", [128, 512], F32) for i in range(4)]
                om = [sb(st, f"p5_om{i}", [128, 512], F32) for i in range(4)]
                bt = [sb(st, f"p5_b{i}", [128, 512], F32) for i in range(4)]
                hb = [sb(st, f"p5_h{j}", [128, 512], F32) for j in range(4)]
                sqb = [sb(st, f"p5_sq{i}", [128, 512], F32) for i in range(2)]
                rs2 = sb(st, "p5_rs2", [128, 512], F32)
                tmp = [sb(st, f"p5_t{i}", [128, 512], F32) for i in range(2)]
                gate = [sb(st, f"p5_g{i}", [128, 512], BF16) for i in range(2)]
                yo = [sb(st, f"p5_yo{i}", [128, 512], BF16) for i in range(2)]
                fw.dma("gpsimd", wa.t[:], w_a[l].rearrange("(n p) e -> p n e", p=128), [], [wa], wa)
                fw.dma("gpsimd", wx.t[:], w_x[l].rearrange("(n p) e -> p n e", p=128), [], [wx], wx)
                A("activation", [P], [kc], out=kc.t[:], in_=P.t[:, PP_LAM:PP_LAM + 4], func=AF.Exp, scale=-1.0)
                A("activation", [kc, cone], [kc], out=kc.t[:], in_=kc.t[:], func=AF.Ln, bias=cone.t[:, 0:1])
                V("tensor_scalar", [kc], [kc], out=kc.t[:], in0=kc.t[:], scalar1=-8.0, scalar2=None, op0=ALU.mult)
                for j in range(4):
                    G("memset", [], [hc[j]], ap=hc[j].t[:], constant=0.0)
                for tb in range(NTB):
                    t0 = tb * 512
                    J = range(4)
                    rxs = [rxin[(tb % 2) * 4 + j] for j in J]
                    for tbl in ([0, 1] if tb == 0 else [tb + 1]):
                        if tbl >= NTB:
                            continue
                        tl = tbl * 512
                        for j in J:
                            rl = rxin[(tbl % 2) * 4 + j]
                            fw.dma("sync", rl.t[:], rx_scr[j * 128:(j + 1) * 128, PAD + tl - 3:PAD + tl + 512],
                                   [], [rl], rl)
                    for j in J:
                        wcol = PP_LCW + j * 4
                        V("tensor_scalar", [rxs[j], P], [xc[j]], out=xc[j].t[:], in0=rxs[j].t[:, 3:515],
                          scalar1=P.t[:, wcol + 3:wcol + 4], scalar2=P.t[:, PP_LCB + j:PP_LCB + j + 1],
                          op0=ALU.mult, op1=ALU.add)
                    for k in range(3):
                        for j in J:
                            wcol = PP_LCW + j * 4
                            V("scalar_tensor_tensor", [rxs[j], P, xc[j]], [xc[j]], out=xc[j].t[:],
                              in0=rxs[j].t[:, k:k + 512], scalar=P.t[:, wcol + k:wcol + k + 1], in1=xc[j].t[:],
                              op0=ALU.mult, op1=ALU.add)
                    for j in J:
                        G("tensor_copy", [xc[j]], [xcb[j]], out=xcb[j].t[:], in_=xc[j].t[:])
                    prs, pis = [], []
                    for j in J:
                        pr = psb[2 * j]
                        PE("matmul", [wa, xcb[j]], [pr], out=pr.t[:], lhsT=wa.t[:, j, :], rhs=xcb[j].t[:],
                           start=True, stop=True)
                        pi = psb[2 * j + 1]
                        PE("matmul", [wx, xcb[j]], [pi], out=pi.t[:], lhsT=wx.t[:, j, :], rhs=xcb[j].t[:],
                           start=True, stop=True)
                        prs.append(pr)
                        pis.append(pi)
                    for j in J:
                        A("activation", [prs[j], P], [rt[j]], out=rt[j].t[:], in_=prs[j].t[:], func=AF.Sigmoid,
                          bias=P.t[:, PP_BA + j:PP_BA + j + 1])
                        A("activation", [pis[j], P], [it[j]], out=it[j].t[:], in_=pis[j].t[:], func=AF.Sigmoid,
                          bias=P.t[:, PP_BX + j:PP_BX + j + 1])
                    for j in J:
                        A("activation", [rt[j], kc], [at[j]], out=at[j].t[:], in_=rt[j].t[:], func=AF.Exp,
                          scale=kc.t[:, j:j + 1])
                    for j in J:
                        V("tensor_tensor", [at[j]], [om[j]], out=om[j].t[:], in0=at[j].t[:], in1=at[j].t[:],
                          op=ALU.mult)
                    for j in J:
                        V("tensor_scalar", [om[j]], [om[j]], out=om[j].t[:], in0=om[j].t[:], scalar1=-1.0,
                          scalar2=1.0, op0=ALU.mult, op1=ALU.add)
                    for j in J:
                        V("tensor_scalar", [om[j]], [om[j]], out=om[j].t[:], in0=om[j].t[:], scalar1=1e-30,
                          scalar2=None, op0=ALU.max)
                    for j in J:
                        A("activation", [om[j]], [om[j]], out=om[j].t[:], in_=om[j].t[:], func=AF.Sqrt)
                    for j in J:
                        V("tensor_tensor", [it[j], xc[j]], [bt[j]], out=bt[j].t[:], in0=it[j].t[:], in1=xc[j].t[:],
                          op=ALU.mult)
                    for j in J:
                        V("tensor_tensor", [bt[j], om[j]], [bt[j]], out=bt[j].t[:], in0=bt[j].t[:],
                          in1=om[j].t[:], op=ALU.mult)
                    for j in J:
                        V("tensor_tensor_scan", [at[j], bt[j], hc[j]], [hb[j]], out=hb[j].t[:],
                          data0=at[j].t[:], data1=bt[j].t[:], initial=hc[j].t[:, 0:1], op0=ALU.mult,
                          op1=ALU.add)
                    for j in J:
                        V("tensor_copy", [hb[j]], [hc[j]], out=hc[j].t[:, 0:1], in_=hb[j].t[:, 511:512])
                    group_rstd(hb, 512, sqb, rs2)
                    for j in range(4):
                        gate_store(hb[j], rs2, P.t[:, PP_GNL + j:PP_GNL + j + 1], 1536 + j * 128, t0,
                                   tmp[j % 2], gate[j % 2], yo[j % 2])
                fw.barrier()

        def make_bg_pool(st):
            return {
                "f32": [sb(st, f"bg_f{i}", [128, 512], F32) for i in range(22)],
                "b16": [sb(st, f"bg_b{i}", [128, 512], BF16) for i in range(10)],
                "uin": [[sb(st, f"bg_u{i}_{j}", [128, 30 + 512], F32) for j in range(4)] for i in range(2)],
                "pw": sb(st, "bg_pw", [128, 4, 512], BF16),
                "wa": sb(st, "bg_wa", [128, 4, 128], BF16),
                "wx": sb(st, "bg_wx", [128, 4, 128], BF16),
                "kc": sb(st, "bg_kc", [128, 4], F32),
                "hc": [sb(st, f"bg_hc{j}", [128, 1], F32) for j in range(4)],
                "banks": [psb[6], psb[7]],
                "brot": [0],
            }

        def gen_conv(l, pool, tbs, first):
            P = pp[l]
            F = pool["f32"]
            B = pool["b16"]
            pw = pool["pw"]
            uin = pool["uin"]
            acc = F[0:4]
            sqb = F[4:6]
            mt, msq, rs, rs2 = F[6], F[7], F[8], F[9]
            xn = F[10:12]
            yb = F[12:16]
            tmp = F[16:18]
            sbf = B[0:4]
            gate = B[4:8]
            yo = B[8:10]

            def bps():
                r = pool["banks"][pool["brot"][0] % 2]
                pool["brot"][0] += 1
                return r

            if first:
                fw.dma("gpsimd", pw.t[:], w_pw[l].rearrange("(c p) n -> p c n", p=128), [], [pw], pw)
            for tb in tbs:
                t0 = tb * 512
                us = uin[tb % 2]
                for tbl in ([tb, tb + 1] if tb == tbs[0] else [tb + 1]):
                    if tbl not in tbs:
                        continue
                    tl = tbl * 512
                    for j in range(4):
                        ul = uin[tbl % 2][j]
                        rd = [DR("u", j, tbl)] + ([DR("u", j, tbl - 1)] if tbl > 0 else [])
                        fw.dma("sync", ul.t[:], u_scr[j * 128:(j + 1) * 128, PAD + tl - 30:PAD + tl + 512],
                               rd, [ul], ul)
                for j in range(4):
                    fw.dma("sync", gate[j].t[:], g_scr[j * 128:(j + 1) * 128, t0:t0 + 512], [DR("g", j, tb)],
                           [gate[j]], gate[j])
                yield
                for j in range(4):
                    wcol = PP_DW + j * CK
                    V("tensor_scalar", [us[j], P], [acc[j]], out=acc[j].t[:], in0=us[j].t[:, 30:542],
                      scalar1=P.t[:, wcol + 30:wcol + 31], scalar2=P.t[:, PP_DWB + j:PP_DWB + j + 1],
                      op0=ALU.mult, op1=ALU.add)
                yield
                for k in range(30):
                    for j in range(4):
                        wcol = PP_DW + j * CK
                        V("scalar_tensor_tensor", [us[j], P, acc[j]], [acc[j]], out=acc[j].t[:],
                          in0=us[j].t[:, k:k + 512], scalar=P.t[:, wcol + k:wcol + k + 1], in1=acc[j].t[:],
                          op0=ALU.mult, op1=ALU.add)
                        yield
                p1 = bps()
                for j in range(4):
                    PE("matmul", [ones_f, acc[j]], [p1], out=p1.t[:], lhsT=ones_f.t[:], rhs=acc[j].t[:],
                       start=(j == 0), stop=(j == 3))
                p2 = bps()
                for j in range(4):
                    sq = sqb[j % 2]
                    A("activation", [acc[j]], [sq], out=sq.t[:], in_=acc[j].t[:], func=AF.Square)
                    PE("matmul", [ones_f, sq], [p2], out=p2.t[:], lhsT=ones_f.t[:], rhs=sq.t[:],
                       start=(j == 0), stop=(j == 3))
                A("activation", [p1], [mt], out=mt.t[:], in_=p1.t[:], func=AF.Copy, scale=1.0 / 512)
                V("tensor_tensor", [mt], [msq], out=msq.t[:], in0=mt.t[:], in1=mt.t[:], op=ALU.mult)
                V("scalar_tensor_tensor", [p2, msq], [rs], out=rs.t[:], in0=p2.t[:], scalar=1.0 / 512,
                  in1=msq.t[:], op0=ALU.mult, op1=ALU.subtract)
                A("activation", [rs, ceps5], [rs], out=rs.t[:], in_=rs.t[:], func=AF.Sqrt, bias=ceps5.t[:, 0:1])
                V("reciprocal", [rs], [rs], out=rs.t[:], in_=rs.t[:])
                yield
                for j in range(4):
                    x1 = xn[j % 2]
                    V("tensor_tensor", [acc[j], mt], [x1], out=x1.t[:], in0=acc[j].t[:], in1=mt.t[:],
                      op=ALU.subtract)
                    V("tensor_tensor", [x1, rs], [x1], out=x1.t[:], in0=x1.t[:], in1=rs.t[:], op=ALU.mult)
                    A("activation", [x1, P], [sbf[j]], out=sbf[j].t[:], in_=x1.t[:], func=AF.Silu,
                      scale=P.t[:, PP_LNG + j:PP_LNG + j + 1], bias=P.t[:, PP_LNB + j:PP_LNB + j + 1])
                    yield
                for co in range(4):
                    pb = bps()
                    for ci in range(4):
                        PE("matmul", [pw, sbf[ci]], [pb], out=pb.t[:], lhsT=pw.t[:, ci, co * 128:(co + 1) * 128],
                           rhs=sbf[ci].t[:], start=(ci == 0), stop=(ci == 3))
                    V("tensor_copy", [pb], [yb[co]], out=yb[co].t[:], in_=pb.t[:])
                    yield
                group_rstd(yb, 512, sqb, rs2, pb=bps())
                yield
                for co in range(4):
                    gate_store(yb[co], rs2, P.t[:, PP_GNC + co:PP_GNC + co + 1], co * 128, t0, tmp[co % 2],
                               gate[co], yo[co % 2], preloaded=True)
                    yield

        def gen_lru(l, pool, tbs, first):
            P = pp[l]
            F = pool["f32"]
            B = pool["b16"]
            wa, wx, kc, hc = pool["wa"], pool["wx"], pool["kc"], pool["hc"]
            rxin = [pool["uin"][0][j] for j in range(4)] + [pool["uin"][1][j] for j in range(4)]
            xc = F[0:4]
            rt = F[4:8]
            it = F[8:12]
            om = F[12:16]
            hb = F[16:20]
            sqb = F[20:22]
            rs2 = F[4]
            tmp = F[5:7]
            xcb = B[0:4]
            gate = B[4:8]
            yo = B[8:10]
            J = range(4)

            def bps():
                r = pool["banks"][pool["brot"][0] % 2]
                pool["brot"][0] += 1
                return r

            if first:
                fw.dma("gpsimd", wa.t[:], w_a[l].rearrange("(n p) e -> p n e", p=128), [], [wa], wa)
                fw.dma("gpsimd", wx.t[:], w_x[l].rearrange("(n p) e -> p n e", p=128), [], [wx], wx)
                A("activation", [P], [kc], out=kc.t[:], in_=P.t[:, PP_LAM:PP_LAM + 4], func=AF.Exp, scale=-1.0)
                A("activation", [kc, cone], [kc], out=kc.t[:], in_=kc.t[:], func=AF.Ln, bias=cone.t[:, 0:1])
                V("tensor_scalar", [kc], [kc], out=kc.t[:], in0=kc.t[:], scalar1=-8.0, scalar2=None, op0=ALU.mult)
                for j in J:
                    G("memset", [], [hc[j]], ap=hc[j].t[:], constant=0.0)
            yield
            for tb in tbs:
                t0 = tb * 512
                rxs = [rxin[(tb % 2) * 4 + j] for j in J]
                for tbl in ([tb, tb + 1] if tb == tbs[0] else [tb + 1]):
                    if tbl not in tbs:
                        continue
                    tl = tbl * 512
                    for j in J:
                        rl = rxin[(tbl % 2) * 4 + j]
                        rd = [DR("rx", j, tbl)] + ([DR("rx", j, tbl - 1)] if tbl > 0 else [])
                        fw.dma("sync", rl.t[:, 0:515], rx_scr[j * 128:(j + 1) * 128, PAD + tl - 3:PAD + tl + 512],
                               rd, [rl], rl)
                for j in J:
                    fw.dma("sync", gate[j].t[:], g_scr[1536 + j * 128:1536 + (j + 1) * 128, t0:t0 + 512],
                           [DR("g", 12 + j, tb)], [gate[j]], gate[j])
                yield
                for j in J:
                    wcol = PP_LCW + j * 4
                    V("tensor_scalar", [rxs[j], P], [xc[j]], out=xc[j].t[:], in0=rxs[j].t[:, 3:515],
                      scalar1=P.t[:, wcol + 3:wcol + 4], scalar2=P.t[:, PP_LCB + j:PP_LCB + j + 1],
                      op0=ALU.mult, op1=ALU.add)
                yield
                for k in range(3):
                    for j in J:
                        wcol = PP_LCW + j * 4
                        V("scalar_tensor_tensor", [rxs[j], P, xc[j]], [xc[j]], out=xc[j].t[:],
                          in0=rxs[j].t[:, k:k + 512], scalar=P.t[:, wcol + k:wcol + k + 1], in1=xc[j].t[:],
                          op0=ALU.mult, op1=ALU.add)
                    yield
                for j in J:
                    A("activation", [xc[j]], [xcb[j]], out=xcb[j].t[:], in_=xc[j].t[:], func=AF.Copy)
                yield
                for j in J:
                    pr = bps()
                    PE("matmul", [wa, xcb[j]], [pr], out=pr.t[:], lhsT=wa.t[:, j, :], rhs=xcb[j].t[:],
                       start=True, stop=True)
                    A("activation", [pr, P], [rt[j]], out=rt[j].t[:], in_=pr.t[:], func=AF.Sigmoid,
                      bias=P.t[:, PP_BA + j:PP_BA + j + 1])
                    pi = bps()
                    PE("matmul", [wx, xcb[j]], [pi], out=pi.t[:], lhsT=wx.t[:, j, :], rhs=xcb[j].t[:],
                       start=True, stop=True)
                    A("activation", [pi, P], [it[j]], out=it[j].t[:], in_=pi.t[:], func=AF.Sigmoid,
                      bias=P.t[:, PP_BX + j:PP_BX + j + 1])
                    yield
                for j in J:
                    A("activation", [rt[j], kc], [rt[j]], out=rt[j].t[:], in_=rt[j].t[:], func=AF.Exp,
                      scale=kc.t[:, j:j + 1])
                yield
                for j in J:
                    V("tensor_tensor", [rt[j]], [om[j]], out=om[j].t[:], in0=rt[j].t[:], in1=rt[j].t[:],
                      op=ALU.mult)
                yield
                for j in J:
                    V("tensor_scalar", [om[j]], [om[j]], out=om[j].t[:], in0=om[j].t[:], scalar1=-1.0,
                      scalar2=1.0, op0=ALU.mult, op1=ALU.add)
                yield
                for j in J:
                    V("tensor_scalar", [om[j]], [om[j]], out=om[j].t[:], in0=om[j].t[:], scalar1=1e-30,
                      scalar2=None, op0=ALU.max)
                yield
                for j in J:
                    A("activation", [om[j]], [om[j]], out=om[j].t[:], in_=om[j].t[:], func=AF.Sqrt)
                yield
                for j in J:
                    V("tensor_tensor", [it[j], xc[j]], [it[j]], out=it[j].t[:], in0=it[j].t[:], in1=xc[j].t[:],
                      op=ALU.mult)
                yield
                for j in J:
                    V("tensor_tensor", [it[j], om[j]], [it[j]], out=it[j].t[:], in0=it[j].t[:], in1=om[j].t[:],
                      op=ALU.mult)
                yield
                for j in J:
                    V("tensor_tensor_scan", [rt[j], it[j], hc[j]], [hb[j]], out=hb[j].t[:], data0=rt[j].t[:],
                      data1=it[j].t[:], initial=hc[j].t[:, 0:1], op0=ALU.mult, op1=ALU.add)
                    yield
                for j in J:
                    V("tensor_copy", [hb[j]], [hc[j]], out=hc[j].t[:, 0:1], in_=hb[j].t[:, 511:512])
                yield
                group_rstd(hb, 512, sqb, rs2, pb=bps())
                yield
                for j in J:
                    gate_store(hb[j], rs2, P.t[:, PP_GNL + j:PP_GNL + j + 1], 1536 + j * 128, t0, tmp[j % 2],
                               gate[j], yo[j % 2], preloaded=True)
                    yield

        def phase_attn(l):
            fw.phase = "phase_attn"
            P = pp[l]
            SC = 128.0 ** -0.5
            with ExitStack() as st:
                qb = [sb(st, f"p4_q{i}", [128, 512], BF16) for i in range(2)]
                kb = [sb(st, f"p4_k{i}", [128, T], BF16) for i in range(2)]
                vb = [sb(st, f"p4_v{i}", [128, NTT, 128], BF16) for i in range(3)]
                eb = [sb(st, f"p4_e{i}", [128, 512], F32) for i in range(4)]
                spb = [sb(st, f"p4_sp{i}", [128, 512], BF16) for i in range(4)]
                gbf = [sb(st, f"p4_gb{i}", [128, 512], F32) for i in range(2)]
                ab = [sb(st, f"p4_a{i}", [128, 512], BF16) for i in range(3)]
                Sb = [sb(st, f"p4_S{i}", [128, 512], BF16) for i in range(2)]
                ob = [sb(st, f"p4_o{i}", [128, 8, 512], F32) for i in range(2)]
                sqb = [sb(st, f"p4_sq{i}", [128, 512], F32) for i in range(2)]
                rs2 = sb(st, "p4_rs2", [128, 512], F32)
                tmp = [sb(st, f"p4_t{i}", [128, 512], F32) for i in range(2)]
                gate = [sb(st, f"p4_g{i}", [128, 512], BF16) for i in range(2)]
                yo = [sb(st, f"p4_yo{i}", [128, 512], BF16) for i in range(2)]
                zps = psb[0:3]
                fps = psb[3:5]
                ops = psb[5:7]
                chains = []
                tiles = []
                for QB in range(NTB):
                    for h in range(8):
                        nk = 4 * (QB + 1)
                        ci = len(chains)
                        chains.append((QB, h, nk))
                        for i in range(nk):
                            kt = nk - 1 - i
                            o = max(0, kt * 128 - QB * 512)
                            tiles.append((ci, i, kt, o))
                N = len(tiles)

                def load_chain(ci):
                    QB, h, nk = chains[ci]
                    p = ci % 2
                    fw.dma("sync", qb[p].t[:], q_scr[h * 128:(h + 1) * 128, QB * 512:(QB + 1) * 512], [], [qb[p]],
                           qb[p])
                    fw.dma("sync", kb[p].t[:, 0:nk * 128], k_scr[h * 128:(h + 1) * 128, 0:nk * 128], [], [kb[p]],
                           kb[p])

                def load_v(ci):
                    QB, h, nk = chains[ci]
                    p3 = ci % 3
                    fw.dma("sync", vb[p3].t[:, 0:nk, :], v_scr[h, :, 0:nk, :], [], [vb[p3]], vb[p3])

                def st0(n):
                    ci, i, kt, o = tiles[n]
                    p = ci % 2
                    if i == 0:
                        if ci == 0:
                            load_chain(0)
                        load_v(ci)
                        if ci + 1 < len(chains):
                            load_chain(ci + 1)
                    z = zps[n % 3]
                    PE("matmul", [kb[p], qb[p]], [z], out=z.t[:, o:512], lhsT=kb[p].t[:, kt * 128:(kt + 1) * 128],
                       rhs=qb[p].t[:, o:512], start=True, stop=True)

                def st1(n):
                    ci, i, kt, o = tiles[n]
                    QB, h, nk = chains[ci]
                    z = zps[n % 3]
                    e = eb[n % 4]
                    sp = spb[n % 4]
                    A("activation", [z], [e], out=e.t[:, o:512], in_=z.t[:, o:512], func=AF.Exp, scale=SC)
                    if kt >= 4 * QB:
                        V("tensor_tensor", [e, mask0], [e], out=e.t[:, o:512], in0=e.t[:, o:512],
                          in1=mask0.t[:, 0:512 - o], op=ALU.mult)
                    A("activation", [e, cone], [sp], out=sp.t[:, o:512], in_=e.t[:, o:512], func=AF.Ln,
                      bias=cone.t[:, 0:1])

                def st2(n):
                    ci, i, kt, o = tiles[n]
                    QB, h, nk = chains[ci]
                    sp = spb[n % 4]
                    S = Sb[ci % 2]
                    f = fps[n % 2]
                    if i == 0:
                        G("memset", [], [S], ap=S.t[:], constant=0.0)
                    PE("matmul", [tri, sp], [f], out=f.t[:, o:512], lhsT=tri.t[:], rhs=sp.t[:, o:512], start=True,
                       stop=(i == 0))
                    if i > 0:
                        PE("matmul", [ones_bf, S], [f], out=f.t[:, o:512], lhsT=ones_bf.t[:], rhs=S.t[:, o:512],
                           start=False, stop=True)
                    if i < nk - 1:
                        V("tensor_tensor", [S, sp], [S], out=S.t[:, o:512], in0=S.t[:, o:512], in1=sp.t[:, o:512],
                          op=ALU.add)
                    g = gbf[n % 2]
                    A("activation", [f], [g], out=g.t[:, o:512], in_=f.t[:, o:512], func=AF.Exp, scale=-1.0)
                    a = ab[n % 3]
                    e = eb[n % 4]
                    if o > 0:
                        G("memset", [], [a], ap=a.t[:, 0:o], constant=0.0)
                    V("tensor_tensor", [e, g], [a], out=a.t[:, o:512], in0=e.t[:, o:512], in1=g.t[:, o:512],
                      op=ALU.mult)

                def st3(n):
                    ci, i, kt, o = tiles[n]
                    QB, h, nk = chains[ci]
                    p = ci % 3
                    a = ab[n % 3]
                    op_ = ops[ci % 2]
                    PE("matmul", [vb[p], a], [op_], out=op_.t[:], lhsT=vb[p].t[:, kt, :], rhs=a.t[:],
                       start=(i == 0), stop=(i == nk - 1))
                    if i == nk - 1:
                        o_ = ob[QB % 2]
                        V("tensor_copy", [op_], [o_], out=o_.t[:, h, :], in_=op_.t[:])
                        if h == 7:
                            group_rstd([(o_, o_.t[:, hh, :]) for hh in range(8)], 1024, sqb, rs2, pb=psb[7])
                            for hh in range(8):
                                gate_store((o_, o_.t[:, hh, :]), rs2, P.t[:, PP_GNA + hh:PP_GNA + hh + 1],
                                           512 + hh * 128, QB * 512, tmp[hh % 2], gate[hh % 2], yo[hh % 2])

                for s_ in range(N + 6):
                    if s_ < N:
                        st0(s_)
                    if 0 <= s_ - 2 < N:
                        st1(s_ - 2)
                    if 0 <= s_ - 4 < N:
                        st2(s_ - 4)
                    if 0 <= s_ - 6 < N:
                        st3(s_ - 6)
                fw.barrier()

        def phase_memkv(l):
            fw.phase = "phase_memkv"
            with ExitStack() as st:
                wkv = sb(st, "p0_wkv", [128, 16, 1024], BF16)
                gt = sb(st, "p0_g", [128, D], F32)
                xb = [sb(st, f"p0_x{i}", [128, D], F32) for i in range(2)]
                hb = [sb(st, f"p0_h{i}", [128, D], BF16) for i in range(2)]
                junk = sb(st, "p0_junk", [128, D], BF16)
                ssq = [sb(st, f"p0_ssq{i}", [128, 1], F32) for i in range(2)]
                rstd = [sb(st, f"p0_rstd{i}", [128, 1], F32) for i in range(2)]
                memT = sb(st, "p0_memT", [128, 16, MEM], BF16)
                fw.dma("gpsimd", wkv.t[:], w_kv[l].rearrange("(c p) n -> p c n", p=128), [], [wkv], wkv)
                fw.dma("sync", gt.t[:], gbc[4 + l], [], [gt], gt)
                for kt in range(2):
                    fw.dma("sync", xb[kt].t[:], mem_in[kt * 128:(kt + 1) * 128, :], [], [xb[kt]], xb[kt])
                    norm_to_hT(xb[kt], gt, hb[kt], junk, ssq[kt], rstd[kt], memT, 0,
                               oview=lambda c0, c1, kt=kt: memT.t[:, c0:c1, kt * 128:(kt + 1) * 128])
                for h in range(4):
                    pb = nps()
                    for c in range(16):
                        PE("matmul", [wkv, memT], [pb], inc=(c == 15), out=pb.t[:, 0:MEM],
                           lhsT=wkv.t[:, c, h * 128:(h + 1) * 128], rhs=memT.t[:, c, :], start=(c == 0),
                           stop=(c == 15))
                    V("tensor_copy", [pb], [k2T], out=k2T.t[:, h, :], in_=pb.t[:, 0:MEM])
                for kt in range(2):
                    pb = nps()
                    for c in range(16):
                        PE("matmul", [wkv, memT], [pb], inc=(c == 15), out=pb.t[:],
                           lhsT=memT.t[:, c, kt * 128:(kt + 1) * 128], rhs=wkv.t[:, c, 512:1024],
                           start=(c == 0), stop=(c == 15))
                    V("tensor_copy", [pb], [v2], out=v2.t[:, kt, :], in_=pb.t[:])
                fw.barrier()

        def phase_wout(l):
            fw.phase = "phase_wout"
            with ExitStack() as st:
                wout = sb(st, "p6_wout", [128, 16, D], BF16)
                gt = sb(st, "p6_g", [128, D], F32)
                yT = [sb(st, f"p6_yT{i}", [128, 16, 512], BF16) for i in range(2)]
                xb = [sb(st, f"p6_x{i}", [128, D], F32) for i in range(3)]
                hb = [sb(st, f"p6_h{i}", [128, D], BF16) for i in range(2)]
                junk = sb(st, "p6_junk", [128, D], BF16)
                ssq = [sb(st, f"p6_ssq{i}", [128, 1], F32) for i in range(2)]
                rstd = [sb(st, f"p6_rstd{i}", [128, 1], F32) for i in range(2)]
                hTt = [sb(st, f"p6_hT{i}", [128, 16, 128], BF16) for i in range(2)]
                fw.dma("gpsimd", wout.t[:], w_out[l].rearrange("(c p) n -> p c n", p=128), [], [wout], wout)
                fw.dma("sync", gt.t[:], gbc[2 + l], [], [gt], gt)
                xsrc = x_in if l == 0 else xres
                yv = y_scr.rearrange("(c p) t -> p c t", p=128)
                fw.dma("sync", yT[0].t[:], yv[:, :, 0:512], [], [yT[0]], yT[0])
                fw.dma("sync", xb[0].t[:], xsrc[0:128, :], [], [xb[0]], xb[0])
                pending = None
                for tb in range(NTB):
                    y = yT[tb % 2]
                    if tb + 1 < NTB:
                        yn = yT[(tb + 1) % 2]
                        fw.dma("sync", yn.t[:], yv[:, :, (tb + 1) * 512:(tb + 2) * 512], [], [yn], yn)
                    for tq in range(4):
                        tt = tb * 4 + tq
                        i = tt % 2
                        x = xb[tt % 3]
                        if tt + 1 < NTT:
                            xn_ = xb[(tt + 1) % 3]
                            fw.dma("sync", xn_.t[:], xsrc[(tt + 1) * 128:(tt + 2) * 128, :], [], [xn_], xn_)
                        for nb in range(4):
                            pb = nps()
                            for c in range(16):
                                PE("matmul", [y, wout], [pb], inc=(c == 15), out=pb.t[:],
                                   lhsT=y.t[:, c, tq * 128:(tq + 1) * 128], rhs=wout.t[:, c, nb * 512:(nb + 1) * 512],
                                   start=(c == 0), stop=(c == 15))
                            V("tensor_tensor", [pb, x], [x], out=x.t[:, nb * 512:(nb + 1) * 512], in0=pb.t[:],
                              in1=x.t[:, nb * 512:(nb + 1) * 512], op=ALU.add)
                        if pending is not None:
                            pending()
                        fw.dma("sync", xres[tt * 128:(tt + 1) * 128, :], x.t[:], [x], [], x)
                        norm_a(x, gt, hb[i], junk, ssq[i], rstd[i])
                        pending = (lambda i=i, tt=tt: norm_b(hb[i], hTt[i], dst=hT_scr[:, :, tt * 128:(tt + 1) * 128]))
                pending()
                fw.barrier()

        def phase_xattn(l):
            fw.phase = "phase_xattn"
            last = (l == nl - 1)
            SC = 128.0 ** -0.5
            with ExitStack() as st:
                wq = sb(st, "p7_wq", [128, 16, 512], BF16)
                wo = sb(st, "p7_wo", [128, 4, D], BF16)
                gt = sb(st, "p7_g", [128, D], F32)
                h2T = [sb(st, f"p7_h2T{i}", [128, 16, 512], BF16) for i in range(2)]
                q2 = [sb(st, f"p7_q2{i}", [128, 4, 512], BF16) for i in range(2)]
                Eb = [sb(st, f"p7_E{i}", [128, 512], BF16) for i in range(4)]
                rden = [sb(st, f"p7_rd{i}", [128, 512], F32) for i in range(2)]
                o2 = [sb(st, f"p7_o2{i}", [128, 4, 512], BF16) for i in range(2)]
                xb = [sb(st, f"p7_x{i}", [128, D], F32) for i in range(3)]
                ob_ = [sb(st, f"p7_ob{i}", [128, D], F32) for i in range(2)]
                hb = [sb(st, f"p7_h{i}", [128, D], BF16) for i in range(2)]
                junk = sb(st, "p7_junk", [128, D], BF16)
                ssq = [sb(st, f"p7_ssq{i}", [128, 1], F32) for i in range(2)]
                rstd = [sb(st, f"p7_rstd{i}", [128, 1], F32) for i in range(2)]
                hTt = [sb(st, f"p7_hT{i}", [128, 16, 128], BF16) for i in range(2)]
                fw.dma("gpsimd", wq.t[:], w_q[l].rearrange("(c p) n -> p c n", p=128), [], [wq], wq)
                fw.dma("gpsimd", wo.t[:], w_o[l].rearrange("(c p) n -> p c n", p=128), [], [wo], wo)
                fw.dma("sync", gt.t[:], gbc[6] if last else gbc[l + 1], [], [gt], gt)
                fw.dma("sync", h2T[0].t[:], hT_scr[:, :, 0:512], [], [h2T[0]], h2T[0])
                state = {"ne": 0, "pending": None}

                def S1(tb):
                    hT = h2T[tb % 2]
                    if tb + 1 < NTB:
                        hn = h2T[(tb + 1) % 2]
                        fw.dma("sync", hn.t[:], hT_scr[:, :, (tb + 1) * 512:(tb + 2) * 512], [], [hn], hn)
                    q = q2[tb % 2]
                    for h in range(4):
                        pb = nps()
                        for c in range(16):
                            PE("matmul", [wq, hT], [pb], inc=(c == 15), out=pb.t[:],
                               lhsT=wq.t[:, c, h * 128:(h + 1) * 128], rhs=hT.t[:, c, :], start=(c == 0),
                               stop=(c == 15))
                        if h % 2 == 0:
                            V("tensor_copy", [pb], [q], out=q.t[:, h, :], in_=pb.t[:])
                        else:
                            A("activation", [pb], [q], out=q.t[:, h, :], in_=pb.t[:], func=AF.Copy)

                def S2(tb):
                    q = q2[tb % 2]
                    o2t = o2[tb % 2]
                    for h in range(4):
                        es = []
                        for kt in range(2):
                            pb = nps()
                            PE("matmul", [k2T, q], [pb], out=pb.t[:], lhsT=k2T.t[:, h, kt * 128:(kt + 1) * 128],
                               rhs=q.t[:, h, :], start=True, stop=True)
                            e = Eb[state["ne"] % 4]
                            state["ne"] += 1
                            A("activation", [pb], [e], out=e.t[:], in_=pb.t[:], func=AF.Exp, scale=SC)
                            es.append(e)
                        po = nps()
                        for kt in range(2):
                            PE("matmul", [v2, es[kt]], [po], out=po.t[:], lhsT=v2.t[:, kt, h * 128:(h + 1) * 128],
                               rhs=es[kt].t[:], start=(kt == 0), stop=(kt == 1))
                        pd = nps()
                        for kt in range(2):
                            PE("matmul", [ones_bf, es[kt]], [pd], out=pd.t[:], lhsT=ones_bf.t[:], rhs=es[kt].t[:],
                               start=(kt == 0), stop=(kt == 1))
                        rd = rden[h % 2]
                        V("reciprocal", [pd], [rd], out=rd.t[:], in_=pd.t[:])
                        V("tensor_tensor", [po, rd], [o2t], out=o2t.t[:, h, :], in0=po.t[:], in1=rd.t[:], op=ALU.mult)

                def S3(tb):
                    o2t = o2[tb % 2]
                    for tq in range(4):
                        tt = tb * 4 + tq
                        i = tt % 2
                        x = xb[tt % 3]
                        if tt == 0:
                            fw.dma("sync", x.t[:], xres[0:128, :], [], [x], x)
                        if tt + 1 < NTT:
                            xn_ = xb[(tt + 1) % 3]
                            fw.dma("sync", xn_.t[:], xres[(tt + 1) * 128:(tt + 2) * 128, :], [], [xn_], xn_)
                        for nb in range(4):
                            pb = nps()
                            for hh in range(4):
                                PE("matmul", [o2t, wo], [pb], inc=(hh == 3), out=pb.t[:],
                                   lhsT=o2t.t[:, hh, tq * 128:(tq + 1) * 128], rhs=wo.t[:, hh, nb * 512:(nb + 1) * 512],
                                   start=(hh == 0), stop=(hh == 3))
                            V("tensor_tensor", [pb, x], [x], out=x.t[:, nb * 512:(nb + 1) * 512], in0=pb.t[:],
                              in1=x.t[:, nb * 512:(nb + 1) * 512], op=ALU.add)
                        if state["pending"] is not None:
                            state["pending"]()
                            state["pending"] = None
                        if last:
                            norm_stats(x, junk, ssq[i], rstd[i], ceps6)
                            V("scalar_tensor_tensor", [x, rstd[i], gt], [ob_[i]], out=ob_[i].t[:], in0=x.t[:],
                              scalar=rstd[i].t[:, 0:1], in1=gt.t[:], op0=ALU.mult, op1=ALU.mult)
                            tokf = fw.dma("sync", out[tt * 128:(tt + 1) * 128, :], ob_[i].t[:], [ob_[i]], [], ob_[i])
                            final_toks.append(tokf)
                        else:
                            fw.dma("sync", xres[tt * 128:(tt + 1) * 128, :], x.t[:], [x], [], x)
                            norm_a(x, gt, hb[i], junk, ssq[i], rstd[i])
                            state["pending"] = (lambda i=i, tt=tt: norm_b(
                                hb[i], hTt[i], dst=hT_scr[:, :, tt * 128:(tt + 1) * 128]))

                for it_ in range(NTB + 2):
                    if it_ < NTB:
                        S1(it_)
                    if 0 <= it_ - 1 < NTB:
                        S2(it_ - 1)
                    if 0 <= it_ - 2 < NTB:
                        S3(it_ - 2)
                if state["pending"] is not None:
                    state["pending"]()
                fw.barrier()

        final_toks = []
        for l in range(nl):
            phase_inproj(l)
            if stop == f"p2_{l}":
                return nc
            phase_attn(l)
            if stop == f"p4_{l}":
                return nc
            phase_memkv(l)
            phase_wout(l)
            if stop == f"p6_{l}":
                return nc
            phase_xattn(l)
            if stop == f"p7_{l}":
                return nc
    return nc


_CACHE = {}


def _layout_inputs(inp, T, nl=NL):
    f = np.float32
    W = np.asarray(inp["w_in"], f)
    w_in = np.empty((nl, NCH, 128, 16, 256), f)
    for l in range(nl):
        for ci, (kind, ca, cb, aux) in enumerate(CHUNKS):
            blk = np.concatenate([W[l][:, ca:ca + 128], W[l][:, cb:cb + 128]], axis=1)
            w_in[l, ci] = blk.reshape(16, 128, 256).transpose(1, 0, 2)
    gl = [inp["mix_norm_g"][0], inp["mix_norm_g"][1], inp["xattn_norm_g"][0], inp["xattn_norm_g"][1],
          inp["mem_norm_g"][0], inp["mem_norm_g"][1], inp["final_norm_g"]]
    gbc = np.stack([np.broadcast_to(np.asarray(g, f)[None, :], (128, D)) for g in gl]).copy()

    def cols(v):
        v = np.asarray(v, f)
        return v.reshape(-1, 128).T

    pp = np.zeros((nl, 128, NPP), f)
    for l in range(nl):
        dw = np.asarray(inp["conv_dw_w"][l], f)
        pp[l, :, PP_DW:PP_DW + 124] = dw.T.reshape(4, 128, CK).transpose(1, 0, 2).reshape(128, 124)
        pp[l, :, PP_DWB:PP_DWB + 4] = cols(inp["conv_dw_b"][l])
        pp[l, :, PP_LNG:PP_LNG + 4] = cols(inp["conv_ln_g"][l])
        pp[l, :, PP_LNB:PP_LNB + 4] = cols(inp["conv_ln_b"][l])
        lw = np.asarray(inp["lru_conv_w"][l], f)
        pp[l, :, PP_LCW:PP_LCW + 16] = lw.T.reshape(4, 128, 4).transpose(1, 0, 2).reshape(128, 16)
        pp[l, :, PP_LCB:PP_LCB + 4] = cols(inp["lru_conv_b"][l])
        pp[l, :, PP_BA:PP_BA + 4] = cols(inp["lru_ba"][l])
        pp[l, :, PP_BX:PP_BX + 4] = cols(inp["lru_bx"][l])
        pp[l, :, PP_LAM:PP_LAM + 4] = cols(inp["lru_lambda"][l])
        pp[l, :, PP_GNC:PP_GNC + 4] = cols(inp["out_norm_conv"][l])
        pp[l, :, PP_GNA:PP_GNA + 8] = cols(inp["out_norm_attn"][l])
        pp[l, :, PP_GNL:PP_GNL + 4] = cols(inp["out_norm_lru"][l])
    common = {
        "w_in": w_in,
        "w_out": np.ascontiguousarray(inp["w_out"], f),
        "wq": np.ascontiguousarray(inp["xattn_wq"], f),
        "wkv": np.ascontiguousarray(inp["xattn_wkv"], f),
        "wo": np.ascontiguousarray(inp["xattn_wo"], f),
        "pw": np.ascontiguousarray(inp["conv_pw_w"], f),
        "wa": np.ascontiguousarray(np.asarray(inp["lru_wa"], f).reshape(nl, 512, 128)),
        "wx": np.ascontiguousarray(np.asarray(inp["lru_wx"], f).reshape(nl, 512, 128)),
        "gbc": gbc,
        "pp": pp,
    }
    return common


def kernel(**inputs):
    x = np.asarray(inputs["x"], np.float32)
    mem = np.asarray(inputs["mem"], np.float32)
    B, S, _ = x.shape
    common = _layout_inputs(inputs, S)
    key = ("full", S)
    if key not in _CACHE:
        _CACHE[key] = build_program(S)
    nc = _CACHE[key]
    in_maps = []
    for c in range(8):
        b = c % B
        m = dict(common)
        m["x"] = np.ascontiguousarray(x[b])
        m["mem"] = np.ascontiguousarray(mem[b])
        in_maps.append(m)
    res = run_bass_kernel_spmd(nc, in_maps, core_ids=list(range(8)))
    return np.stack([res.results[b]["out"] for b in range(B)], axis=0)
```
